# Optimizing a Trainium2 kernel written in Bass

```python
import math
import jax, jax.numpy as jnp
from jax import lax
import numpy as np

D_MODEL = 1024
BATCH = 1
SEQ = 16384
DEPTH = 2

GRID_W = 64
CTX_LEN = 256
EPS = 1e-6
F32 = jnp.float32

W_SSM = 256
SSM_GROUP = 16
SSM_GROUPS = W_SSM // SSM_GROUP
SSM_STATE = 64
W_LRU = 256
LRU_BLOCKS = 8
LRU_BLOCK = W_LRU // LRU_BLOCKS
LRU_CONV = 4
LRU_PAD = (2, 1)
LRU_C = 8.0
DA_HEADS = 4
DA_HEAD = 32
DA_VDIM = 2 * DA_HEAD
W_DA = DA_HEADS * DA_VDIM
ROPE_BASE = 10000.0
Q_BLOCK = 128
LAMBDA_INIT_BASE = 0.8
LAMBDA_INIT_AMP = 0.6
LAMBDA_INIT_RATE = 0.3
GLA_HEADS = 4
GLA_DK = 32
GLA_DV = 64
W_GLA = GLA_HEADS * GLA_DV
GLA_RANK = 16
GLA_TAU = 16.0
GLA_CHUNK = 64
N_BRANCH = 4
W_BRANCH = 256
D_FF = 2816
FFN_CONV = 3
FFN_PAD = (1, 1)

IN_LAYOUT = (
    ('ssm_u', W_SSM),
    ('lru_x', W_LRU), ('lru_y', W_LRU),
    ('da_q', DA_HEADS * 2 * DA_HEAD), ('da_k', DA_HEADS * 2 * DA_HEAD), ('da_v', W_DA),
    ('gla_q', GLA_HEADS * GLA_DK), ('gla_k', GLA_HEADS * GLA_DK), ('gla_v', W_GLA), ('gla_g', W_GLA),
    ('gla_a', 2 * GLA_RANK),
    ('gates', N_BRANCH * D_MODEL),
)
IN_TOTAL = sum(s for _, s in IN_LAYOUT)
ALL_PIECES = tuple(n for n, _ in IN_LAYOUT)
CTX_STATE_PIECES = ('ssm_u', 'lru_x', 'da_k', 'da_v', 'gla_k', 'gla_v', 'gla_a')

kernel_name = 'hybrid_gated_s5_rglru_diffattn_gla_dit'


def _rmsnorm(x, g):
    xf = x.astype(F32)
    y = xf * lax.rsqrt(jnp.mean(xf * xf, axis=-1, keepdims=True) + EPS)
    return (y * g.astype(F32)).astype(x.dtype)


def _adaln(cond, w, b, n_chunks):
    m = jax.nn.silu(cond) @ w[:, :n_chunks * D_MODEL] + b[:n_chunks * D_MODEL]
    return [t[:, None, :] for t in jnp.split(m, n_chunks, axis=-1)]


def _modulate(h, shift, scale):
    return h * (1.0 + scale) + shift


def _flip(t, rev):
    return t[:, ::-1] if (rev and t is not None) else t


def _dwconv(x, w, b, pad):
    y = lax.conv_general_dilated(x, w[:, None, :].astype(x.dtype), window_strides=(1,),
                                 padding=[pad], dimension_numbers=('NWC', 'WIO', 'NWC'),
                                 feature_group_count=x.shape[-1])
    return y + b.astype(x.dtype)


def _project(h, w_in, names):
    offs, o = {}, 0
    for n, s in IN_LAYOUT:
        offs[n] = (o, s)
        o += s
    w = w_in if names == ALL_PIECES else jnp.concatenate(
        [w_in[:, offs[n][0]:offs[n][0] + offs[n][1]] for n in names], axis=1)
    z = h @ w
    sizes = [offs[n][1] for n in names]
    parts = jnp.split(z, list(np.cumsum(sizes)[:-1]), axis=-1)
    return dict(zip(names, parts))


def _axial_rope(rows, dim):
    row = jnp.repeat(jnp.arange(rows), GRID_W).astype(F32)
    col = jnp.tile(jnp.arange(GRID_W), rows).astype(F32)
    half = dim // 2
    inv = 1.0 / (ROPE_BASE ** (jnp.arange(0, half, 2, dtype=F32) / half))
    ang = jnp.concatenate([row[:, None] * inv, col[:, None] * inv], axis=-1)
    return jnp.cos(ang), jnp.sin(ang)


def _apply_axial_rope(x, cos, sin):
    q = x.shape[-1] // 4
    xr = x.reshape(x.shape[:-1] + (2, 2, q))
    x1, x2 = xr[..., 0, :], xr[..., 1, :]
    c = cos.reshape(cos.shape[0], 2, q)
    s = sin.reshape(sin.shape[0], 2, q)
    out = jnp.stack([x1 * c - x2 * s, x1 * s + x2 * c], axis=-2)
    return out.reshape(x.shape).astype(x.dtype)


def _s5_discretize(lam_re, lam_im, log_step, b_re, b_im):
    lam_re, lam_im = lam_re.astype(F32), lam_im.astype(F32)
    dt = jnp.exp(log_step.astype(F32))[:, None]
    mag = jnp.exp(lam_re * dt)
    lb_re, lb_im = mag * jnp.cos(lam_im * dt), mag * jnp.sin(lam_im * dt)
    num_re, num_im = lb_re - 1.0, lb_im
    den = lam_re * lam_re + lam_im * lam_im
    k_re = (num_re * lam_re + num_im * lam_im) / den
    k_im = (num_im * lam_re - num_re * lam_im) / den
    b_re, b_im = b_re.astype(F32), b_im.astype(F32)
    bb_re = k_re[..., None] * b_re - k_im[..., None] * b_im
    bb_im = k_re[..., None] * b_im + k_im[..., None] * b_re
    return lb_re, lb_im, bb_re, bb_im


def _complex_affine_combine(e1, e2):
    a1r, a1i, b1r, b1i = e1
    a2r, a2i, b2r, b2i = e2
    return (a2r * a1r - a2i * a1i, a2r * a1i + a2i * a1r,
            a2r * b1r - a2i * b1i + b2r, a2r * b1i + a2i * b1r + b2i)


def _s5_scan(u, lb_re, lb_im, bb_re, bb_im, h0_re, h0_im):
    bu_re = jnp.einsum('blgc,gpc->blgp', u, bb_re)
    bu_im = jnp.einsum('blgc,gpc->blgp', u, bb_im)
    bu_re = bu_re.at[:, 0].add(lb_re * h0_re - lb_im * h0_im)
    bu_im = bu_im.at[:, 0].add(lb_re * h0_im + lb_im * h0_re)
    a_re = jnp.broadcast_to(lb_re, bu_re.shape)
    a_im = jnp.broadcast_to(lb_im, bu_im.shape)
    _, _, h_re, h_im = lax.associative_scan(_complex_affine_combine, (a_re, a_im, bu_re, bu_im), axis=1)
    return h_re, h_im


def _s5_readout(h_re, h_im, c_re, c_im):
    y = (jnp.einsum('blgp,gcp->blgc', h_re, c_re.astype(F32))
         - jnp.einsum('blgp,gcp->blgc', h_im, c_im.astype(F32)))
    return y.reshape(y.shape[0], y.shape[1], W_SSM)


def _s5_mixer(u_c, u_l, lam_re, lam_im, log_step, b_re, b_im, c_re, c_im, d_skip, w_glu, with_ctx):
    bn, dt = u_l.shape[0], u_l.dtype
    grp = lambda u: u.astype(F32).reshape(bn, u.shape[1], SSM_GROUPS, SSM_GROUP)
    g_c, g_l = grp(u_c), grp(u_l)
    d32 = d_skip.astype(F32)
    y_l = u_l.astype(F32) * d32
    y_c = u_c.astype(F32) * d32 if with_ctx else None
    zero = jnp.zeros((bn, SSM_GROUPS, SSM_STATE), F32)
    for di in range(2):
        rev = di == 1
        lb_re, lb_im, bb_re, bb_im = _s5_discretize(lam_re[di], lam_im[di], log_step[di], b_re[di], b_im[di])
        hc_re, hc_im = _s5_scan(_flip(g_c, rev), lb_re, lb_im, bb_re, bb_im, zero, zero)
        hl_re, hl_im = _s5_scan(_flip(g_l, rev), lb_re, lb_im, bb_re, bb_im, hc_re[:, -1], hc_im[:, -1])
        y_l = y_l + _flip(_s5_readout(hl_re, hl_im, c_re[di], c_im[di]), rev)
        if with_ctx:
            y_c = y_c + _flip(_s5_readout(hc_re, hc_im, c_re[di], c_im[di]), rev)

    def glu(y):
        z = jax.nn.gelu(y)
        return (z * jax.nn.sigmoid(z @ w_glu.astype(F32))).astype(dt)
    return (glu(y_c) if with_ctx else None), glu(y_l)


def _rglru_gates(xc, wr, br, wi, bi, lam):
    bn, n, _ = xc.shape
    blk = xc.reshape(bn, n, LRU_BLOCKS, LRU_BLOCK)
    r = jax.nn.sigmoid(jnp.einsum('blnc,ncd->blnd', blk, wr.astype(F32)).reshape(bn, n, W_LRU) + br.astype(F32))
    i = jax.nn.sigmoid(jnp.einsum('blnc,ncd->blnd', blk, wi.astype(F32)).reshape(bn, n, W_LRU) + bi.astype(F32))
    log_a = LRU_C * r * jax.nn.log_sigmoid(lam.astype(F32))
    a = jnp.exp(log_a)
    b = jnp.sqrt(-jnp.expm1(2.0 * log_a)) * (i * xc)
    return a, b


def _linear_scan(a, b, h0):
    b = b.at[:, 0].add(a[:, 0] * h0)
    _, h = lax.associative_scan(lambda e1, e2: (e1[0] * e2[0], e2[0] * e1[1] + e2[1]), (a, b), axis=1)
    return h


def _rglru_mixer(z_c, z_l, conv_w, conv_b, wr, br, wi, bi, lam, with_ctx):
    bn, dt = z_l['lru_x'].shape[0], z_l['lru_x'].dtype
    xc_l = _dwconv(z_l['lru_x'], conv_w, conv_b, LRU_PAD).astype(F32)
    xc_c = _dwconv(z_c['lru_x'], conv_w, conv_b, LRU_PAD).astype(F32)
    zero = jnp.zeros((bn, W_LRU), F32)
    h_l, h_c = 0.0, 0.0
    for di in range(2):
        rev = di == 1
        a_c, b_c = _rglru_gates(_flip(xc_c, rev), wr[di], br[di], wi[di], bi[di], lam[di])
        hs_c = _linear_scan(a_c, b_c, zero)
        a_l, b_l = _rglru_gates(_flip(xc_l, rev), wr[di], br[di], wi[di], bi[di], lam[di])
        hs_l = _linear_scan(a_l, b_l, hs_c[:, -1])
        h_l = h_l + _flip(hs_l, rev)
        if with_ctx:
            h_c = h_c + _flip(hs_c, rev)
    y_l = (h_l * jax.nn.gelu(z_l['lru_y'].astype(F32))).astype(dt)
    y_c = (h_c * jax.nn.gelu(z_c['lru_y'].astype(F32))).astype(dt) if with_ctx else None
    return y_c, y_l


def _diff_softmax(q, k, v, lam):
    s = jnp.einsum('bhcqd,bhckd->bhcqk', q, k).astype(F32) * (DA_HEAD ** -0.5)
    p = jax.nn.softmax(s, axis=-1)
    w = p[:, :, 0] - lam * p[:, :, 1]
    return jnp.einsum('bhqk,bhkv->bhqv', w.astype(v.dtype), v)


def _diff_attention(z_c, z_l, q_norm, k_norm, lam_p, out_norm, cos, sin, lam_init, with_ctx):
    bn, n_lat, _ = z_l['da_q'].shape

    def heads_qk(t, g):
        t = _rmsnorm(t.reshape(bn, t.shape[1], DA_HEADS, 2, DA_HEAD), g)
        return t.transpose(0, 2, 3, 1, 4)

    def heads_v(t):
        return t.reshape(bn, t.shape[1], DA_HEADS, DA_VDIM).transpose(0, 2, 1, 3)

    q_l = _apply_axial_rope(heads_qk(z_l['da_q'], q_norm), cos, sin)
    k_l = _apply_axial_rope(heads_qk(z_l['da_k'], k_norm), cos, sin)
    v_l = heads_v(z_l['da_v'])
    k_c = heads_qk(z_c['da_k'], k_norm)
    v_c = heads_v(z_c['da_v'])
    lp = lam_p.astype(F32)
    lam = jnp.exp(jnp.sum(lp[0] * lp[1])) - jnp.exp(jnp.sum(lp[2] * lp[3])) + lam_init
    k_all = jnp.concatenate([k_c, k_l], axis=3)
    v_all = jnp.concatenate([v_c, v_l], axis=2)
    nb = n_lat // Q_BLOCK
    q_blocks = q_l.reshape(bn, DA_HEADS, 2, nb, Q_BLOCK, DA_HEAD).transpose(3, 0, 1, 2, 4, 5)
    o_l = lax.map(lambda qb: _diff_softmax(qb, k_all, v_all, lam), q_blocks)
    o_l = o_l.transpose(1, 2, 0, 3, 4).reshape(bn, DA_HEADS, n_lat, DA_VDIM)

    def finish(o):
        o = _rmsnorm(o, out_norm) * (1.0 - lam_init)
        return o.transpose(0, 2, 1, 3).reshape(bn, o.shape[2], W_DA)

    y_c = finish(_diff_softmax(heads_qk(z_c['da_q'], q_norm), k_c, v_c, lam)) if with_ctx else None
    return y_c, finish(o_l)


def _gla_chunked(q, k, v, log_a, s0, with_out):
    bn, n_tok, nh, _ = k.shape
    dv = v.shape[-1]
    nc = n_tok // GLA_CHUNK
    chunks = lambda t: t.reshape(bn, nc, GLA_CHUNK, nh, t.shape[-1]).transpose(1, 0, 3, 2, 4)
    kc, vc = chunks(k), chunks(v)
    b = jnp.cumsum(chunks(log_a), axis=-2)
    b_last = b[..., -1:, :]
    kv = jnp.einsum('nbhjd,nbhjv->nbhdv', kc * jnp.exp(b_last - b), vc)
    decay = jnp.exp(b_last[..., 0, :])

    def step(s, xs):
        dec, kvn = xs
        return dec[..., None] * s + kvn, (s if with_out else None)
    s_fin, s_start = lax.scan(step, s0, (decay, kv))
    if not with_out:
        return None, s_fin
    qc = chunks(q) * jnp.exp(b)
    mask = jnp.tril(jnp.ones((GLA_CHUNK, GLA_CHUNK), dtype=bool))
    att = jnp.where(mask, jnp.einsum('nbhid,nbhjd->nbhij', qc, kc * jnp.exp(-b)), 0.0)
    o = jnp.einsum('nbhij,nbhjv->nbhiv', att, vc) + jnp.einsum('nbhid,nbhdv->nbhiv', qc, s_start)
    return o.transpose(1, 0, 3, 2, 4).reshape(bn, n_tok, nh, dv), s_fin


def _gla_mixer(z_c, z_l, wa2, ba, out_norm, with_ctx):
    bn, dt = z_l['gla_q'].shape[0], z_l['gla_q'].dtype
    heads = lambda t, d: t.astype(F32).reshape(bn, t.shape[1], GLA_HEADS, d)
    q_l = heads(z_l['gla_q'], GLA_DK) * (GLA_DK ** -0.5)
    k_l, v_l = heads(z_l['gla_k'], GLA_DK), heads(z_l['gla_v'], GLA_DV)
    q_c = heads(z_c['gla_q'], GLA_DK) * (GLA_DK ** -0.5) if with_ctx else None
    k_c, v_c = heads(z_c['gla_k'], GLA_DK), heads(z_c['gla_v'], GLA_DV)

    def log_gate(a_low, di):
        z = a_low[..., di * GLA_RANK:(di + 1) * GLA_RANK].astype(F32) @ wa2[di].astype(F32) + ba[di].astype(F32)
        return heads(jax.nn.log_sigmoid(z) / GLA_TAU, GLA_DK)

    s0 = jnp.zeros((bn, GLA_HEADS, GLA_DK, GLA_DV), F32)
    o_l, o_c = 0.0, 0.0
    for di in range(2):
        rev = di == 1
        out_c, s_c = _gla_chunked(_flip(q_c, rev), _flip(k_c, rev), _flip(v_c, rev),
                                  _flip(log_gate(z_c['gla_a'], di), rev), s0, with_ctx)
        out_l, _ = _gla_chunked(_flip(q_l, rev), _flip(k_l, rev), _flip(v_l, rev),
                                _flip(log_gate(z_l['gla_a'], di), rev), s_c, True)
        o_l = o_l + _flip(out_l, rev)
        if with_ctx:
            o_c = o_c + _flip(out_c, rev)

    def finish(o, g):
        o = _rmsnorm(o, out_norm).reshape(bn, o.shape[1], W_GLA)
        return (o * jax.nn.silu(g.astype(F32))).astype(dt)
    return (finish(o_c, z_c['gla_g']) if with_ctx else None), finish(o_l, z_l['gla_g'])


def _merge(gate_logits, branches, w_branch, w_out):
    bn, n, _ = gate_logits.shape
    g = jax.nn.sigmoid(gate_logits.reshape(bn, n, N_BRANCH, D_MODEL))
    y = jnp.stack(branches, axis=2)
    proj = jnp.einsum('blnw,nwd->blnd', y, w_branch)
    return jnp.einsum('blnd,blnd->bld', g, proj) @ w_out


def _conv_ffn(h, w_up, conv_w, conv_b, w_down):
    a, gt = jnp.split(h @ w_up, 2, axis=-1)
    gt = _dwconv(gt, conv_w, conv_b, FFN_PAD)
    return (jax.nn.gelu(gt) * a) @ w_down


def _diff_lambda_init(layer):
    return LAMBDA_INIT_BASE - LAMBDA_INIT_AMP * math.exp(-LAMBDA_INIT_RATE * layer)


def setup_inputs(seed: int = 0) -> dict:
    key = jax.random.key(seed)
    ks = iter(jax.random.split(key, 40))
    nrm = lambda shape, scale: jax.random.normal(next(ks), shape, F32) * scale
    gain = lambda shape: 1.0 + nrm(shape, 0.05)
    L2 = (DEPTH, 2)
    lam_u = jax.random.uniform(next(ks), L2 + (W_LRU,), F32, 0.9, 0.999) ** (1.0 / LRU_C)
    return {
        'x': nrm((BATCH, SEQ, D_MODEL), 1.0),
        'c': nrm((BATCH, D_MODEL), 1.0),
        'ctx': nrm((BATCH, CTX_LEN, D_MODEL), 1.0),
        'c_ctx': nrm((D_MODEL,), 1.0),
        'w_ada': nrm((DEPTH, D_MODEL, 6 * D_MODEL), 0.5 * D_MODEL ** -0.5),
        'b_ada': nrm((DEPTH, 6 * D_MODEL), 0.01),
        'norm1_g': gain((DEPTH, D_MODEL)),
        'norm2_g': gain((DEPTH, D_MODEL)),
        'w_in': nrm((DEPTH, D_MODEL, IN_TOTAL), D_MODEL ** -0.5),
        'ssm_lam_re': -0.5 + nrm(L2 + (SSM_GROUPS, SSM_STATE), 0.01),
        'ssm_lam_im': jnp.pi * jnp.arange(SSM_STATE, dtype=F32) + nrm(L2 + (SSM_GROUPS, SSM_STATE), 0.01),
        'ssm_log_step': jax.random.uniform(next(ks), L2 + (SSM_GROUPS,), F32, math.log(1e-3), math.log(1e-1)),
        'ssm_b_re': nrm(L2 + (SSM_GROUPS, SSM_STATE, SSM_GROUP), (2 * SSM_GROUP) ** -0.5),
        'ssm_b_im': nrm(L2 + (SSM_GROUPS, SSM_STATE, SSM_GROUP), (2 * SSM_GROUP) ** -0.5),
        'ssm_c_re': nrm(L2 + (SSM_GROUPS, SSM_GROUP, SSM_STATE), SSM_STATE ** -0.5),
        'ssm_c_im': nrm(L2 + (SSM_GROUPS, SSM_GROUP, SSM_STATE), SSM_STATE ** -0.5),
        'ssm_d': nrm((DEPTH, W_SSM), 0.5),
        'ssm_w_glu': nrm((DEPTH, W_SSM, W_SSM), W_SSM ** -0.5),
        'lru_conv_w': nrm((DEPTH, LRU_CONV, W_LRU), LRU_CONV ** -0.5),
        'lru_conv_b': nrm((DEPTH, W_LRU), 0.01),
        'lru_wr': nrm(L2 + (LRU_BLOCKS, LRU_BLOCK, LRU_BLOCK), LRU_BLOCK ** -0.5),
        'lru_br': nrm(L2 + (W_LRU,), 0.01),
        'lru_wi': nrm(L2 + (LRU_BLOCKS, LRU_BLOCK, LRU_BLOCK), LRU_BLOCK ** -0.5),
        'lru_bi': nrm(L2 + (W_LRU,), 0.01),
        'lru_lam': jnp.log(lam_u) - jnp.log1p(-lam_u),
        'da_q_norm': gain((DEPTH, DA_HEAD)),
        'da_k_norm': gain((DEPTH, DA_HEAD)),
        'da_lam': nrm((DEPTH, 4, DA_HEAD), 0.1),
        'da_out_norm': gain((DEPTH, DA_VDIM)),
        'gla_wa2': nrm(L2 + (GLA_RANK, GLA_HEADS * GLA_DK), GLA_RANK ** -0.5),
        'gla_ba': nrm(L2 + (GLA_HEADS * GLA_DK,), 0.1),
        'gla_out_norm': gain((DEPTH, GLA_DV)),
        'w_branch': nrm((DEPTH, N_BRANCH, W_BRANCH, D_MODEL), W_BRANCH ** -0.5),
        'w_out': nrm((DEPTH, D_MODEL, D_MODEL), D_MODEL ** -0.5),
        'w_up': nrm((DEPTH, D_MODEL, 2 * D_FF), D_MODEL ** -0.5),
        'ffn_conv_w': nrm((DEPTH, FFN_CONV, D_FF), FFN_CONV ** -0.5),
        'ffn_conv_b': nrm((DEPTH, D_FF), 0.01),
        'w_down': nrm((DEPTH, D_FF, D_MODEL), D_FF ** -0.5),
    }


def reference(x, c, ctx, c_ctx, w_ada, b_ada, norm1_g, norm2_g, w_in,
              ssm_lam_re, ssm_lam_im, ssm_log_step, ssm_b_re, ssm_b_im, ssm_c_re, ssm_c_im, ssm_d, ssm_w_glu,
              lru_conv_w, lru_conv_b, lru_wr, lru_br, lru_wi, lru_bi, lru_lam,
              da_q_norm, da_k_norm, da_lam, da_out_norm,
              gla_wa2, gla_ba, gla_out_norm,
              w_branch, w_out, w_up, ffn_conv_w, ffn_conv_b, w_down):
    rows = x.shape[1] // GRID_W
    cos, sin = _axial_rope(rows, DA_HEAD)
    h_ctx = ctx
    for l in range(DEPTH):
        with_ctx = l < DEPTH - 1
        sh1, sc1, g1, sh2, sc2, g2 = _adaln(c, w_ada[l], b_ada[l], 6)
        cmod = _adaln(c_ctx[None], w_ada[l], b_ada[l], 6 if with_ctx else 2)
        hn_l = _modulate(_rmsnorm(x, norm1_g[l]), sh1, sc1)
        hn_c = _modulate(_rmsnorm(h_ctx, norm1_g[l]), cmod[0], cmod[1])
        z_l = _project(hn_l, w_in[l], ALL_PIECES)
        z_c = _project(hn_c, w_in[l], ALL_PIECES if with_ctx else CTX_STATE_PIECES)
        ya_c, ya_l = _s5_mixer(z_c['ssm_u'], z_l['ssm_u'], ssm_lam_re[l], ssm_lam_im[l], ssm_log_step[l],
                               ssm_b_re[l], ssm_b_im[l], ssm_c_re[l], ssm_c_im[l], ssm_d[l], ssm_w_glu[l], with_ctx)
        yb_c, yb_l = _rglru_mixer(z_c, z_l, lru_conv_w[l], lru_conv_b[l], lru_wr[l], lru_br[l],
                                  lru_wi[l], lru_bi[l], lru_lam[l], with_ctx)
        yc_c, yc_l = _diff_attention(z_c, z_l, da_q_norm[l], da_k_norm[l], da_lam[l], da_out_norm[l],
                                     cos, sin, _diff_lambda_init(l), with_ctx)
        yd_c, yd_l = _gla_mixer(z_c, z_l, gla_wa2[l], gla_ba[l], gla_out_norm[l], with_ctx)
        x = x + g1 * _merge(z_l['gates'], (ya_l, yb_l, yc_l, yd_l), w_branch[l], w_out[l])
        x = x + g2 * _conv_ffn(_modulate(_rmsnorm(x, norm2_g[l]), sh2, sc2),
                               w_up[l], ffn_conv_w[l], ffn_conv_b[l], w_down[l])
        if with_ctx:
            h_ctx = h_ctx + cmod[2] * _merge(z_c['gates'], (ya_c, yb_c, yc_c, yd_c), w_branch[l], w_out[l])
            h_ctx = h_ctx + cmod[5] * _conv_ffn(_modulate(_rmsnorm(h_ctx, norm2_g[l]), cmod[3], cmod[4]),
                                                w_up[l], ffn_conv_w[l], ffn_conv_b[l], w_down[l])
    return x
```

```python
import numpy as np
from contextlib import ExitStack
import concourse.bass as bass
import concourse.mybir as mybir
from concourse.bass_utils import run_bass_kernel_spmd

F32 = mybir.dt.float32
BF16 = mybir.dt.bfloat16
I32 = mybir.dt.int32
ALU = mybir.AluOpType
AF = mybir.ActivationFunctionType

D = 1024
SEQ = 16384
NCORE = 8
TL = SEQ // NCORE
CTX = 256
EPS = 1e-6
IN_MIX = 2336
D_FF = 2816


class Prog:
    SEM_MAX = 30000
    DMA_POOL = 8

    def __init__(self, nc):
        self.nc = nc
        self.eng = {"pe": nc.tensor, "act": nc.scalar, "dve": nc.vector,
                    "pool": nc.gpsimd, "sp": nc.sync}
        self.ops = []

    def op(self, eng, fn, reads=(), writes=(), dma=False):
        norm = lambda bs: tuple(b if isinstance(b, tuple) else (b,) for b in bs)
        self.ops.append(dict(eng=eng, fn=fn, reads=norm(reads), writes=norm(writes), dma=dma))

    def dma(self, eng, out, in_, reads=(), writes=(), **kw):
        e = self.eng[eng]
        self.op(eng, lambda: e.dma_start(out=out, in_=in_, **kw), reads, writes, dma=True)

    def emit(self, stack):
        nc = self.nc
        ops = self.ops
        n = len(ops)
        last_w, readers, desc = {}, {}, {}

        def related(p):
            out = [p[:i] for i in range(1, len(p) + 1)]
            out.extend(desc.get(p, ()))
            return out

        def register(p):
            for i in range(1, len(p)):
                desc.setdefault(p[:i], set()).add(p)

        deps = [set() for _ in range(n)]
        for i, o in enumerate(ops):
            for b in o["reads"]:
                for q in related(b):
                    if q in last_w:
                        deps[i].add(last_w[q])
            for b in o["writes"]:
                for q in related(b):
                    if q in last_w:
                        deps[i].add(last_w[q])
                    for r in readers.get(q, ()):
                        if r != i:
                            deps[i].add(r)
            for b in o["reads"]:
                register(b)
                readers.setdefault(b, []).append(i)
            for b in o["writes"]:
                register(b)
                for q in list(desc.get(b, ())):
                    last_w.pop(q, None)
                    readers.pop(q, None)
                last_w[b] = i
                readers[b] = []
            deps[i].discard(i)

        def stream(o):
            return ("d:" if o["dma"] else "c:") + o["eng"]

        needed = [False] * n
        for i, o in enumerate(ops):
            si = stream(o)
            keep = {}
            for d in deps[i]:
                sd = stream(ops[d])
                if sd == si and sd == "c:pe":
                    continue
                if sd.startswith("d:"):
                    keep[(sd, d)] = d
                elif sd not in keep or d > keep[sd]:
                    keep[sd] = d
            deps[i] = keep
            for d in keep.values():
                needed[d] = True
        P = self.DMA_POOL
        cnt = {}
        semval = [None] * n
        dma_prev = [None] * n
        for i, o in enumerate(ops):
            if o["fn"] is None:
                continue
            s = stream(o)
            if o["dma"]:
                c = cnt.get(s, 0)
                cnt[s] = c + 1
                slot, m = c % P, c // P + 1
                semval[i] = ((s, slot), 16 * m)
                if m > 1:
                    dma_prev[i] = ((s, slot), 16 * (m - 1))
                continue
            if not needed[i]:
                continue
            c = cnt.get(s, 0) + 1
            cnt[s] = c
            semval[i] = ((s, "e%d" % ((c - 1) // self.SEM_MAX)), (c - 1) % self.SEM_MAX + 1)
        sems = {}

        def get_sem(k):
            if k not in sems:
                sems[k] = stack.enter_context(nc.semaphore(("s_%s_%s" % k).replace(":", "_")))
            return sems[k]

        waited = {}

        def do_wait(engname, k, v):
            kk = (engname, k)
            if waited.get(kk, -1) >= v:
                return
            waited[kk] = v
            self.eng[engname].wait_ge(get_sem(k), v)

        for i, o in enumerate(ops):
            for sd, d in sorted(deps[i].items(), key=lambda t: t[1]):
                k, v = semval[d]
                do_wait(o["eng"], k, v)
            if dma_prev[i] is not None:
                do_wait(o["eng"], *dma_prev[i])
            if o["fn"] is None:
                continue
            ins = o["fn"]()
            if semval[i] is not None:
                k, v = semval[i]
                ins.then_inc(get_sem(k), 16 if o["dma"] else 1)
        self.ops = []
        return cnt


class K:
    def __init__(self):
        self.nc = bass.Bass("TRN2", target_bir_lowering=False)
        self.p = Prog(self.nc)
        self.st = ExitStack()
        self._dq = 0
        self.psn = 0

    def din(self, name, shape, dt=F32):
        return self.nc.dram_tensor(name, list(shape), dt, kind="ExternalInput").ap()

    def dout(self, name, shape, dt=F32):
        return self.nc.dram_tensor(name, list(shape), dt, kind="ExternalOutput").ap()

    def sb(self, name, shape, dt=F32):
        return self.st.enter_context(self.nc.sbuf_tensor("sb_" + name, list(shape), dt))

    def ps(self, name, shape, dt=F32):
        return self.st.enter_context(self.nc.psum_tensor("ps_" + name, list(shape), dt))

    def dq(self):
        self._dq += 1
        return ("sp", "pool")[self._dq % 2]

    def finish(self, out_bufs):
        self.p.op("sp", None, reads=out_bufs)
        self.p.emit(self.st)
        self.st.close()
        return self.nc


def rev_ap(ap2d):
    n = ap2d.shape[-1]
    last = ap2d[:, n - 1:n]
    return bass.AP(ap2d.tensor, last.offset, [list(ap2d.ap[0]), [-ap2d.ap[-1][0], n]])


def emit_consts(k):
    nc, p = k.nc, k.p
    ones = k.sb("ones", [128, 128], F32)
    p.op("dve", lambda: nc.vector.memset(ones[:], 1.0), writes=["ones"])
    k.ones = ones


def emit_adaln(k, wada, bada_t, cvec, vec_ids, tag):
    nc, p = k.nc, k.p
    nv = len(vec_ids)
    scv = k.sb(tag + "scv", [128, 8, 2], F32)
    bsb = k.sb(tag + "bsb", [128, 48], F32)
    mod = k.sb(tag + "mod", [128, nv * 8, 2], F32)
    psa = k.ps(tag + "psa", [128, nv * 8, 2], F32)
    p.dma("sp", scv[:], cvec, writes=[tag + "scv"])
    p.dma("sp", bsb[:], bada_t, writes=[tag + "bsb"])
    p.op("act", lambda: nc.scalar.activation(out=scv[:], in_=scv[:], func=AF.Silu),
         reads=[tag + "scv"], writes=[tag + "scv"])
    wv = wada.rearrange("(k p) m -> p k m", p=128)
    wst = [k.sb(tag + "wst%d" % i, [128, 8, 128], F32) for i in range(2)]
    it = 0
    for vi, v in enumerate(vec_ids):
        for jj in range(8):
            b = it % 2
            it += 1
            c0 = v * D + jj * 128
            j = vi * 8 + jj
            p.dma(k.dq(), wst[b][:], wv[:, :, c0:c0 + 128], writes=[(tag + "wst", b)])
            for kk in range(8):
                p.op("pe", (lambda b=b, kk=kk, j=j: nc.tensor.matmul(
                    psa[:, j, :], lhsT=wst[b][:, kk, :], rhs=scv[:, kk, :],
                    start=(kk == 0), stop=(kk == 7))),
                    reads=[(tag + "wst", b), tag + "scv"], writes=[tag + "psa"])
    v0 = vec_ids[0]
    for c in range(2):
        p.op("dve", (lambda c=c: nc.vector.tensor_tensor(
            out=mod[:, :, c], in0=psa[:, :, c], in1=bsb[:, v0 * 8:(v0 + nv) * 8], op=ALU.add)),
            reads=[tag + "psa", tag + "bsb"], writes=[(tag + "mod", "c%d" % c)])
    return mod


def emit_norm_mod(k, xT, ntok, blocks, gT, mod, sh_i, sc_i, hnT, tag, xname, psb):
    nc, p = k.nc, k.p
    gsb = k.sb(tag + "g", [128, 8], F32)
    A = k.sb(tag + "A", [128, 8, 2], F32)
    p.dma("sp", gsb[:], gT, writes=[tag + "g"])
    for v in range(2):
        p.op("dve", (lambda v=v: nc.vector.scalar_tensor_tensor(
            out=A[:, :, v], in0=mod[:, sc_i * 8:(sc_i + 1) * 8, v], scalar=1.0, in1=gsb[:],
            op0=ALU.add, op1=ALU.mult)),
            reads=[tag + "g", k.modname],
            writes=[(tag + "A", v)])
    if not hasattr(k, "_nm"):
        k._nm = (k.sb("nm_rstd", [128, ntok], F32),
                 [k.sb("nm_sq%d" % i, [128, 512], F32) for i in range(2)],
                 [k.sb("nm_tmp%d" % i, [128, 512], F32) for i in range(2)])
    rstd, sq, tmp = k._nm
    it = 0
    for bi, (c0, c1, v) in enumerate(blocks):
        w = c1 - c0
        pt, pn = psb[bi % len(psb)]
        for kk in range(8):
            b = it % 2
            it += 1
            p.op("act", (lambda b=b, kk=kk, c0=c0, c1=c1, w=w: nc.scalar.activation(
                out=sq[b][:, :w], in_=xT[:, kk, c0:c1], func=AF.Square)),
                reads=[(xname, kk)], writes=[("nm_sq", b)])
            p.op("pe", (lambda b=b, kk=kk, w=w, pt=pt: nc.tensor.matmul(
                pt[:, :w], lhsT=k.ones[:], rhs=sq[b][:, :w], start=(kk == 0), stop=(kk == 7))),
                reads=["ones", ("nm_sq", b)], writes=[pn])
        p.op("dve", (lambda c0=c0, c1=c1, w=w, pt=pt: nc.vector.tensor_scalar(
            out=rstd[:, c0:c1], in0=pt[:, :w], scalar1=1.0 / D, scalar2=EPS, op0=ALU.mult, op1=ALU.add)),
            reads=[pn], writes=[("nm_rstd", bi)])
        p.op("act", (lambda c0=c0, c1=c1: nc.scalar.activation(
            out=rstd[:, c0:c1], in_=rstd[:, c0:c1], func=AF.Sqrt)),
            reads=[("nm_rstd", bi)], writes=[("nm_rstd", bi)])
        p.op("dve", (lambda c0=c0, c1=c1: nc.vector.reciprocal(
            out=rstd[:, c0:c1], in_=rstd[:, c0:c1])),
            reads=[("nm_rstd", bi)], writes=[("nm_rstd", bi)])
        for kk in range(8):
            b = it % 2
            it += 1
            p.op("dve", (lambda b=b, kk=kk, c0=c0, c1=c1, w=w: nc.vector.tensor_tensor(
                out=tmp[b][:, :w], in0=xT[:, kk, c0:c1], in1=rstd[:, c0:c1], op=ALU.mult)),
                reads=[(xname, kk), ("nm_rstd", bi)], writes=[("nm_tmp", b)])
            p.op("act", (lambda b=b, kk=kk, c0=c0, c1=c1, w=w, v=v: nc.scalar.activation(
                out=hnT[:, kk, c0:c1], in_=tmp[b][:, :w], func=AF.Identity,
                scale=A[:, kk, v:v + 1], bias=mod[:, sh_i * 8 + kk, v:v + 1])),
                reads=[("nm_tmp", b), (tag + "A", v), k.modname],
                writes=[(tag + "hnT", kk, bi)])
    return rstd


def build_phaseA():
    k = K()
    nc, p = k.nc, k.p
    NT = TL + CTX
    xT_d = k.din("xT", [D, TL])
    cT_d = k.din("cT", [D, CTX])
    cvec_d = k.din("cvec", [128, 8, 2])
    wada_d = k.din("wada", [D, 6 * D])
    bada_d = k.din("bada", [128, 48])
    g_d = k.din("g1", [128, 8])
    win_d = k.din("win", [D, IN_MIX])
    zT_d = k.dout("zT", [IN_MIX, NT])

    emit_consts(k)
    xT = k.sb("xT_sb", [128, 8, NT], F32)
    hnT = k.sb("hnT", [128, 8, NT], BF16)
    xv = xT_d.rearrange("(k p) t -> p k t", p=128)
    cv = cT_d.rearrange("(k p) t -> p k t", p=128)
    for kk in range(8):
        p.dma(k.dq(), xT[:, kk, 0:TL], xv[:, kk, :], writes=[("xT", kk)])
        p.dma(k.dq(), xT[:, kk, TL:NT], cv[:, kk, :], writes=[("xT", kk)])
    k.modname = "Amod"
    mod = emit_adaln(k, wada_d, bada_d, cvec_d, [0, 1], "A")
    banks = [(k.ps("bank%d" % i, [128, 512], F32), "bank%d" % i) for i in range(7)]
    blocks = [(i * 512, (i + 1) * 512, 0) for i in range(4)] + [(TL, NT, 1)]
    emit_norm_mod(k, xT, NT, blocks, g_d, mod, 0, 1, hnT, "n1", "xT", banks)
    wv = win_d.rearrange("(k p) m -> p k m", p=128)
    wst = [k.sb("wst%d" % i, [128, 8, 128], F32) for i in range(2)]
    wbf = [k.sb("wbf%d" % i, [128, 8, 128], BF16) for i in range(2)]
    ost = [k.sb("ost%d" % i, [128, NT], F32) for i in range(2)]
    nm = (IN_MIX + 127) // 128
    bi = 0
    for m in range(nm):
        c0 = m * 128
        mw = min(128, IN_MIX - c0)
        b = m % 2
        p.dma(k.dq(), wst[b][:, :, :mw], wv[:, :, c0:c0 + mw], writes=[("wst", b)])
        p.op("pool", (lambda b=b, mw=mw: nc.gpsimd.tensor_copy(out=wbf[b][:, :, :mw], in_=wst[b][:, :, :mw])),
             reads=[("wst", b)], writes=[("wbf", b)])
        for tb, (t0, t1, v) in enumerate(blocks):
            w = t1 - t0
            pt, pn = banks[bi % len(banks)]
            bi += 1
            for kk in range(8):
                p.op("pe", (lambda b=b, kk=kk, mw=mw, t0=t0, t1=t1, w=w, pt=pt: nc.tensor.matmul(
                    pt[:mw, :w], lhsT=wbf[b][:, kk, :mw], rhs=hnT[:, kk, t0:t1], start=(kk == 0), stop=(kk == 7))),
                    reads=[("wbf", b), ("n1hnT", kk, tb)], writes=[pn])
            if tb % 2 == 0:
                p.op("act", (lambda b=b, mw=mw, t0=t0, t1=t1, w=w, pt=pt: nc.scalar.copy(
                    out=ost[b][:mw, t0:t1], in_=pt[:mw, :w])), reads=[pn], writes=[("ost", b, tb)])
            else:
                p.op("dve", (lambda b=b, mw=mw, t0=t0, t1=t1, w=w, pt=pt: nc.vector.tensor_copy(
                    out=ost[b][:mw, t0:t1], in_=pt[:mw, :w])), reads=[pn], writes=[("ost", b, tb)])
        p.dma(k.dq(), zT_d[c0:c0 + mw, :], ost[b][:mw, :], reads=[("ost", b)], writes=[("zout", m)])
    return k.finish(["zout"])


def _ft(a):
    return np.ascontiguousarray(np.asarray(a, np.float32).reshape(-1, 128).T)


def run_phaseA(xT, cT, c, c_ctx, w_ada_l, b_ada_l, g_l, w_in_l):
    nc = build_phaseA()
    cvec = np.ascontiguousarray(np.stack([_ft(c.reshape(-1)), _ft(c_ctx.reshape(-1))], axis=-1))
    common = dict(cT=np.ascontiguousarray(cT), cvec=cvec, wada=np.ascontiguousarray(w_ada_l),
                  bada=_ft(b_ada_l), g1=_ft(g_l), win=np.ascontiguousarray(w_in_l[:, :IN_MIX]))
    in_maps = [dict(common, xT=np.ascontiguousarray(xT[:, i * TL:(i + 1) * TL])) for i in range(NCORE)]
    res = run_bass_kernel_spmd(nc, in_maps, core_ids=list(range(NCORE)))
    return [r["zT"] for r in res.results]


def _E(k, eng):
    return k.p.eng[eng]


def op_tt(k, eng, out, in0, in1, op, r, w):
    e = _E(k, eng)
    k.p.op(eng, lambda: e.tensor_tensor(out=out, in0=in0, in1=in1, op=op), r, w)


def op_ts(k, eng, out, in0, s1, s2, op0, op1, r, w):
    e = _E(k, eng)
    if op1 is None:
        k.p.op(eng, lambda: e.tensor_scalar(out=out, in0=in0, scalar1=s1, scalar2=None, op0=op0), r, w)
    else:
        k.p.op(eng, lambda: e.tensor_scalar(out=out, in0=in0, scalar1=s1, scalar2=s2, op0=op0, op1=op1), r, w)


def op_stt(k, eng, out, in0, scalar, in1, op0, op1, r, w):
    e = _E(k, eng)
    k.p.op(eng, lambda: e.scalar_tensor_tensor(out=out, in0=in0, scalar=scalar, in1=in1, op0=op0, op1=op1), r, w)


def op_act(k, out, in_, func, r, w, scale=1.0, bias=0.0):
    nc = k.nc
    k.p.op("act", lambda: nc.scalar.activation(out=out, in_=in_, func=func, bias=bias, scale=scale), r, w)


def op_copy(k, eng, out, in_, r, w):
    e = _E(k, eng)
    if eng == "act":
        k.p.op(eng, lambda: e.copy(out=out, in_=in_), r, w)
    else:
        k.p.op(eng, lambda: e.tensor_copy(out=out, in_=in_), r, w)


def op_mm(k, out, lhsT, rhs, start, stop, r, w):
    nc = k.nc
    k.p.op("pe", lambda: nc.tensor.matmul(out, lhsT=lhsT, rhs=rhs, start=start, stop=stop), r, w)


def op_scan(k, eng, out, d0, d1, init, r, w, op0=None, op1=None):
    e = _E(k, eng)
    op0 = op0 or ALU.mult
    op1 = op1 or ALU.add
    k.p.op(eng, lambda: e.tensor_tensor_scan(out=out, data0=d0, data1=d1, initial=init, op0=op0, op1=op1), r, w)


def op_memset(k, eng, ap, val, w):
    e = _E(k, eng)
    k.p.op(eng, lambda: e.memset(ap, val), (), w)


def bcast_cols(col_ap, n):
    return bass.AP(col_ap.tensor, col_ap.offset, [list(col_ap.ap[0]), [0, n]])


def emit_identity(k):
    nc = k.nc
    io = k.sb("ident_i", [128, 128], I32)
    ident = k.sb("ident", [128, 128], F32)
    k.p.op("pool", lambda: nc.gpsimd.iota(io[:], pattern=[[1, 128]], base=0, channel_multiplier=-1), (), ["ident_i"])
    op_copy(k, "dve", ident[:], io[:], ["ident_i"], ["ident"])
    op_ts(k, "dve", ident[:], ident[:], 0.0, None, ALU.is_equal, None, ["ident"], ["ident"])
    k.ident = ident
    return ident


TWO_PI = 6.283185307179586
C1_2PI = 6.28125
C2_2PI = TWO_PI - 6.28125
PI_SAFE = 3.1415925


def emit_sin(k, out, ang, shift, tmp, tmpi, r, w, tag):
    wn = [tag + "_t"]
    wi = [tag + "_i"]
    op_ts(k, "dve", tmp, ang, shift, 1.0 / TWO_PI, ALU.add, ALU.mult, r, wn)
    op_copy(k, "dve", tmpi, tmp, wn, wi)
    op_copy(k, "dve", tmp, tmpi, wi, wn)
    op_stt(k, "dve", out, tmp, -C1_2PI, ang, ALU.mult, ALU.add, wn + list(r), w)
    if shift != 0.0:
        op_ts(k, "dve", out, out, shift, None, ALU.add, None, w, w)
    op_stt(k, "dve", out, tmp, -C2_2PI, out, ALU.mult, ALU.add, wn + list(w), w)
    op_ts(k, "dve", out, out, -PI_SAFE, PI_SAFE, ALU.max, ALU.min, w, w)
    op_act(k, out, out, AF.Sin, w, w)


NSEQ = CTX + SEQ
S5_T = 1024


def build_s5():
    k = K()
    nc, p = k.nc, k.p
    T = S5_T
    u_d = k.din("uT", [32, NSEQ])
    lam_d = k.din("lam", [128, 6])
    bp_d = k.din("bpad", [128, 2, 2, 32])
    ct_d = k.din("ctp", [128, 2, 2, 32])
    y_d = k.dout("ys", [32, NSEQ])
    emit_consts(k)
    ident = emit_identity(k)
    lam = k.sb("lam", [128, 6], F32)
    bp = k.sb("bp", [128, 2, 2, 32], F32)
    ct = k.sb("ct", [128, 2, 2, 32], F32)
    p.dma("sp", lam[:], lam_d, writes=["lam"])
    p.dma("sp", bp[:], bp_d, writes=["bp"])
    p.dma("sp", ct[:], ct_d, writes=["ct"])
    for di in range(2):
        op_ts(k, "dve", ct[:, di, 1, :], ct[:, di, 1, :], -1.0, None, ALU.mult, None, ["ct"], ["ct"])
    sc = k.sb("s5sc", [128, 40], F32)
    sci = k.sb("s5sci", [128, 8], I32)
    col = lambda i: sc[:, i:i + 1]
    bbT = [[k.sb("bbT%d%d" % (di, ri), [32, 128], F32) for ri in range(2)] for di in range(2)]
    bbf = k.sb("bbf", [128, 2, 2, 32], F32)
    pst = k.ps("pst", [32, 4, 128], F32)
    for di in range(2):
        b0 = 16 * di
        S = lambda i: col(b0 + i)
        rw = ["s5sc"]
        op_act(k, S(0), lam[:, 4 + di:5 + di], AF.Exp, ["lam"], rw)
        op_tt(k, "dve", S(1), lam[:, di:di + 1], S(0), ALU.mult, ["lam"] + rw, rw)
        op_act(k, S(1), S(1), AF.Exp, rw, rw)
        op_tt(k, "dve", S(2), lam[:, 2 + di:3 + di], S(0), ALU.mult, ["lam"] + rw, rw)
        emit_sin(k, S(4), S(2), 0.0, S(13), sci[:, 0:1], rw, rw, "s5r")
        emit_sin(k, S(3), S(2), 0.5 * np.pi, S(13), sci[:, 0:1], rw, rw, "s5r")
        op_tt(k, "dve", S(5), S(1), S(3), ALU.mult, rw, rw)
        op_tt(k, "dve", S(6), S(1), S(4), ALU.mult, rw, rw)
        op_ts(k, "dve", S(7), S(5), -1.0, None, ALU.add, None, rw, rw)
        op_tt(k, "dve", S(8), lam[:, di:di + 1], lam[:, di:di + 1], ALU.mult, ["lam"], rw)
        op_stt(k, "dve", S(8), lam[:, 2 + di:3 + di], lam[:, 2 + di:3 + di], S(8), ALU.mult, ALU.add, ["lam"] + rw, rw)
        op_copy(k, "dve", S(12), S(8), rw, rw)
        k.p.op("dve", (lambda o=S(8), i=S(12): nc.vector.reciprocal(out=o, in_=i)), rw, rw)
        op_tt(k, "dve", S(11), S(7), lam[:, di:di + 1], ALU.mult, ["lam"] + rw, rw)
        op_stt(k, "dve", S(9), S(6), lam[:, 2 + di:3 + di], S(11), ALU.mult, ALU.add, ["lam"] + rw, rw)
        op_tt(k, "dve", S(9), S(9), S(8), ALU.mult, rw, rw)
        op_tt(k, "dve", S(11), S(7), lam[:, 2 + di:3 + di], ALU.mult, ["lam"] + rw, rw)
        op_stt(k, "dve", S(10), S(6), lam[:, di:di + 1], S(11), ALU.mult, ALU.subtract, ["lam"] + rw, rw)
        op_tt(k, "dve", S(10), S(10), S(8), ALU.mult, rw, rw)
        op_ts(k, "dve", bbf[:, di, 0, :], bp[:, di, 1, :], S(10), -1.0, ALU.mult, ALU.mult, ["bp"] + rw, [("bbf", di, 0)])
        op_stt(k, "dve", bbf[:, di, 0, :], bp[:, di, 0, :], S(9), bbf[:, di, 0, :], ALU.mult, ALU.add, ["bp", ("bbf", di, 0)] + rw, [("bbf", di, 0)])
        op_ts(k, "dve", bbf[:, di, 1, :], bp[:, di, 0, :], S(10), None, ALU.mult, None, ["bp"] + rw, [("bbf", di, 1)])
        op_stt(k, "dve", bbf[:, di, 1, :], bp[:, di, 1, :], S(9), bbf[:, di, 1, :], ALU.mult, ALU.add, ["bp", ("bbf", di, 1)] + rw, [("bbf", di, 1)])
        for ri in range(2):
            op_mm(k, pst[:, di * 2 + ri, :], bbf[:, di, ri, :], ident[:], True, True, [("bbf", di, ri), "ident"], ["pst"])
            op_copy(k, "dve", bbT[di][ri][:], pst[:, di * 2 + ri, :], ["pst"], [("bbT", di, ri)])
    jfi = k.sb("jfi", [128, T], I32)
    jf = k.sb("jf", [128, T], F32)
    tang = k.sb("tang", [128, T], F32)
    ttmp = k.sb("ttmp", [128, T], F32)
    tti = k.sb("tti", [128, T], I32)
    p.op("pool", lambda: nc.gpsimd.iota(jfi[:], pattern=[[1, T]], base=0, channel_multiplier=0), (), ["jfi"])
    op_copy(k, "dve", jf[:], jfi[:], ["jfi"], ["jf"])
    tab = [[k.sb("tab%d%d" % (di, cs), [128, T], F32) for cs in range(2)] for di in range(2)]
    for di in range(2):
        op_ts(k, "dve", tang[:], jf[:], col(16 * di + 2), None, ALU.mult, None, ["jf", "s5sc"], ["tang"])
        emit_sin(k, tab[di][1][:], tang[:], 0.0, ttmp[:], tti[:], ["tang"], [("tab", di, 1)], "s5t")
        emit_sin(k, tab[di][0][:], tang[:], 0.5 * np.pi, ttmp[:], tti[:], ["tang"], [("tab", di, 0)], "s5t")
    segs = [(0, CTX)] + [(CTX + i * T, CTX + (i + 1) * T) for i in range(SEQ // T)]
    ub = [k.sb("ub%d" % i, [32, T], F32) for i in range(2)]
    bur = [k.ps("bur%d" % i, [128, 512], F32) for i in range(2)]
    bui = [k.ps("bui%d" % i, [128, 512], F32) for i in range(2)]
    yps = [k.ps("yps%d" % i, [32, 512], F32) for i in range(2)]
    W = {n: k.sb("s5" + n, [128, T], F32) for n in ("br", "bi", "pr", "pi", "hr", "hi", "t1", "t2")}
    yst = [k.sb("yst%d" % i, [32, T], F32) for i in range(2)]
    carry = k.sb("carry", [128, 4], F32)
    it = 0
    for di in range(2):
        order = segs if di == 0 else [segs[0]] + segs[:0:-1]
        cosT, sinT = tab[di]
        rcol = col(16 * di + 1)
        cth, sth = cosT[:, 1:2], sinT[:, 1:2]
        for si, (t0, t1) in enumerate(order):
            n = t1 - t0
            b = it % 2
            it += 1
            first = si == 0
            fw = (lambda ap: ap) if di == 0 else rev_ap
            cv = cosT[:, 0:n] if di == 0 else rev_ap(cosT[:, 0:n])
            sv = sinT[:, 0:n] if di == 0 else rev_ap(sinT[:, 0:n])
            tabr = [("tab", di, 0), ("tab", di, 1)]
            p.dma("sp", ub[b][:, :n], u_d[:, t0:t1], writes=[("ub", b)])
            nb = (n + 511) // 512
            for j in range(nb):
                c0, c1 = j * 512, min(n, (j + 1) * 512)
                w = c1 - c0
                op_mm(k, bur[j][:, :w], bbT[di][0][:], ub[b][:, c0:c1], True, True, [("bbT", di, 0), ("ub", b)], [("bur", j)])
                op_mm(k, bui[j][:, :w], bbT[di][1][:], ub[b][:, c0:c1], True, True, [("bbT", di, 1), ("ub", b)], [("bui", j)])
                op_tt(k, "dve", W["t1"][:, c0:c1], bur[j][:, :w], cv[:, c0:c1], ALU.mult, [("bur", j)] + tabr, [("t1", j)])
                op_tt(k, "dve", W["t2"][:, c0:c1], bui[j][:, :w], sv[:, c0:c1], ALU.mult, [("bui", j)] + tabr, [("t2", j)])
                op_tt(k, "pool", W["br"][:, c0:c1], W["t1"][:, c0:c1], W["t2"][:, c0:c1], ALU.add, [("t1", j), ("t2", j)], [("br", j)])
                op_tt(k, "dve", W["t1"][:, c0:c1], bui[j][:, :w], cv[:, c0:c1], ALU.mult, [("bui", j)] + tabr, [("t1", j)])
                op_tt(k, "dve", W["t2"][:, c0:c1], bur[j][:, :w], sv[:, c0:c1], ALU.mult, [("bur", j)] + tabr, [("t2", j)])
                op_tt(k, "pool", W["bi"][:, c0:c1], W["t1"][:, c0:c1], W["t2"][:, c0:c1], ALU.subtract, [("t1", j), ("t2", j)], [("bi", j)])
            rb = bcast_cols(rcol, n)
            ire = 0.0 if first else carry[:, 2:3]
            iim = 0.0 if first else carry[:, 3:4]
            op_scan(k, "dve", fw(W["pr"][:, :n]), rb, fw(W["br"][:, :n]), ire, ["br", "s5sc", "carry"], ["pr"])
            op_scan(k, "dve", fw(W["pi"][:, :n]), rb, fw(W["bi"][:, :n]), iim, ["bi", "s5sc", "carry"], ["pi"])
            op_tt(k, "dve", W["t1"][:, :n], W["pr"][:, :n], cv, ALU.mult, ["pr"] + tabr, ["t1"])
            op_tt(k, "pool", W["t2"][:, :n], W["pi"][:, :n], sv, ALU.mult, ["pi"] + tabr, ["t2"])
            op_tt(k, "dve", W["hr"][:, :n], W["t1"][:, :n], W["t2"][:, :n], ALU.subtract, ["t1", "t2"], ["hr"])
            op_tt(k, "pool", W["t1"][:, :n], W["pr"][:, :n], sv, ALU.mult, ["pr"] + tabr, ["t1"])
            op_tt(k, "dve", W["t2"][:, :n], W["pi"][:, :n], cv, ALU.mult, ["pi"] + tabr, ["t2"])
            op_tt(k, "pool", W["hi"][:, :n], W["t1"][:, :n], W["t2"][:, :n], ALU.add, ["t1", "t2"], ["hi"])
            lc = n - 1 if di == 0 else 0
            op_tt(k, "dve", carry[:, 0:1], W["hr"][:, lc:lc + 1], cth, ALU.mult, ["hr"] + tabr, ["carry0"])
            op_tt(k, "dve", carry[:, 1:2], W["hi"][:, lc:lc + 1], sth, ALU.mult, ["hi"] + tabr, ["carry1"])
            op_tt(k, "dve", carry[:, 2:3], carry[:, 0:1], carry[:, 1:2], ALU.subtract, ["carry0", "carry1", "pr", "pi"], ["carry"])
            op_tt(k, "dve", carry[:, 0:1], W["hr"][:, lc:lc + 1], sth, ALU.mult, ["hr", "carry"] + tabr, ["carry0"])
            op_tt(k, "dve", carry[:, 1:2], W["hi"][:, lc:lc + 1], cth, ALU.mult, ["hi", "carry"] + tabr, ["carry1"])
            op_tt(k, "dve", carry[:, 3:4], carry[:, 0:1], carry[:, 1:2], ALU.add, ["carry0", "carry1"], ["carry"])
            if di == 1:
                p.dma("pool", yst[b][:, :n], y_d[:, t0:t1], reads=[("yout", t0)], writes=[("yst", b)])
            for j in range(nb):
                c0, c1 = j * 512, min(n, (j + 1) * 512)
                w = c1 - c0
                op_mm(k, yps[j][:, :w], ct[:, di, 0, :], W["hr"][:, c0:c1], True, False, ["ct", "hr"], [("yps", j)])
                op_mm(k, yps[j][:, :w], ct[:, di, 1, :], W["hi"][:, c0:c1], False, True, ["ct", "hi"], [("yps", j)])
                if di == 0:
                    op_copy(k, "act", yst[b][:, c0:c1], yps[j][:, :w], [("yps", j)], [("yst", b)])
                else:
                    op_tt(k, "dve", yst[b][:, c0:c1], yst[b][:, c0:c1], yps[j][:, :w], ALU.add, [("yps", j), ("yst", b)], [("yst", b)])
            p.dma("sp", y_d[:, t0:t1], yst[b][:, :n], reads=[("yst", b)], writes=[("yout", t0)])
    return k.finish(["yout"])


def s5_inputs(core, zu_all, lam_re, lam_im, lstep, b_re, b_im, c_re, c_im):
    g0 = 2 * core
    lam = np.zeros((128, 6), np.float32)
    bp = np.zeros((128, 2, 2, 32), np.float32)
    ct = np.zeros((128, 2, 2, 32), np.float32)
    for di in range(2):
        for gl in range(2):
            g = g0 + gl
            sl = slice(gl * 64, (gl + 1) * 64)
            lam[sl, di] = lam_re[di, g]
            lam[sl, 2 + di] = lam_im[di, g]
            lam[sl, 4 + di] = lstep[di, g]
            bp[sl, di, 0, gl * 16:(gl + 1) * 16] = b_re[di, g]
            bp[sl, di, 1, gl * 16:(gl + 1) * 16] = b_im[di, g]
            ct[sl, di, 0, gl * 16:(gl + 1) * 16] = c_re[di, g].T
            ct[sl, di, 1, gl * 16:(gl + 1) * 16] = c_im[di, g].T
    return dict(uT=np.ascontiguousarray(zu_all[32 * core:32 * core + 32]), lam=lam, bpad=bp, ctp=ct)


LRU_T = 1024


def build_lru():
    k = K()
    nc, p = k.nc, k.p
    T = LRU_T
    x_d = k.din("lxT", [32, NSEQ])
    y_d = k.din("lyT", [32, NSEQ])
    par_d = k.din("lpar", [32, 12])
    w_d = k.din("lw", [32, 2, 2, 32])
    o_d = k.dout("lo", [32, NSEQ])
    par = k.sb("lpar", [32, 12], F32)
    w = k.sb("lw", [32, 2, 2, 32], F32)
    cl = k.sb("lcl", [32, 2], F32)
    p.dma("sp", par[:], par_d, writes=["lpar"])
    p.dma("sp", w[:], w_d, writes=["lw"])
    op_act(k, cl[:], par[:, 9:11], AF.Exp, ["lpar"], ["lcl"], scale=-1.0)
    op_ts(k, "dve", cl[:], cl[:], 1.0, None, ALU.add, None, ["lcl"], ["lcl"])
    op_act(k, cl[:], cl[:], AF.Ln, ["lcl"], ["lcl"])
    op_ts(k, "dve", cl[:], cl[:], -8.0, None, ALU.mult, None, ["lcl"], ["lcl"])
    segs = [(0, CTX, 0, CTX)] + [(CTX + i * T, CTX + (i + 1) * T, CTX, NSEQ) for i in range(SEQ // T)]
    xs = [k.sb("lxs%d" % i, [32, T + 3], F32) for i in range(2)]
    W = {n: k.sb("l" + n, [32, T], F32) for n in ("xc", "r", "i", "a", "q", "b", "h")}
    ys = [k.sb("lys%d" % i, [32, T], F32) for i in range(2)]
    hf = [k.sb("lhf%d" % i, [32, T], F32) for i in range(2)]
    pr = [k.ps("lpr%d" % i, [32, 512], F32) for i in range(2)]
    pi = [k.ps("lpi%d" % i, [32, 512], F32) for i in range(2)]
    carry = k.sb("lcarry", [32, 1], F32)
    it = 0
    for di in range(2):
        order = segs if di == 0 else [segs[0]] + segs[:0:-1]
        for si, (t0, t1, lo, hi) in enumerate(order):
            n = t1 - t0
            b = it % 2
            it += 1
            fw = (lambda ap: ap) if di == 0 else rev_ap
            a0, a1 = max(lo, t0 - 2), min(hi, t1 + 1)
            if a0 > t0 - 2:
                op_memset(k, "pool", xs[b][:, 0:2], 0.0, [("lxs", b)])
            if a1 < t1 + 1:
                op_memset(k, "pool", xs[b][:, n + 2:n + 3], 0.0, [("lxs", b)])
            p.dma("sp", xs[b][:, a0 - (t0 - 2):a1 - (t0 - 2)], x_d[:, a0:a1], writes=[("lxs", b)])
            X = xs[b]
            op_ts(k, "dve", W["xc"][:, :n], X[:, 0:n], par[:, 0:1], par[:, 4:5], ALU.mult, ALU.add, [("lxs", b), "lpar"], ["xc"])
            for j in range(1, 4):
                op_stt(k, "dve", W["xc"][:, :n], X[:, j:j + n], par[:, j:j + 1], W["xc"][:, :n], ALU.mult, ALU.add, [("lxs", b), "lpar", "xc"], ["xc"])
            nb = (n + 511) // 512
            for j in range(nb):
                c0, c1 = j * 512, min(n, (j + 1) * 512)
                wd = c1 - c0
                op_mm(k, pr[j][:, :wd], w[:, di, 0, :], W["xc"][:, c0:c1], True, True, ["lw", "xc"], [("lpr", j)])
                op_mm(k, pi[j][:, :wd], w[:, di, 1, :], W["xc"][:, c0:c1], True, True, ["lw", "xc"], [("lpi", j)])
                op_act(k, W["r"][:, c0:c1], pr[j][:, :wd], AF.Sigmoid, [("lpr", j), "lpar"], [("r", j)], bias=par[:, 5 + di:6 + di])
                op_act(k, W["i"][:, c0:c1], pi[j][:, :wd], AF.Sigmoid, [("lpi", j), "lpar"], [("i", j)], bias=par[:, 7 + di:8 + di])
            op_act(k, W["a"][:, :n], W["r"][:, :n], AF.Exp, ["r", "lcl"], ["a"], scale=cl[:, di:di + 1])
            op_tt(k, "dve", W["q"][:, :n], W["a"][:, :n], W["a"][:, :n], ALU.mult, ["a"], ["q"])
            op_ts(k, "dve", W["q"][:, :n], W["q"][:, :n], -1.0, 1.0, ALU.mult, ALU.add, ["q"], ["q"])
            op_act(k, W["q"][:, :n], W["q"][:, :n], AF.Sqrt, ["q"], ["q"])
            op_tt(k, "pool", W["b"][:, :n], W["i"][:, :n], W["xc"][:, :n], ALU.mult, ["i", "xc"], ["b"])
            op_tt(k, "dve", W["b"][:, :n], W["b"][:, :n], W["q"][:, :n], ALU.mult, ["b", "q"], ["b"])
            init = 0.0 if si == 0 else carry[:, 0:1]
            op_scan(k, "dve", fw(W["h"][:, :n]), fw(W["a"][:, :n]), fw(W["b"][:, :n]), init, ["a", "b", "lcarry"], ["h"])
            lc = n - 1 if di == 0 else 0
            op_copy(k, "dve", carry[:, 0:1], W["h"][:, lc:lc + 1], ["h"], ["lcarry"])
            if di == 0:
                p.dma("pool", o_d[:, t0:t1], W["h"][:, :n], reads=["h"], writes=[("lout", t0)])
            else:
                p.dma("pool", hf[b][:, :n], o_d[:, t0:t1], reads=[("lout", t0)], writes=[("lhf", b)])
                p.dma("sp", ys[b][:, :n], y_d[:, t0:t1], writes=[("lys", b)])
                op_act(k, ys[b][:, :n], ys[b][:, :n], AF.Gelu_apprx_tanh, [("lys", b)], [("lys", b)])
                op_tt(k, "pool", hf[b][:, :n], hf[b][:, :n], W["h"][:, :n], ALU.add, [("lhf", b), "h"], [("lhf", b)])
                op_tt(k, "dve", hf[b][:, :n], hf[b][:, :n], ys[b][:, :n], ALU.mult, [("lhf", b), ("lys", b)], [("lhf", b)])
                p.dma("sp", o_d[:, t0:t1], hf[b][:, :n], reads=[("lhf", b)], writes=[("lout", t0)])
    return k.finish(["lout"])


def lru_inputs(core, zx_all, zy_all, conv_w, conv_b, wr, br, wi, bi, lam):
    ch = slice(32 * core, 32 * core + 32)
    par = np.zeros((32, 12), np.float32)
    par[:, 0:4] = conv_w[:, ch].T
    par[:, 4] = conv_b[ch]
    for di in range(2):
        par[:, 5 + di] = br[di, ch]
        par[:, 7 + di] = bi[di, ch]
        par[:, 9 + di] = lam[di, ch]
    w = np.zeros((32, 2, 2, 32), np.float32)
    for di in range(2):
        w[:, di, 0, :] = wr[di, core]
        w[:, di, 1, :] = wi[di, core]
    return dict(lxT=np.ascontiguousarray(zx_all[ch]), lyT=np.ascontiguousarray(zy_all[ch]), lpar=par, lw=w)


GLA_C = 64
NCHUNK = NSEQ // GLA_C


def build_gla():
    k = K()
    nc, p = k.nc, k.p
    q_d = k.din("gq", [32, NSEQ])
    k_d = k.din("gk", [32, NSEQ])
    a_d = k.din("ga", [32, NSEQ])
    vt_d = k.din("gvt", [64, NCHUNK, 32])
    kt_d = k.din("gkt", [64, NCHUNK, 32])
    w_d = k.din("gw", [33, 2, 32])
    o_d = k.dout("go", [32, NSEQ])
    ioi = k.sb("gioi", [64, 64], I32)
    iof = k.sb("giof", [64, 64], F32)
    p.op("pool", lambda: nc.gpsimd.iota(ioi[:], pattern=[[1, 64]], base=0, channel_multiplier=-1), (), ["gioi"])
    op_copy(k, "dve", iof[:], ioi[:], ["gioi"], ["giof"])
    msk = {}
    for nm, cmp in (("ge", ALU.is_ge), ("le", ALU.is_le), ("lt", ALU.is_lt), ("gt", ALU.is_gt)):
        msk[nm] = k.sb("gm" + nm, [64, 64], F32)
        op_ts(k, "dve", msk[nm][:], iof[:], 0.0, None, cmp, None, ["giof"], ["gm" + nm])
    amask, triI, triS = [], [], []
    for di in range(2):
        am = k.sb("gam%d" % di, [64, 8, 64], F32)
        src = msk["ge"] if di == 0 else msk["le"]
        for n in range(8):
            op_copy(k, "dve", am[:, n, :], src[:], ["gmge", "gmle"], ["gam%d" % di])
        ti = k.sb("gti%d" % di, [64, 64], F32)
        ts_ = k.sb("gts%d" % di, [64, 64], F32)
        op_ts(k, "dve", ti[:], src[:], -1.0 / 16, None, ALU.mult, None, ["gmge", "gmle"], ["gti%d" % di])
        op_ts(k, "dve", ts_[:], (msk["lt"] if di == 0 else msk["gt"])[:], -1.0 / 16, None, ALU.mult, None, ["gmlt", "gmgt"], ["gts%d" % di])
        amask.append(am); triI.append(ti); triS.append(ts_)
    w = k.sb("gw", [33, 2, 32], F32)
    p.dma("sp", w[:], w_d, writes=["gw"])
    NB = 2
    a1 = [k.sb("ga1%d" % i, [33, 512], F32) for i in range(NB)]
    qb = [k.sb("gqb%d" % i, [32, 512], F32) for i in range(NB)]
    kb = [k.sb("gkb%d" % i, [32, 512], F32) for i in range(NB)]
    vt = [k.sb("gvt%d" % i, [64, 8, 32], F32) for i in range(NB)]
    kt = [k.sb("gkt%d" % i, [64, 8, 32], F32) for i in range(NB)]
    of = [k.sb("gof%d" % i, [32, 512], F32) for i in range(NB)]
    for i in range(NB):
        op_memset(k, "pool", a1[i][32:33, :], 1.0, [("ga1", i, "one")])
    L = k.sb("gL", [64, 8, 32], F32)
    kd = k.sb("gkd", [64, 8, 32], F32)
    eb = k.sb("geb", [32, 8, 64], F32)
    enb = k.sb("genb", [32, 8, 64], F32)
    qe = k.sb("gqe", [32, 512], F32)
    ke = k.sb("gke", [32, 512], F32)
    attm = k.sb("gattm", [64, 8, 64], F32)
    Sl = k.sb("gS", [32, 9, 32], F32)
    pz = k.ps("gpz", [64, 8, 32], F32)
    pg = k.ps("gpg", [64, 8, 32], F32)
    pb = k.ps("gpb", [32, 8, 64], F32)
    pa = k.ps("gpa", [64, 8, 64], F32)
    pkv = k.ps("gpkv", [32, 8, 32], F32)
    po = k.ps("gpo", [32, 8, 64], F32)
    blocks = [(0, 4)] + [(4 + 8 * i, 8) for i in range(32)]
    it = 0
    for di in range(2):
        order = blocks if di == 0 else [blocks[0]] + blocks[:0:-1]
        for bi_, (n0, nch) in enumerate(order):
            b = it % NB
            it += 1
            t0, n = n0 * 64, nch * 64
            t1 = t0 + n
            p.dma("sp", a1[b][0:32, :n], a_d[:, t0:t1], writes=[("ga1", b, "a")])
            p.dma("pool", qb[b][:, :n], q_d[:, t0:t1], writes=[("gqb", b)])
            p.dma("sp", kb[b][:, :n], k_d[:, t0:t1], writes=[("gkb", b)])
            p.dma("pool", vt[b][:, :nch, :], vt_d[:, n0:n0 + nch, :], writes=[("gvt", b)])
            p.dma("sp", kt[b][:, :nch, :], kt_d[:, n0:n0 + nch, :], writes=[("gkt", b)])
            if di == 1:
                p.dma("pool", of[b][:, :n], o_d[:, t0:t1], reads=[("gout", t0)], writes=[("gof", b)])
            if bi_ == 0:
                op_memset(k, "dve", Sl[:, 0, :], 0.0, [("gS", 0)])
            for c in range(nch):
                op_mm(k, pz[:, c, :], a1[b][:, c * 64:(c + 1) * 64], w[:, di, :], True, True, [("ga1", b), "gw"], ["gpz"])
            op_act(k, L[:, :nch, :], pz[:, :nch, :], AF.Exp, ["gpz"], ["gL"], scale=-1.0)
            op_ts(k, "dve", L[:, :nch, :], L[:, :nch, :], 1.0, None, ALU.add, None, ["gL"], ["gL"])
            op_act(k, L[:, :nch, :], L[:, :nch, :], AF.Ln, ["gL"], ["gL"])
            op_mm(k, pg[:, :nch, :], triS[di][:], L[:, :nch, :], True, True, ["gts%d" % di, "gL"], ["gpg"])
            op_act(k, kd[:, :nch, :], pg[:, :nch, :], AF.Exp, ["gpg"], ["gkd"])
            op_tt(k, "dve", kd[:, :nch, :], kd[:, :nch, :], kt[b][:, :nch, :], ALU.mult, ["gkd", ("gkt", b)], ["gkd"])
            for c in range(nch):
                op_mm(k, pb[:, c, :], L[:, c, :], triI[di][:], True, True, ["gL", "gti%d" % di], ["gpb"])
            op_act(k, eb[:, :nch, :], pb[:, :nch, :], AF.Exp, ["gpb"], ["geb"])
            op_act(k, enb[:, :nch, :], pb[:, :nch, :], AF.Exp, ["gpb"], ["genb"], scale=-1.0)
            ebf = eb[:].rearrange("p c j -> p (c j)")
            enbf = enb[:].rearrange("p c j -> p (c j)")
            op_stt(k, "dve", qe[:, :n], ebf[:, :n], GLA_C_SCALE, qb[b][:, :n], ALU.mult, ALU.mult, ["geb", ("gqb", b)], ["gqe"])
            op_tt(k, "pool", ke[:, :n], enbf[:, :n], kb[b][:, :n], ALU.mult, ["genb", ("gkb", b)], ["gke"])
            for c in range(nch):
                op_mm(k, pa[:, c, :], ke[:, c * 64:(c + 1) * 64], qe[:, c * 64:(c + 1) * 64], True, True, ["gke", "gqe"], ["gpa"])
            op_tt(k, "dve", attm[:, :nch, :], pa[:, :nch, :], amask[di][:, :nch, :], ALU.mult, ["gpa", "gam%d" % di], ["gattm"])
            for c in range(nch):
                op_mm(k, pkv[:, c, :], kd[:, c, :], vt[b][:, c, :], True, True, ["gkd", ("gvt", b)], ["gpkv"])
            cho = list(range(nch)) if di == 0 else list(range(nch - 1, -1, -1))
            lastj = 63 if di == 0 else 0
            for i, c in enumerate(cho):
                op_stt(k, "dve", Sl[:, i + 1, :], Sl[:, i, :], eb[:, c, lastj:lastj + 1], pkv[:, c, :], ALU.mult, ALU.add,
                       [("gS", i), "geb", "gpkv"], [("gS", i + 1)])
            for i, c in enumerate(cho):
                op_mm(k, po[:, c, :], vt[b][:, c, :], attm[:, c, :], True, False, [("gvt", b), "gattm"], ["gpo"])
                op_mm(k, po[:, c, :], Sl[:, i, :], qe[:, c * 64:(c + 1) * 64], False, True, [("gS", i), "gqe"], ["gpo"])
            pof = po[:].rearrange("p c j -> p (c j)")
            if di == 0:
                op_copy(k, "act", of[b][:, :n], pof[:, :n], ["gpo"], [("gof", b)])
            else:
                op_tt(k, "dve", of[b][:, :n], of[b][:, :n], pof[:, :n], ALU.add, ["gpo", ("gof", b)], [("gof", b)])
            p.dma("sp", o_d[:, t0:t1], of[b][:, :n], reads=[("gof", b)], writes=[("gout", t0)])
            op_copy(k, "dve", Sl[:, 0, :], Sl[:, nch, :], [("gS", nch)], [("gS", 0)])
    return k.finish(["gout"])


GLA_C_SCALE = 32 ** -0.5


def gla_inputs(core, zq, zk, zv, zg_a, wa2, ba):
    h, vh = core // 2, core % 2
    qT = np.ascontiguousarray(zq[32 * h:32 * h + 32])
    kT = np.ascontiguousarray(zk[32 * h:32 * h + 32])
    vT = zv[64 * h + 32 * vh:64 * h + 32 * vh + 32]
    tok = lambda xT: np.ascontiguousarray(xT.T.reshape(NCHUNK, 64, 32).transpose(1, 0, 2))
    w = np.zeros((33, 2, 32), np.float32)
    for di in range(2):
        w[16 * di:16 * di + 16, di, :] = wa2[di][:, 32 * h:32 * h + 32]
        w[32, di, :] = ba[di][32 * h:32 * h + 32]
    return dict(gq=qT, gk=kT, ga=np.ascontiguousarray(zg_a), gvt=tok(vT), gkt=tok(kT), gw=w)


NKT = NSEQ // 128
QH = SEQ // 2
LN1E4 = float(np.log(10000.0))


def build_da(lam_init, nqb=QH // 512):
    k = K()
    nc, p = k.nc, k.p
    q_d = k.din("dq", [64, QH])
    k_d = k.din("dk", [64, NSEQ])
    vt_d = k.din("dvt", [128, NKT, 64])
    qc_d = k.din("dqc", [64, CTX])
    par_d = k.din("dpar", [64, 4])
    lam_d = k.din("dlam", [32, 4])
    y_d = k.dout("dy", [64, QH])
    yc_d = k.dout("dyc", [64, CTX])
    emit_consts(k)
    par = k.sb("dpar", [64, 4], F32)
    lp = k.sb("dlp", [32, 4], F32)
    p.dma("sp", par[:], par_d, writes=["dpar"])
    p.dma("sp", lp[:], lam_d, writes=["dlp"])
    sc = k.sb("dsc", [64, 16], F32)
    col = lambda i: sc[:, i:i + 1]
    R_ = ["dsc"]
    pr2 = k.sb("dpr2", [32, 2], F32)
    op_tt(k, "dve", pr2[:, 0:1], lp[:, 0:1], lp[:, 1:2], ALU.mult, ["dlp"], ["dpr2"])
    op_tt(k, "dve", pr2[:, 1:2], lp[:, 2:3], lp[:, 3:4], ALU.mult, ["dlp"], ["dpr2"])
    pm = k.ps("dpm", [128, 512], F32)
    op_mm(k, pm[0:64, 0:2], k.ones[0:32, 0:64], pr2[:], True, True, ["ones", "dpr2"], ["dpm"])
    op_act(k, sc[:, 3:5], pm[0:64, 0:2], AF.Exp, ["dpm"], R_)
    op_tt(k, "dve", col(0), col(4), col(3), ALU.subtract, R_, R_)
    op_ts(k, "dve", col(0), col(0), -float(lam_init), None, ALU.add, None, R_, R_)
    op_ts(k, "dve", col(1), par[:, 2:3], 1.0 - float(lam_init), None, ALU.mult, None, ["dpar"], R_)
    op_ts(k, "dve", col(2), par[:, 0:1], 32 ** -0.5, None, ALU.mult, None, ["dpar"], R_)
    pi_ = k.sb("dpi", [64, 4], I32)
    p.op("pool", lambda: nc.gpsimd.iota(pi_[:, 0:1], pattern=[[0, 1]], base=0, channel_multiplier=1), (), ["dpi"])
    op_ts(k, "dve", pi_[:, 1:2], pi_[:, 0:1], 7, None, ALU.bitwise_and, None, ["dpi"], ["dpi"])
    op_ts(k, "dve", pi_[:, 2:3], pi_[:, 0:1], 4, 1, ALU.logical_shift_right, ALU.bitwise_and, ["dpi"], ["dpi"])
    op_copy(k, "dve", sc[:, 5:7], pi_[:, 1:3], ["dpi"], R_)
    op_act(k, col(7), col(5), AF.Exp, R_, R_, scale=-LN1E4 / 8.0)
    op_tt(k, "dve", col(9), col(7), col(6), ALU.mult, R_, R_)
    op_tt(k, "dve", col(8), col(7), col(9), ALU.subtract, R_, R_)
    jfi = k.sb("djfi", [64, 256], I32)
    jf = k.sb("djf", [64, 256], F32)
    ang = k.sb("dang", [64, 256], F32)
    ttmp = k.sb("dttmp", [64, 256], F32)
    tti = k.sb("dtti", [64, 256], I32)
    p.op("pool", lambda: nc.gpsimd.iota(jfi[:], pattern=[[1, 256]], base=0, channel_multiplier=0), (), ["djfi"])
    op_copy(k, "dve", jf[:], jfi[:], ["djfi"], ["djf"])
    T = {n: k.sb("dT" + n, [64, 256], F32) for n in ("crk", "srk", "crq", "srq", "cc", "sc")}

    def table(cn, sn, n, scale_col, add_col):
        if add_col is None:
            op_ts(k, "dve", ang[:, :n], jf[:, :n], scale_col, None, ALU.mult, None, ["djf"] + R_, ["dang"])
        else:
            op_ts(k, "dve", ang[:, :n], jf[:, :n], add_col, scale_col, ALU.add, ALU.mult, ["djf", "dpar"] + R_, ["dang"])
        emit_sin(k, T[sn][:, :n], ang[:, :n], 0.0, ttmp[:, :n], tti[:, :n], ["dang"], ["dT" + sn], "dts")
        emit_sin(k, T[cn][:, :n], ang[:, :n], 0.5 * np.pi, ttmp[:, :n], tti[:, :n], ["dang"], ["dT" + cn], "dts")
    table("crk", "srk", 256, col(8), None)
    table("crq", "srq", 128, col(8), par[:, 3:4])
    table("cc", "sc", 64, col(9), None)
    op_ts(k, "dve", T["cc"][:, :64], T["cc"][:, :64], -1.0, None, ALU.add, None, ["dTcc"], ["dTcc"])
    ri = k.sb("dri", [64, 64], I32)
    rf = k.sb("drf", [64, 64], F32)
    mi = k.sb("dmi", [64, 64], I32)
    mf = k.sb("dmf", [64, 64], F32)
    e1 = k.sb("de1", [64, 64], F32)
    RmT = k.sb("dRmT", [64, 64], F32)
    p.op("pool", lambda: nc.gpsimd.iota(ri[:], pattern=[[1, 64]], base=0, channel_multiplier=-1), (), ["dri"])
    p.op("pool", lambda: nc.gpsimd.iota(mi[:], pattern=[[1, 64]], base=0, channel_multiplier=0), (), ["dmi"])
    op_copy(k, "dve", rf[:], ri[:], ["dri"], ["drf"])
    op_ts(k, "dve", mi[:], mi[:], 3, 1, ALU.logical_shift_right, ALU.bitwise_and, ["dmi"], ["dmi"])
    op_copy(k, "dve", mf[:], mi[:], ["dmi"], ["dmf"])
    op_ts(k, "dve", e1[:], rf[:], -8.0, None, ALU.is_equal, None, ["drf"], ["de1"])
    op_ts(k, "dve", RmT[:], rf[:], 8.0, None, ALU.is_equal, None, ["drf"], ["dRmT"])
    op_tt(k, "dve", RmT[:], RmT[:], mf[:], ALU.mult, ["dRmT", "dmf"], ["dRmT"])
    op_ts(k, "dve", mf[:], mf[:], -1.0, 1.0, ALU.mult, ALU.add, ["dmf"], ["dmf"])
    op_tt(k, "dve", e1[:], e1[:], mf[:], ALU.mult, ["de1", "dmf"], ["de1"])
    op_tt(k, "dve", RmT[:], RmT[:], e1[:], ALU.subtract, ["dRmT", "de1"], ["dRmT"])
    blk = k.sb("dblk", [64, 64], F32)
    op_memset(k, "dve", blk[:], 0.0, ["dblk"])
    op_memset(k, "dve", blk[0:32, 0:32], 1.0 / 32, ["dblk"])
    op_memset(k, "dve", blk[32:64, 32:64], 1.0 / 32, ["dblk"])
    o64 = k.sb("do64", [64, 64], F32)
    op_memset(k, "dve", o64[:], 1.0 / 64, ["do64"])
    sel = k.sb("dsel", [65, 64], F32)
    op_memset(k, "dve", sel[:], 0.0, ["dsel"])
    op_memset(k, "dve", sel[64:65, :], 1.0, ["dsel"])
    kh = k.sb("dkh", [64, NSEQ], BF16)
    qh = k.sb("dqh", [64, QH], BF16)
    qch = k.sb("dqch", [64, CTX], BF16)
    vaug = k.sb("dvaug", [128, NKT, 65], BF16)
    op_memset(k, "pool", vaug[:, :, 64:65], 1.0, [("dvaug", "one")])
    vst = [k.sb("dvst%d" % i, [128, 13, 64], F32) for i in range(2)]
    for i in range(10):
        b = i % 2
        p.dma(k.dq(), vst[b][:], vt_d[:, 13 * i:13 * i + 13, :], writes=[("dvst", b)])
        op_copy(k, "pool", vaug[:, 13 * i:13 * i + 13, 0:64], vst[b][:], [("dvst", b)], [("dvaug", i)])
    xs = [k.sb("dxs%d" % i, [64, 512], F32) for i in range(2)]
    W = {n: k.sb("dw" + n, [64, 512], F32) for n in ("sq", "rs", "xn", "cb", "sb", "t1")}
    bankA = k.ps("dbA", [128, 512], F32)
    bankB = k.ps("dbB", [128, 512], F32)
    pss = bankA[0:64, :]
    prx = bankB[0:64, :]
    it = 0

    def prep(src_d, c0, n, gcol, dst, dname, rope, crn, srn, a0):
        nonlocal it
        b = it % 2
        it += 1
        p.dma(k.dq(), xs[b][:, :n], src_d[:, c0:c0 + n], writes=[("dxs", b)])
        op_tt(k, "pool", W["sq"][:, :n], xs[b][:, :n], xs[b][:, :n], ALU.mult, [("dxs", b)], ["dwsq"])
        op_mm(k, pss[:, :n], blk[:], W["sq"][:, :n], True, True, ["dblk", "dwsq"], ["dbA"])
        op_ts(k, "dve", W["rs"][:, :n], pss[:, :n], EPS, None, ALU.add, None, ["dbA"], ["dwrs"])
        op_act(k, W["rs"][:, :n], W["rs"][:, :n], AF.Sqrt, ["dwrs"], ["dwrs"])
        k.p.op("dve", (lambda o=W["rs"][:, :n]: nc.vector.reciprocal(out=o, in_=o)), ["dwrs"], ["dwrs"])
        if not rope:
            op_stt(k, "dve", dst, xs[b][:, :n], gcol, W["rs"][:, :n], ALU.mult, ALU.mult, [("dxs", b), "dwrs", "dpar"] + R_, [dname])
            return
        op_stt(k, "dve", W["xn"][:, :n], xs[b][:, :n], gcol, W["rs"][:, :n], ALU.mult, ALU.mult, [("dxs", b), "dwrs", "dpar"] + R_, ["dwxn"])
        op_mm(k, prx[:, :n], RmT[:], W["xn"][:, :n], True, True, ["dRmT", "dwxn"], ["dbB"])
        na = n // 64
        v3 = lambda ap: ap.rearrange("p (a b) -> p a b", b=64)
        bc_a = lambda t2: bass.AP(t2.tensor, t2[:, a0:a0 + 1].offset, [list(t2.ap[0]), [1, na], [0, 64]])
        bc_b = lambda t2: bass.AP(t2.tensor, t2[:, 0:1].offset, [list(t2.ap[0]), [0, na], [1, 64]])
        op_tt(k, "pool", v3(W["cb"][:, :n]), bc_a(T[crn][:]), bc_b(T["cc"][:]), ALU.add, ["dT" + crn, "dTcc"], ["dwcb"])
        op_tt(k, "pool", v3(W["sb"][:, :n]), bc_a(T[srn][:]), bc_b(T["sc"][:]), ALU.add, ["dT" + srn, "dTsc"], ["dwsb"])
        op_tt(k, "dve", W["cb"][:, :n], W["cb"][:, :n], W["xn"][:, :n], ALU.mult, ["dwcb", "dwxn"], ["dwcb"])
        op_tt(k, "dve", W["sb"][:, :n], W["sb"][:, :n], prx[:, :n], ALU.mult, ["dwsb", "dbB"], ["dwsb"])
        op_tt(k, "dve", dst, W["cb"][:, :n], W["sb"][:, :n], ALU.add, ["dwcb", "dwsb"], [dname])

    prep(k_d, 0, CTX, par[:, 1:2], kh[:, 0:CTX], ("dkh", "c"), False, None, None, 0)
    for i in range(SEQ // 512):
        prep(k_d, CTX + 512 * i, 512, par[:, 1:2], kh[:, CTX + 512 * i:CTX + 512 * (i + 1)], ("dkh", i), True, "crk", "srk", 8 * i)
    prep(qc_d, 0, CTX, col(2), qch[:], "dqch", False, None, None, 0)
    for i in range(nqb):
        prep(q_d, 512 * i, 512, col(2), qh[:, 512 * i:512 * (i + 1)], ("dqh", i), True, "crq", "srq", 8 * i)
    pS = [[k.ps("dpS%d%d" % (c, i), [128, 512], F32) for i in range(2)] for c in range(2)]
    pO = [bankA[0:65, :], bankB[0:65, :]]
    pOn = ["dbA", "dbB"]
    Pb = [[k.sb("dP%d%d" % (c, i), [128, 512], BF16) for i in range(2)] for c in range(2)]
    osb = [k.sb("dosb%d" % c, [65, 512], F32) for c in range(2)]
    F = {n: k.sb("df" + n, [64, 512], F32) for n in ("rd", "o0", "o1", "o", "sq", "rs")}

    def attend(qsrc, qname, n, kts, out_d, oc0):
        nk = len(kts)
        for ki, kt in enumerate(kts):
            b = ki % 2
            kname = ("dkh", "c") if kt < 2 else ("dkh", (kt - 2) // 4)
            for c in range(2):
                rows = slice(32 * c, 32 * c + 32)
                op_mm(k, pS[c][b][:, :n], kh[rows, kt * 128:(kt + 1) * 128], qsrc[rows, :n], True, True, [kname, qname], [("dpS", c, b)])
                op_act(k, Pb[c][b][:, :n], pS[c][b][:, :n], AF.Exp, [("dpS", c, b)], [("dP", c, b)])
                op_mm(k, pO[c][:, :n], vaug[:, kt, :], Pb[c][b][:, :n], ki == 0, ki == nk - 1, ["dvaug", ("dP", c, b)], [pOn[c]])
        for c in range(2):
            op_copy(k, "act", osb[c][:, :n], pO[c][:, :n], [pOn[c]], [("dosb", c)])
            op_mm(k, pm[0:64, :n], sel[:], osb[c][:, :n], True, True, ["dsel", ("dosb", c)], ["dpm"])
            k.p.op("dve", (lambda o=F["rd"][:, :n], i_=pm[0:64, :n]: nc.vector.reciprocal(out=o, in_=i_)), ["dpm"], ["dfrd"])
            op_tt(k, "dve", F["o%d" % c][:, :n], osb[c][0:64, :n], F["rd"][:, :n], ALU.mult, [("dosb", c), "dfrd"], ["dfo%d" % c])
        op_stt(k, "dve", F["o"][:, :n], F["o1"][:, :n], col(0), F["o0"][:, :n], ALU.mult, ALU.add, ["dfo0", "dfo1"] + R_, ["dfo"])
        op_tt(k, "pool", F["sq"][:, :n], F["o"][:, :n], F["o"][:, :n], ALU.mult, ["dfo"], ["dfsq"])
        op_mm(k, pm[0:64, :n], o64[:], F["sq"][:, :n], True, True, ["do64", "dfsq"], ["dpm"])
        op_ts(k, "dve", F["rs"][:, :n], pm[0:64, :n], EPS, None, ALU.add, None, ["dpm"], ["dfrs"])
        op_act(k, F["rs"][:, :n], F["rs"][:, :n], AF.Sqrt, ["dfrs"], ["dfrs"])
        k.p.op("dve", (lambda o=F["rs"][:, :n]: nc.vector.reciprocal(out=o, in_=o)), ["dfrs"], ["dfrs"])
        op_stt(k, "dve", F["o"][:, :n], F["o"][:, :n], col(1), F["rs"][:, :n], ALU.mult, ALU.mult, ["dfo", "dfrs"] + R_, ["dfo"])
        p.dma("sp", out_d[:, oc0:oc0 + n], F["o"][:, :n], reads=["dfo"], writes=[("dout", id(out_d), oc0)])

    attend(qch, "dqch", CTX, [0, 1], yc_d, 0)
    for i in range(nqb):
        attend(qh[:, 512 * i:512 * (i + 1)], ("dqh", i), 512, list(range(NKT)), y_d, 512 * i)
    return k.finish(["dout"])


def da_inputs(core, zq, zk, zv, zqc, q_norm, k_norm, out_norm, da_lam):
    h, qh = core // 2, core % 2
    rows = slice(64 * h, 64 * h + 64)
    par = np.zeros((64, 4), np.float32)
    par[:, 0] = np.tile(q_norm, 2)
    par[:, 1] = np.tile(k_norm, 2)
    par[:, 2] = out_norm
    par[:, 3] = 128.0 * qh
    vt = np.ascontiguousarray(zv[rows].T.reshape(NKT, 128, 64).transpose(1, 0, 2))
    return dict(dq=np.ascontiguousarray(zq[rows, QH * qh:QH * (qh + 1)]), dk=np.ascontiguousarray(zk[rows]),
                dvt=vt, dqc=np.ascontiguousarray(zqc[rows]), dpar=par, dlam=np.ascontiguousarray(np.asarray(da_lam, np.float32).T))


NL = 1024
C_HL, C_HR, C_CTX0 = NL, NL + 1, NL + 2
NFF = D_FF // 128


class Banks:
    def __init__(self, k, n):
        self.b = [(k.ps("bank%d" % i, [128, 512], F32), "bank%d" % i) for i in range(n)]
        self.i = 0

    def next(self):
        self.i += 1
        return self.b[self.i % len(self.b)]


class WStream:
    def __init__(self, k, tag, nk, mw, nbuf=2):
        self.k, self.tag, self.nbuf = k, tag, nbuf
        self.st = [k.sb(tag + "s%d" % i, [128, nk, mw], F32) for i in range(nbuf)]
        self.bf = [k.sb(tag + "b%d" % i, [128, nk, mw], BF16) for i in range(nbuf)]
        self.i = 0

    def load(self, view, nk=None, mw=None):
        k = self.k
        b = self.i % self.nbuf
        self.i += 1
        st, bf = self.st[b], self.bf[b]
        nk = nk or st.shape[1]
        mw = mw or st.shape[2]
        k.p.dma(k.dq(), st[:, :nk, :mw], view, writes=[(self.tag + "s", b)])
        op_copy(k, "pool", bf[:, :nk, :mw], st[:, :nk, :mw], [(self.tag + "s", b)], [(self.tag + "b", b)])
        return bf, (self.tag + "b", b)


def build_phaseC(with_ctx):
    k = K()
    nc, p = k.nc, k.p
    NTS = NL + 2
    nsec = 3 if with_ctx else 2
    x_d = k.din("xT", [nsec, D, NTS])
    cvec_d = k.din("cvec", [128, 8, 2])
    wada_d = k.din("wada", [D, 6 * D])
    bada_d = k.din("bada", [128, 48])
    g1_d = k.din("g1", [128, 8])
    g2_d = k.din("g2", [128, 8])
    wg_d = k.din("wg", [D, 4 * D])
    u_d = k.din("bu", [nsec, 256, NTS])
    ys_d = k.din("bys", [nsec, 256, NTS])
    yb_d = k.din("byb", [nsec, 256, NTS])
    yc_d = k.din("byc", [nsec, 256, NTS])
    go_d = k.din("bgo", [nsec, 256, NTS])
    gg_d = k.din("bgg", [nsec, 256, NTS])
    sp_d = k.din("spar", [128, 8])
    wglu_d = k.din("wglu", [256, 256])
    wbr_d = k.din("wbr", [4, 256, D])
    wo_d = k.din("wo", [D, D])
    wup_d = k.din("wup", [D, 2 * D_FF])
    fcw_d = k.din("fcw", [128, NFF, 4])
    wdn_d = k.din("wdn", [D_FF, D])
    xo_d = k.dout("xo", [2, D, NL])
    co_d = k.dout("co", [D, CTX]) if with_ctx else None

    emit_consts(k)
    k.modname = "Cmod"
    mod = emit_adaln(k, wada_d, bada_d, cvec_d, [0, 1, 2, 3, 4, 5], "C")
    banks = Banks(k, 7)
    spar = k.sb("spar", [128, 8], F32)
    fcw = k.sb("fcw", [128, NFF, 4], F32)
    p.dma("sp", spar[:], sp_d, writes=["spar"])
    p.dma("sp", fcw[:], fcw_d, writes=["fcw"])
    blk64 = k.sb("blk64", [128, 128], F32)
    op_memset(k, "dve", blk64[:], 0.0, ["blk64"])
    op_memset(k, "dve", blk64[0:64, 0:64], 1.0 / 64, ["blk64"])
    op_memset(k, "dve", blk64[64:128, 64:128], 1.0 / 64, ["blk64"])
    wglu = WStream(k, "wglu", 2, 256, nbuf=1)
    wglu_bf, wglu_n = wglu.load(wglu_d.rearrange("(k p) m -> p k m", p=128))

    xT = k.sb("xT_sb", [128, 8, NTS], F32)
    hnT = k.sb("hnT", [128, 8, NTS], BF16)
    yT = k.sb("yT", [128, 8, NTS], BF16)
    mg = k.sb("mg", [128, 8, NTS], BF16)
    ws_g = WStream(k, "wsg", 8, 128, nbuf=4)
    ws_b = WStream(k, "wsb", 2, 128, nbuf=3)
    ws_o = ws_g
    ws_u = ws_g
    ws_d = WStream(k, "wsd", 1, 1024, nbuf=2)
    stg = [k.sb("stg%d" % i, [128, NTS], F32) for i in range(3)]
    stg_i = [0]

    def stage():
        stg_i[0] += 1
        b = stg_i[0] % 3
        return stg[b], ("stg", b)

    zf = k.sb("zf", [128, 2, NTS], F32)
    zb = k.sb("zb", [128, 2, NTS], BF16)
    acc = k.sb("acc", [128, NTS], F32)
    sig = [k.sb("sig%d" % i, [128, 512], F32) for i in range(2)]
    tmpm = [k.sb("tmpm%d" % i, [128, 512], F32) for i in range(2)]
    gts = k.sb("gts", [128, NL + 2], F32)
    cvt = acc
    asb = stg[0]
    hT = [k.sb("hT%d" % i, [128, NL], BF16) for i in range(2)]

    for sec in range(nsec):
        sfx = "s%d" % sec
        isctx = sec == 2
        if isctx:
            blocks = [(0, CTX, 1)]
            oblocks = [(0, CTX, 1, 0)]
            op_memset(k, "dve", gts[:, 0:1], 0.0, ["gts"])
            op_memset(k, "dve", gts[:, CTX + 1:CTX + 2], 0.0, ["gts"])
        else:
            blocks = [(0, 512, 0), (512, 1024, 0), (NL, NL + 2, 0)]
            oblocks = [(0, 512, 0, 0), (512, 1024, 0, 512)]
        xv = x_d[sec].rearrange("(k p) t -> p k t", p=128)
        for kk in range(8):
            p.dma(k.dq(), xT[:, kk, :], xv[:, kk, :], writes=[("xT", kk)])
        emit_norm_mod(k, xT, NTS, blocks, g1_d, mod, 0, 1, hnT, "n1" + sfx, "xT", banks.b)
        hn_name = "n1" + sfx + "hnT"
        for j in range(2):
            su, sun = stage()
            sy, syn = stage()
            p.dma(k.dq(), su[:], u_d[sec, 128 * j:128 * (j + 1), :], writes=[sun])
            p.dma(k.dq(), sy[:], ys_d[sec, 128 * j:128 * (j + 1), :], writes=[syn])
            op_stt(k, "dve", sy[:], su[:], spar[:, j:j + 1], sy[:], ALU.mult, ALU.add, [sun, syn, "spar"], [syn])
            op_act(k, zf[:, j, :], sy[:], AF.Gelu_apprx_tanh, [syn], [("zf", j)])
            op_copy(k, "pool", zb[:, j, :], zf[:, j, :], [("zf", j)], [("zb", j)])
        for j in range(2):
            for (c0, c1, v) in blocks:
                w = c1 - c0
                pt, pn = banks.next()
                for kk in range(2):
                    op_mm(k, pt[:, :w], wglu_bf[:, kk, 128 * j:128 * (j + 1)], zb[:, kk, c0:c1], kk == 0, kk == 1, [wglu_n, ("zb", kk)], [pn])
                b = c0 // 512 % 2
                op_act(k, sig[b][:, :w], pt[:, :w], AF.Sigmoid, [pn], [("sig", b)])
                op_tt(k, "dve", yT[:, j, c0:c1], sig[b][:, :w], zf[:, j, c0:c1], ALU.mult, [("sig", b), ("zf", j)], [("yT", j, c0)])
        for bi_, src in ((1, yb_d), (2, yc_d)):
            for j in range(2):
                s_, sn_ = stage()
                p.dma(k.dq(), s_[:], src[sec, 128 * j:128 * (j + 1), :], writes=[sn_])
                op_copy(k, "pool", yT[:, 2 * bi_ + j, :], s_[:], [sn_], [("yT", 2 * bi_ + j)])
        for j in range(2):
            so, son = stage()
            sg, sgn = stage()
            p.dma(k.dq(), so[:], go_d[sec, 128 * j:128 * (j + 1), :], writes=[son])
            p.dma(k.dq(), sg[:], gg_d[sec, 128 * j:128 * (j + 1), :], writes=[sgn])
            op_act(k, sg[:], sg[:], AF.Silu, [sgn], [sgn])
            for (c0, c1, v) in blocks:
                w = c1 - c0
                b = c0 // 512 % 2
                op_tt(k, "pool", tmpm[b][:, :w], so[:, c0:c1], so[:, c0:c1], ALU.mult, [son], [("tmpm", b)])
                pt, pn = banks.next()
                op_mm(k, pt[:, :w], blk64[:], tmpm[b][:, :w], True, True, ["blk64", ("tmpm", b)], [pn])
                op_ts(k, "dve", sig[b][:, :w], pt[:, :w], EPS, None, ALU.add, None, [pn], [("sig", b)])
                op_act(k, sig[b][:, :w], sig[b][:, :w], AF.Sqrt, [("sig", b)], [("sig", b)])
                k.p.op("dve", (lambda o=sig[b][:, :w]: nc.vector.reciprocal(out=o, in_=o)), [("sig", b)], [("sig", b)])
                op_stt(k, "dve", tmpm[b][:, :w], so[:, c0:c1], spar[:, 2 + j:3 + j], sig[b][:, :w], ALU.mult, ALU.mult, [son, "spar", ("sig", b), ("tmpm", b)], [("tmpm", b)])
                op_tt(k, "dve", yT[:, 6 + j, c0:c1], tmpm[b][:, :w], sg[:, c0:c1], ALU.mult, [("tmpm", b), sgn], [("yT", 6 + j, c0)])
        wgv = wg_d.rearrange("(k p) m -> p k m", p=128)
        for m in range(8):
            for n in range(4):
                wgb, wgn = ws_g.load(wgv[:, :, n * D + m * 128:n * D + (m + 1) * 128])
                wbb, wbn = ws_b.load(wbr_d[n].rearrange("(k p) m -> p k m", p=128)[:, :, m * 128:(m + 1) * 128])
                for (c0, c1, v) in blocks:
                    w = c1 - c0
                    b = (c0 // 512 + n) % 2
                    ptg, png = banks.next()
                    for kk in range(8):
                        op_mm(k, ptg[:, :w], wgb[:, kk, :], hnT[:, kk, c0:c1], kk == 0, kk == 7, [wgn, (hn_name, kk)], [png])
                    ptp, pnp = banks.next()
                    for kk in range(2):
                        op_mm(k, ptp[:, :w], wbb[:, kk, :], yT[:, 2 * n + kk, c0:c1], kk == 0, kk == 1, [wbn, ("yT", 2 * n + kk)], [pnp])
                    op_act(k, sig[b][:, :w], ptg[:, :w], AF.Sigmoid, [png], [("sig", b)])
                    if n == 0:
                        op_tt(k, "dve", acc[:, c0:c1], sig[b][:, :w], ptp[:, :w], ALU.mult, [("sig", b), pnp], [("acc", c0)])
                    else:
                        op_tt(k, "dve", tmpm[b][:, :w], sig[b][:, :w], ptp[:, :w], ALU.mult, [("sig", b), pnp], [("tmpm", b)])
                        dst = mg[:, m, c0:c1] if n == 3 else acc[:, c0:c1]
                        dn = ("mg", m, c0) if n == 3 else ("acc", c0)
                        op_tt(k, "pool", dst, acc[:, c0:c1], tmpm[b][:, :w], ALU.add, [("acc", c0), ("tmpm", b)], [dn])
        wov = wo_d.rearrange("(k p) m -> p k m", p=128)
        for m in range(8):
            wob, won = ws_o.load(wov[:, :, m * 128:(m + 1) * 128])
            for (c0, c1, v) in blocks:
                w = c1 - c0
                pt, pn = banks.next()
                for kk in range(8):
                    op_mm(k, pt[:, :w], wob[:, kk, :], mg[:, kk, c0:c1], kk == 0, kk == 7, [won, ("mg", kk)], [pn])
                op_stt(k, "dve", xT[:, m, c0:c1], pt[:, :w], mod[:, 16 + m, v:v + 1], xT[:, m, c0:c1], ALU.mult, ALU.add,
                       [pn, "Cmod", ("xT", m)], [("xT", m)])
        emit_norm_mod(k, xT, NTS, blocks, g2_d, mod, 3, 4, hnT, "n2" + sfx, "xT", banks.b)
        hn2 = "n2" + sfx + "hnT"
        wuv = wup_d.rearrange("(k p) m -> p k m", p=128)
        for f in range(NFF):
            wab, wan = ws_u.load(wuv[:, :, f * 128:(f + 1) * 128])
            wtb, wtn = ws_u.load(wuv[:, :, D_FF + f * 128:D_FF + (f + 1) * 128])
            wdb, wdn_ = ws_d.load(wdn_d[f * 128:(f + 1) * 128, :].rearrange("p (o m) -> p o m", o=1))
            hb = f % 2
            for (c0, c1, v) in blocks:
                w = c1 - c0
                pta, pna = banks.next()
                for kk in range(8):
                    op_mm(k, pta[:, :w], wab[:, kk, :], hnT[:, kk, c0:c1], kk == 0, kk == 7, [wan, (hn2, kk)], [pna])
                op_copy(k, "act", asb[:, c0:c1], pta[:, :w], [pna], [("stg", 0, c0)])
                ptt, pnt = banks.next()
                for kk in range(8):
                    op_mm(k, ptt[:, :w], wtb[:, kk, :], hnT[:, kk, c0:c1], kk == 0, kk == 7, [wtn, (hn2, kk)], [pnt])
                if c0 < NL:
                    op_copy(k, "act", gts[:, 1 + c0:1 + c1], ptt[:, :w], [pnt], [("gts", "l", c0)])
                else:
                    op_tt(k, "dve", gts[:, 0:1], ptt[:, 0:1], spar[:, 4 + 2 * sec:5 + 2 * sec], ALU.mult, [pnt, "spar"], [("gts", "hl")])
                    op_tt(k, "dve", gts[:, NL + 1:NL + 2], ptt[:, 1:2], spar[:, 5 + 2 * sec:6 + 2 * sec], ALU.mult, [pnt, "spar"], [("gts", "hr")])
            segs = [(0, CTX if isctx else NL, 0, 0)]
            for (g0, n, oc, ac) in segs:
                op_ts(k, "dve", cvt[:, oc:oc + n], gts[:, g0:g0 + n], fcw[:, f, 0:1], fcw[:, f, 3:4], ALU.mult, ALU.add, ["gts", "fcw"], [("acc", "cv", oc)])
                op_stt(k, "dve", cvt[:, oc:oc + n], gts[:, g0 + 1:g0 + 1 + n], fcw[:, f, 1:2], cvt[:, oc:oc + n], ALU.mult, ALU.add, ["gts", "fcw", ("acc", "cv", oc)], [("acc", "cv", oc)])
                op_stt(k, "dve", cvt[:, oc:oc + n], gts[:, g0 + 2:g0 + 2 + n], fcw[:, f, 2:3], cvt[:, oc:oc + n], ALU.mult, ALU.add, ["gts", "fcw", ("acc", "cv", oc)], [("acc", "cv", oc)])
                op_act(k, cvt[:, oc:oc + n], cvt[:, oc:oc + n], AF.Gelu_apprx_tanh, [("acc", "cv", oc)], [("acc", "cv", oc)])
                op_tt(k, "dve", hT[hb][:, oc:oc + n], cvt[:, oc:oc + n], asb[:, ac:ac + n], ALU.mult, [("acc", "cv", oc), ("stg", 0)], [("hT", hb, oc)])
            for m in range(8):
                for (c0, c1, v, xc0) in oblocks:
                    w = c1 - c0
                    pt, pn = banks.next()
                    op_mm(k, pt[:, :w], wdb[:, 0, m * 128:(m + 1) * 128], hT[hb][:, c0:c1], True, True, [wdn_, ("hT", hb)], [pn])
                    eng = "dve" if (m + c0 // 512) % 2 == 0 else "pool"
                    if eng == "pool":
                        op_act(k, tmpm[m % 2][:, :w], pt[:, :w], AF.Identity, [pn, "Cmod"], [("tmpm", m % 2)], scale=mod[:, 40 + m, v:v + 1])
                        op_tt(k, "pool", xT[:, m, xc0:xc0 + w], xT[:, m, xc0:xc0 + w], tmpm[m % 2][:, :w], ALU.add,
                              [("tmpm", m % 2), ("xT", m)], [("xT", m)])
                    else:
                        op_stt(k, "dve", xT[:, m, xc0:xc0 + w], pt[:, :w], mod[:, 40 + m, v:v + 1], xT[:, m, xc0:xc0 + w], ALU.mult, ALU.add,
                               [pn, "Cmod", ("xT", m)], [("xT", m)])
        if not isctx:
            xov = xo_d[sec].rearrange("(k p) t -> p k t", p=128)
            for kk in range(8):
                p.dma(k.dq(), xov[:, kk, :], xT[:, kk, 0:NL], reads=[("xT", kk)], writes=[("xout", sec, kk)])
        else:
            cov = co_d.rearrange("(k p) t -> p k t", p=128)
            for kk in range(8):
                p.dma(k.dq(), cov[:, kk, :], xT[:, kk, 0:CTX], reads=[("xT", kk)], writes=[("cout", kk)])
    return k.finish(["xout", "cout"] if with_ctx else ["xout"])


def _sec_cols(arrT, core, sec, off):
    s0 = TL * core + NL * sec
    out = np.zeros((arrT.shape[0], NL + 2), np.float32)
    out[:, :NL] = arrT[:, off + s0:off + s0 + NL]
    if s0 > 0:
        out[:, NL] = arrT[:, off + s0 - 1]
    if s0 + NL < SEQ:
        out[:, NL + 1] = arrT[:, off + s0 + NL]
    return out


def _ctx_cols(arrT):
    out = np.zeros((arrT.shape[0], NL + 2), np.float32)
    out[:, :CTX] = arrT[:, :CTX]
    return out


def phaseC_inputs(core, with_ctx, xT_all, cT_all, br, c, c_ctx, w_ada_l, b_ada_l, g1_l, g2_l, w_in_l, ssm_d, w_glu, gla_g,
                  w_branch, w_out, w_up, fcw, fcb, w_down):
    nsec = 3 if with_ctx else 2
    def sec_stack(aT, off, ctxT):
        parts = [_sec_cols(aT, core, s, off) for s in range(2)]
        if with_ctx:
            parts.append(_ctx_cols(ctxT))
        return np.ascontiguousarray(np.stack(parts))
    d = dict(xT=sec_stack(xT_all, 0, cT_all))
    for nm, key in (("bu", "u"), ("bys", "ys"), ("byb", "yb"), ("byc", "yc"), ("bgo", "go"), ("bgg", "gg")):
        d[nm] = sec_stack(br[key], CTX, br[key])
    spar = np.zeros((128, 8), np.float32)
    spar[:, 0:2] = np.asarray(ssm_d, np.float32).reshape(2, 128).T
    spar[:, 2:4] = np.tile(np.asarray(gla_g, np.float32), 2)[:, None]
    for s in range(2):
        s0 = TL * core + NL * s
        spar[:, 4 + 2 * s] = 1.0 if s0 > 0 else 0.0
        spar[:, 5 + 2 * s] = 1.0 if s0 + NL < SEQ else 0.0
    fc = np.zeros((128, NFF, 4), np.float32)
    fc[:, :, 0:3] = np.asarray(fcw, np.float32).T.reshape(NFF, 128, 3).transpose(1, 0, 2)
    fc[:, :, 3] = np.asarray(fcb, np.float32).reshape(NFF, 128).T
    cvec = np.ascontiguousarray(np.stack([_ft(np.asarray(c).reshape(-1)), _ft(np.asarray(c_ctx).reshape(-1))], axis=-1))
    d.update(cvec=cvec, wada=np.ascontiguousarray(w_ada_l), bada=_ft(b_ada_l), g1=_ft(g1_l), g2=_ft(g2_l),
             wg=np.ascontiguousarray(w_in_l[:, IN_MIX:]), spar=spar, wglu=np.ascontiguousarray(w_glu),
             wbr=np.ascontiguousarray(w_branch), wo=np.ascontiguousarray(w_out), wup=np.ascontiguousarray(w_up),
             fcw=fc, wdn=np.ascontiguousarray(w_down))
    return d


_PIECES = (('ssm_u', 256), ('lru_x', 256), ('lru_y', 256), ('da_q', 256), ('da_k', 256), ('da_v', 256),
           ('gla_q', 128), ('gla_k', 128), ('gla_v', 256), ('gla_g', 256), ('gla_a', 32))


def _spmd(nc, in_maps):
    res = run_bass_kernel_spmd(nc, in_maps, core_ids=list(range(NCORE)))
    return res.results


def kernel(x, c, ctx, c_ctx, w_ada, b_ada, norm1_g, norm2_g, w_in,
           ssm_lam_re, ssm_lam_im, ssm_log_step, ssm_b_re, ssm_b_im, ssm_c_re, ssm_c_im, ssm_d, ssm_w_glu,
           lru_conv_w, lru_conv_b, lru_wr, lru_br, lru_wi, lru_bi, lru_lam,
           da_q_norm, da_k_norm, da_lam, da_out_norm,
           gla_wa2, gla_ba, gla_out_norm,
           w_branch, w_out, w_up, ffn_conv_w, ffn_conv_b, w_down):
    A = lambda a: np.asarray(a, dtype=np.float32)
    xT = np.ascontiguousarray(A(x)[0].T)
    cT = np.ascontiguousarray(A(ctx)[0].T)
    depth = A(w_in).shape[0]
    for l in range(depth):
        with_ctx = l < depth - 1
        zs = run_phaseA(xT, cT, A(c), A(c_ctx), A(w_ada)[l], A(b_ada)[l], A(norm1_g)[l], A(w_in)[l])
        z_all = np.concatenate([zs[0][:, TL:]] + [zs[i][:, :TL] for i in range(NCORE)], axis=1)
        z = {}
        o = 0
        for nm, sz in _PIECES:
            z[nm] = z_all[o:o + sz]
            o += sz
        r = _spmd(build_s5(), [s5_inputs(i, z['ssm_u'], A(ssm_lam_re)[l], A(ssm_lam_im)[l], A(ssm_log_step)[l],
                                         A(ssm_b_re)[l], A(ssm_b_im)[l], A(ssm_c_re)[l], A(ssm_c_im)[l]) for i in range(NCORE)])
        ys = np.concatenate([q["ys"] for q in r], axis=0)
        r = _spmd(build_lru(), [lru_inputs(i, z['lru_x'], z['lru_y'], A(lru_conv_w)[l], A(lru_conv_b)[l], A(lru_wr)[l],
                                           A(lru_br)[l], A(lru_wi)[l], A(lru_bi)[l], A(lru_lam)[l]) for i in range(NCORE)])
        yb = np.concatenate([q["lo"] for q in r], axis=0)
        r = _spmd(build_gla(), [gla_inputs(i, z['gla_q'], z['gla_k'], z['gla_v'], z['gla_a'], A(gla_wa2)[l], A(gla_ba)[l])
                                for i in range(NCORE)])
        go = np.concatenate([q["go"] for q in r], axis=0)
        lam_init = 0.8 - 0.6 * float(np.exp(-0.3 * l))
        zq_lat = np.ascontiguousarray(z['da_q'][:, CTX:])
        zq_ctx = np.ascontiguousarray(z['da_q'][:, :CTX])
        r = _spmd(build_da(lam_init), [da_inputs(i, zq_lat, z['da_k'], z['da_v'], zq_ctx, A(da_q_norm)[l], A(da_k_norm)[l],
                                                 A(da_out_norm)[l], A(da_lam)[l]) for i in range(NCORE)])
        yc = np.zeros((256, NSEQ), np.float32)
        for i in range(NCORE):
            h, qh = i // 2, i % 2
            yc[64 * h:64 * h + 64, CTX + QH * qh:CTX + QH * (qh + 1)] = r[i]["dy"]
            if qh == 0:
                yc[64 * h:64 * h + 64, :CTX] = r[i]["dyc"]
        br = dict(u=z['ssm_u'], ys=ys, yb=yb, yc=yc, go=go, gg=z['gla_g'])
        r = _spmd(build_phaseC(with_ctx), [phaseC_inputs(i, with_ctx, xT, cT, br, A(c), A(c_ctx), A(w_ada)[l], A(b_ada)[l],
                                                         A(norm1_g)[l], A(norm2_g)[l], A(w_in)[l], A(ssm_d)[l], A(ssm_w_glu)[l],
                                                         A(gla_out_norm)[l], A(w_branch)[l], A(w_out)[l], A(w_up)[l],
                                                         A(ffn_conv_w)[l], A(ffn_conv_b)[l], A(w_down)[l]) for i in range(NCORE)])
        xT = np.concatenate([np.concatenate([q["xo"][0], q["xo"][1]], axis=1) for q in r], axis=1)
        if with_ctx:
            cT = np.ascontiguousarray(r[0]["co"])
    return np.ascontiguousarray(xT.T)[None].astype(np.float32)
```

```python
import numpy as np
from contextlib import ExitStack
import concourse.bass as bass
import concourse.mybir as mybir
from concourse.bass_utils import run_bass_kernel_spmd

F32 = mybir.dt.float32
BF16 = mybir.dt.bfloat16
I32 = mybir.dt.int32
ALU = mybir.AluOpType
AF = mybir.ActivationFunctionType

D = 1024
SEQ = 16384
NCORE = 8
TL = SEQ // NCORE
CTX = 256
EPS = 1e-6
IN_MIX = 2336
D_FF = 2816


class Prog:
    SEM_MAX = 30000
    DMA_POOL = 16

    def __init__(self, nc):
        self.nc = nc
        self.eng = {"pe": nc.tensor, "act": nc.scalar, "dve": nc.vector,
                    "pool": nc.gpsimd, "sp": nc.sync}
        self.ops = []

    def op(self, eng, fn, reads=(), writes=(), dma=False):
        norm = lambda bs: tuple(b if isinstance(b, tuple) else (b,) for b in bs)
        self.ops.append(dict(eng=eng, fn=fn, reads=norm(reads), writes=norm(writes), dma=dma))

    def dma(self, eng, out, in_, reads=(), writes=(), **kw):
        e = self.eng[eng]
        self.op(eng, lambda: e.dma_start(out=out, in_=in_, **kw), reads, writes, dma=True)

    def emit(self, stack):
        nc = self.nc
        ops = self.ops
        n = len(ops)
        last_w, readers, desc = {}, {}, {}

        def related(p):
            out = [p[:i] for i in range(1, len(p) + 1)]
            out.extend(desc.get(p, ()))
            return out

        def register(p):
            for i in range(1, len(p)):
                desc.setdefault(p[:i], set()).add(p)

        deps = [set() for _ in range(n)]
        for i, o in enumerate(ops):
            for b in o["reads"]:
                for q in related(b):
                    if q in last_w:
                        deps[i].add(last_w[q])
            for b in o["writes"]:
                for q in related(b):
                    if q in last_w:
                        deps[i].add(last_w[q])
                    for r in readers.get(q, ()):
                        if r != i:
                            deps[i].add(r)
            for b in o["reads"]:
                register(b)
                readers.setdefault(b, []).append(i)
            for b in o["writes"]:
                register(b)
                for q in list(desc.get(b, ())):
                    last_w.pop(q, None)
                    readers.pop(q, None)
                last_w[b] = i
                readers[b] = []
            deps[i].discard(i)

        def stream(o):
            return ("d:" if o["dma"] else "c:") + o["eng"]

        needed = [False] * n
        for i, o in enumerate(ops):
            si = stream(o)
            keep = {}
            for d in deps[i]:
                sd = stream(ops[d])
                if sd == si and sd == "c:pe":
                    continue
                if sd.startswith("d:"):
                    keep[(sd, d)] = d
                elif sd not in keep or d > keep[sd]:
                    keep[sd] = d
            deps[i] = keep
            for d in keep.values():
                needed[d] = True
        P = self.DMA_POOL
        cnt = {}
        semval = [None] * n
        dma_prev = [None] * n
        for i, o in enumerate(ops):
            if o["fn"] is None:
                continue
            s = stream(o)
            if o["dma"]:
                c = cnt.get(s, 0)
                cnt[s] = c + 1
                slot, m = c % P, c // P + 1
                semval[i] = ((s, slot), 16 * m)
                if m > 1:
                    dma_prev[i] = ((s, slot), 16 * (m - 1))
                continue
            if not needed[i]:
                continue
            c = cnt.get(s, 0) + 1
            cnt[s] = c
            semval[i] = ((s, "e%d" % ((c - 1) // self.SEM_MAX)), (c - 1) % self.SEM_MAX + 1)
        sems = {}

        def get_sem(k):
            if k not in sems:
                sems[k] = stack.enter_context(nc.semaphore(("s_%s_%s" % k).replace(":", "_")))
            return sems[k]

        waited = {}

        def do_wait(engname, k, v):
            kk = (engname, k)
            if waited.get(kk, -1) >= v:
                return
            waited[kk] = v
            self.eng[engname].wait_ge(get_sem(k), v)

        for i, o in enumerate(ops):
            for sd, d in sorted(deps[i].items(), key=lambda t: t[1]):
                k, v = semval[d]
                do_wait(o["eng"], k, v)
            if dma_prev[i] is not None:
                do_wait(o["eng"], *dma_prev[i])
            if o["fn"] is None:
                continue
            ins = o["fn"]()
            if semval[i] is not None:
                k, v = semval[i]
                ins.then_inc(get_sem(k), 16 if o["dma"] else 1)
        self.ops = []
        return cnt


class K:
    def __init__(self):
        self.nc = bass.Bass("TRN2", target_bir_lowering=False)
        self.p = Prog(self.nc)
        self.st = ExitStack()
        self._dq = 0
        self.psn = 0

    def din(self, name, shape, dt=F32):
        return self.nc.dram_tensor(name, list(shape), dt, kind="ExternalInput").ap()

    def dout(self, name, shape, dt=F32):
        return self.nc.dram_tensor(name, list(shape), dt, kind="ExternalOutput").ap()

    def sb(self, name, shape, dt=F32):
        return self.st.enter_context(self.nc.sbuf_tensor("sb_" + name, list(shape), dt))

    def ps(self, name, shape, dt=F32):
        return self.st.enter_context(self.nc.psum_tensor("ps_" + name, list(shape), dt))

    def dq(self):
        return "sp"

    def finish(self, out_bufs):
        self.p.op("sp", None, reads=out_bufs)
        self.p.emit(self.st)
        self.st.close()
        return self.nc


def rev_ap(ap2d):
    n = ap2d.shape[-1]
    last = ap2d[:, n - 1:n]
    return bass.AP(ap2d.tensor, last.offset, [list(ap2d.ap[0]), [-ap2d.ap[-1][0], n]])


def emit_consts(k):
    nc, p = k.nc, k.p
    ones = k.sb("ones", [128, 128], F32)
    p.op("dve", lambda: nc.vector.memset(ones[:], 1.0), writes=["ones"])
    k.ones = ones


def emit_adaln(k, wada, bada_t, cvec, vec_ids, tag):
    nc, p = k.nc, k.p
    nv = len(vec_ids)
    scv = k.sb(tag + "scv", [128, 8, 2], F32)
    bsb = k.sb(tag + "bsb", [128, 48], F32)
    mod = k.sb(tag + "mod", [128, nv * 8, 2], F32)
    psa = k.ps(tag + "psa", [128, nv * 8, 2], F32)
    p.dma("sp", scv[:], cvec, writes=[tag + "scv"])
    p.dma("sp", bsb[:], bada_t, writes=[tag + "bsb"])
    p.op("act", lambda: nc.scalar.activation(out=scv[:], in_=scv[:], func=AF.Silu),
         reads=[tag + "scv"], writes=[tag + "scv"])
    wv = wada.rearrange("(k p) m -> p k m", p=128)
    wst = [k.sb(tag + "wst%d" % i, [128, 8, 128], F32) for i in range(2)]
    it = 0
    for vi, v in enumerate(vec_ids):
        for jj in range(8):
            b = it % 2
            it += 1
            c0 = v * D + jj * 128
            j = vi * 8 + jj
            p.dma(k.dq(), wst[b][:], wv[:, :, c0:c0 + 128], writes=[(tag + "wst", b)])
            for kk in range(8):
                p.op("pe", (lambda b=b, kk=kk, j=j: nc.tensor.matmul(
                    psa[:, j, :], lhsT=wst[b][:, kk, :], rhs=scv[:, kk, :],
                    start=(kk == 0), stop=(kk == 7))),
                    reads=[(tag + "wst", b), tag + "scv"], writes=[tag + "psa"])
    v0 = vec_ids[0]
    for c in range(2):
        p.op("dve", (lambda c=c: nc.vector.tensor_tensor(
            out=mod[:, :, c], in0=psa[:, :, c], in1=bsb[:, v0 * 8:(v0 + nv) * 8], op=ALU.add)),
            reads=[tag + "psa", tag + "bsb"], writes=[(tag + "mod", "c%d" % c)])
    return mod


def emit_norm_mod(k, xT, ntok, blocks, gT, mod, sh_i, sc_i, hnT, tag, xname, psb):
    nc, p = k.nc, k.p
    gsb = k.sb(tag + "g", [128, 8], F32)
    A = k.sb(tag + "A", [128, 8, 2], F32)
    p.dma("sp", gsb[:], gT, writes=[tag + "g"])
    for v in range(2):
        p.op("dve", (lambda v=v: nc.vector.scalar_tensor_tensor(
            out=A[:, :, v], in0=mod[:, sc_i * 8:(sc_i + 1) * 8, v], scalar=1.0, in1=gsb[:],
            op0=ALU.add, op1=ALU.mult)),
            reads=[tag + "g", k.modname],
            writes=[(tag + "A", v)])
    if not hasattr(k, "_nm"):
        k._nm = (k.sb("nm_rstd", [128, ntok], F32),
                 [k.sb("nm_sq%d" % i, [128, 512], F32) for i in range(2)],
                 [k.sb("nm_tmp%d" % i, [128, 512], F32) for i in range(2)])
    rstd, sq, tmp = k._nm
    it = 0
    for bi, (c0, c1, v) in enumerate(blocks):
        w = c1 - c0
        pt, pn = psb[bi % len(psb)]
        for kk in range(8):
            b = it % 2
            it += 1
            p.op("act", (lambda b=b, kk=kk, c0=c0, c1=c1, w=w: nc.scalar.activation(
                out=sq[b][:, :w], in_=xT[:, kk, c0:c1], func=AF.Square)),
                reads=[(xname, kk)], writes=[("nm_sq", b)])
            p.op("pe", (lambda b=b, kk=kk, w=w, pt=pt: nc.tensor.matmul(
                pt[:, :w], lhsT=k.ones[:], rhs=sq[b][:, :w], start=(kk == 0), stop=(kk == 7))),
                reads=["ones", ("nm_sq", b)], writes=[pn])
        p.op("dve", (lambda c0=c0, c1=c1, w=w, pt=pt: nc.vector.tensor_scalar(
            out=rstd[:, c0:c1], in0=pt[:, :w], scalar1=1.0 / D, scalar2=EPS, op0=ALU.mult, op1=ALU.add)),
            reads=[pn], writes=[("nm_rstd", bi)])
        p.op("act", (lambda c0=c0, c1=c1: nc.scalar.activation(
            out=rstd[:, c0:c1], in_=rstd[:, c0:c1], func=AF.Sqrt)),
            reads=[("nm_rstd", bi)], writes=[("nm_rstd", bi)])
        p.op("dve", (lambda c0=c0, c1=c1: nc.vector.reciprocal(
            out=rstd[:, c0:c1], in_=rstd[:, c0:c1])),
            reads=[("nm_rstd", bi)], writes=[("nm_rstd", bi)])
        for kk in range(8):
            b = it % 2
            it += 1
            p.op("dve", (lambda b=b, kk=kk, c0=c0, c1=c1, w=w: nc.vector.tensor_tensor(
                out=tmp[b][:, :w], in0=xT[:, kk, c0:c1], in1=rstd[:, c0:c1], op=ALU.mult)),
                reads=[(xname, kk), ("nm_rstd", bi)], writes=[("nm_tmp", b)])
            p.op("act", (lambda b=b, kk=kk, c0=c0, c1=c1, w=w, v=v: nc.scalar.activation(
                out=hnT[:, kk, c0:c1], in_=tmp[b][:, :w], func=AF.Identity,
                scale=A[:, kk, v:v + 1], bias=mod[:, sh_i * 8 + kk, v:v + 1])),
                reads=[("nm_tmp", b), (tag + "A", v), k.modname],
                writes=[(tag + "hnT", kk, bi)])
    return rstd


def build_phaseA():
    k = K()
    nc, p = k.nc, k.p
    NT = TL + CTX
    xT_d = k.din("xT", [D, TL])
    cT_d = k.din("cT", [D, CTX])
    cvec_d = k.din("cvec", [128, 8, 2])
    wada_d = k.din("wada", [D, 6 * D])
    bada_d = k.din("bada", [128, 48])
    g_d = k.din("g1", [128, 8])
    win_d = k.din("win", [D, IN_MIX])
    zT_d = k.dout("zT", [IN_MIX, NT])

    emit_consts(k)
    xT = k.sb("xT_sb", [128, 8, NT], F32)
    hnT = k.sb("hnT", [128, 8, NT], BF16)
    xv = xT_d.rearrange("(k p) t -> p k t", p=128)
    cv = cT_d.rearrange("(k p) t -> p k t", p=128)
    for kk in range(8):
        p.dma(k.dq(), xT[:, kk, 0:TL], xv[:, kk, :], writes=[("xT", kk)])
        p.dma(k.dq(), xT[:, kk, TL:NT], cv[:, kk, :], writes=[("xT", kk)])
    k.modname = "Amod"
    mod = emit_adaln(k, wada_d, bada_d, cvec_d, [0, 1], "A")
    banks = [(k.ps("bank%d" % i, [128, 512], F32), "bank%d" % i) for i in range(7)]
    blocks = [(i * 512, (i + 1) * 512, 0) for i in range(4)] + [(TL, NT, 1)]
    emit_norm_mod(k, xT, NT, blocks, g_d, mod, 0, 1, hnT, "n1", "xT", banks)
    wv = win_d.rearrange("(k p) m -> p k m", p=128)
    wst = [k.sb("wst%d" % i, [128, 8, 128], F32) for i in range(2)]
    wbf = [k.sb("wbf%d" % i, [128, 8, 128], BF16) for i in range(2)]
    ost = [k.sb("ost%d" % i, [128, NT], F32) for i in range(2)]
    nm = (IN_MIX + 127) // 128
    bi = 0
    for m in range(nm):
        c0 = m * 128
        mw = min(128, IN_MIX - c0)
        b = m % 2
        p.dma(k.dq(), wst[b][:, :, :mw], wv[:, :, c0:c0 + mw], writes=[("wst", b)])
        p.op("pool", (lambda b=b, mw=mw: nc.gpsimd.tensor_copy(out=wbf[b][:, :, :mw], in_=wst[b][:, :, :mw])),
             reads=[("wst", b)], writes=[("wbf", b)])
        for tb, (t0, t1, v) in enumerate(blocks):
            w = t1 - t0
            pt, pn = banks[bi % len(banks)]
            bi += 1
            for kk in range(8):
                p.op("pe", (lambda b=b, kk=kk, mw=mw, t0=t0, t1=t1, w=w, pt=pt: nc.tensor.matmul(
                    pt[:mw, :w], lhsT=wbf[b][:, kk, :mw], rhs=hnT[:, kk, t0:t1], start=(kk == 0), stop=(kk == 7))),
                    reads=[("wbf", b), ("n1hnT", kk, tb)], writes=[pn])
            if tb % 2 == 0:
                p.op("act", (lambda b=b, mw=mw, t0=t0, t1=t1, w=w, pt=pt: nc.scalar.copy(
                    out=ost[b][:mw, t0:t1], in_=pt[:mw, :w])), reads=[pn], writes=[("ost", b, tb)])
            else:
                p.op("dve", (lambda b=b, mw=mw, t0=t0, t1=t1, w=w, pt=pt: nc.vector.tensor_copy(
                    out=ost[b][:mw, t0:t1], in_=pt[:mw, :w])), reads=[pn], writes=[("ost", b, tb)])
        p.dma(k.dq(), zT_d[c0:c0 + mw, :], ost[b][:mw, :], reads=[("ost", b)], writes=[("zout", m)])
    return k.finish(["zout"])


def _ft(a):
    return np.ascontiguousarray(np.asarray(a, np.float32).reshape(-1, 128).T)


def run_phaseA(xT, cT, c, c_ctx, w_ada_l, b_ada_l, g_l, w_in_l):
    nc = build_phaseA()
    cvec = np.ascontiguousarray(np.stack([_ft(c.reshape(-1)), _ft(c_ctx.reshape(-1))], axis=-1))
    common = dict(cT=np.ascontiguousarray(cT), cvec=cvec, wada=np.ascontiguousarray(w_ada_l),
                  bada=_ft(b_ada_l), g1=_ft(g_l), win=np.ascontiguousarray(w_in_l[:, :IN_MIX]))
    in_maps = [dict(common, xT=np.ascontiguousarray(xT[:, i * TL:(i + 1) * TL])) for i in range(NCORE)]
    res = run_bass_kernel_spmd(nc, in_maps, core_ids=list(range(NCORE)))
    return [r["zT"] for r in res.results]


def _E(k, eng):
    return k.p.eng[eng]


def op_tt(k, eng, out, in0, in1, op, r, w):
    e = _E(k, eng)
    k.p.op(eng, lambda: e.tensor_tensor(out=out, in0=in0, in1=in1, op=op), r, w)


def op_ts(k, eng, out, in0, s1, s2, op0, op1, r, w):
    e = _E(k, eng)
    if op1 is None:
        k.p.op(eng, lambda: e.tensor_scalar(out=out, in0=in0, scalar1=s1, scalar2=None, op0=op0), r, w)
    else:
        k.p.op(eng, lambda: e.tensor_scalar(out=out, in0=in0, scalar1=s1, scalar2=s2, op0=op0, op1=op1), r, w)


def op_stt(k, eng, out, in0, scalar, in1, op0, op1, r, w):
    e = _E(k, eng)
    k.p.op(eng, lambda: e.scalar_tensor_tensor(out=out, in0=in0, scalar=scalar, in1=in1, op0=op0, op1=op1), r, w)


def op_act(k, out, in_, func, r, w, scale=1.0, bias=0.0):
    nc = k.nc
    k.p.op("act", lambda: nc.scalar.activation(out=out, in_=in_, func=func, bias=bias, scale=scale), r, w)


def op_copy(k, eng, out, in_, r, w):
    e = _E(k, eng)
    if eng == "act":
        k.p.op(eng, lambda: e.copy(out=out, in_=in_), r, w)
    else:
        k.p.op(eng, lambda: e.tensor_copy(out=out, in_=in_), r, w)


def op_mm(k, out, lhsT, rhs, start, stop, r, w):
    nc = k.nc
    k.p.op("pe", lambda: nc.tensor.matmul(out, lhsT=lhsT, rhs=rhs, start=start, stop=stop), r, w)


def op_scan(k, eng, out, d0, d1, init, r, w, op0=None, op1=None):
    e = _E(k, eng)
    op0 = op0 or ALU.mult
    op1 = op1 or ALU.add
    k.p.op(eng, lambda: e.tensor_tensor_scan(out=out, data0=d0, data1=d1, initial=init, op0=op0, op1=op1), r, w)


def op_memset(k, eng, ap, val, w):
    e = _E(k, eng)
    k.p.op(eng, lambda: e.memset(ap, val), (), w)


def bcast_cols(col_ap, n):
    return bass.AP(col_ap.tensor, col_ap.offset, [list(col_ap.ap[0]), [0, n]])


def emit_identity(k):
    nc = k.nc
    io = k.sb("ident_i", [128, 128], I32)
    ident = k.sb("ident", [128, 128], F32)
    k.p.op("pool", lambda: nc.gpsimd.iota(io[:], pattern=[[1, 128]], base=0, channel_multiplier=-1), (), ["ident_i"])
    op_copy(k, "dve", ident[:], io[:], ["ident_i"], ["ident"])
    op_ts(k, "dve", ident[:], ident[:], 0.0, None, ALU.is_equal, None, ["ident"], ["ident"])
    k.ident = ident
    return ident


TWO_PI = 6.283185307179586
C1_2PI = 6.28125
C2_2PI = TWO_PI - 6.28125
PI_SAFE = 3.1415925


def emit_sin(k, out, ang, shift, tmp, tmpi, r, w, tag):
    wn = [tag + "_t"]
    wi = [tag + "_i"]
    op_ts(k, "dve", tmp, ang, shift, 1.0 / TWO_PI, ALU.add, ALU.mult, r, wn)
    op_copy(k, "dve", tmpi, tmp, wn, wi)
    op_copy(k, "dve", tmp, tmpi, wi, wn)
    op_stt(k, "dve", out, tmp, -C1_2PI, ang, ALU.mult, ALU.add, wn + list(r), w)
    if shift != 0.0:
        op_ts(k, "dve", out, out, shift, None, ALU.add, None, w, w)
    op_stt(k, "dve", out, tmp, -C2_2PI, out, ALU.mult, ALU.add, wn + list(w), w)
    op_ts(k, "dve", out, out, -PI_SAFE, PI_SAFE, ALU.max, ALU.min, w, w)
    op_act(k, out, out, AF.Sin, w, w)


NSEQ = CTX + SEQ
S5_T = 1024


def build_s5():
    k = K()
    nc, p = k.nc, k.p
    T = S5_T
    u_d = k.din("uT", [32, NSEQ])
    lam_d = k.din("lam", [128, 6])
    bp_d = k.din("bpad", [128, 2, 2, 32])
    ct_d = k.din("ctp", [128, 2, 2, 32])
    y_d = k.dout("ys", [32, NSEQ])
    emit_consts(k)
    ident = emit_identity(k)
    lam = k.sb("lam", [128, 6], F32)
    bp = k.sb("bp", [128, 2, 2, 32], F32)
    ct = k.sb("ct", [128, 2, 2, 32], F32)
    p.dma("sp", lam[:], lam_d, writes=["lam"])
    p.dma("sp", bp[:], bp_d, writes=["bp"])
    p.dma("sp", ct[:], ct_d, writes=["ct"])
    for di in range(2):
        op_ts(k, "dve", ct[:, di, 1, :], ct[:, di, 1, :], -1.0, None, ALU.mult, None, ["ct"], ["ct"])
    sc = k.sb("s5sc", [128, 40], F32)
    sci = k.sb("s5sci", [128, 8], I32)
    col = lambda i: sc[:, i:i + 1]
    bbT = [[k.sb("bbT%d%d" % (di, ri), [32, 128], F32) for ri in range(2)] for di in range(2)]
    bbf = k.sb("bbf", [128, 2, 2, 32], F32)
    pst = k.ps("pst", [32, 4, 128], F32)
    for di in range(2):
        b0 = 16 * di
        S = lambda i: col(b0 + i)
        rw = ["s5sc"]
        op_act(k, S(0), lam[:, 4 + di:5 + di], AF.Exp, ["lam"], rw)
        op_tt(k, "dve", S(1), lam[:, di:di + 1], S(0), ALU.mult, ["lam"] + rw, rw)
        op_act(k, S(1), S(1), AF.Exp, rw, rw)
        op_tt(k, "dve", S(2), lam[:, 2 + di:3 + di], S(0), ALU.mult, ["lam"] + rw, rw)
        emit_sin(k, S(4), S(2), 0.0, S(13), sci[:, 0:1], rw, rw, "s5r")
        emit_sin(k, S(3), S(2), 0.5 * np.pi, S(13), sci[:, 0:1], rw, rw, "s5r")
        op_tt(k, "dve", S(5), S(1), S(3), ALU.mult, rw, rw)
        op_tt(k, "dve", S(6), S(1), S(4), ALU.mult, rw, rw)
        op_ts(k, "dve", S(7), S(5), -1.0, None, ALU.add, None, rw, rw)
        op_tt(k, "dve", S(8), lam[:, di:di + 1], lam[:, di:di + 1], ALU.mult, ["lam"], rw)
        op_stt(k, "dve", S(8), lam[:, 2 + di:3 + di], lam[:, 2 + di:3 + di], S(8), ALU.mult, ALU.add, ["lam"] + rw, rw)
        op_copy(k, "dve", S(12), S(8), rw, rw)
        k.p.op("dve", (lambda o=S(8), i=S(12): nc.vector.reciprocal(out=o, in_=i)), rw, rw)
        op_tt(k, "dve", S(11), S(7), lam[:, di:di + 1], ALU.mult, ["lam"] + rw, rw)
        op_stt(k, "dve", S(9), S(6), lam[:, 2 + di:3 + di], S(11), ALU.mult, ALU.add, ["lam"] + rw, rw)
        op_tt(k, "dve", S(9), S(9), S(8), ALU.mult, rw, rw)
        op_tt(k, "dve", S(11), S(7), lam[:, 2 + di:3 + di], ALU.mult, ["lam"] + rw, rw)
        op_stt(k, "dve", S(10), S(6), lam[:, di:di + 1], S(11), ALU.mult, ALU.subtract, ["lam"] + rw, rw)
        op_tt(k, "dve", S(10), S(10), S(8), ALU.mult, rw, rw)
        op_ts(k, "dve", bbf[:, di, 0, :], bp[:, di, 1, :], S(10), -1.0, ALU.mult, ALU.mult, ["bp"] + rw, [("bbf", di, 0)])
        op_stt(k, "dve", bbf[:, di, 0, :], bp[:, di, 0, :], S(9), bbf[:, di, 0, :], ALU.mult, ALU.add, ["bp", ("bbf", di, 0)] + rw, [("bbf", di, 0)])
        op_ts(k, "dve", bbf[:, di, 1, :], bp[:, di, 0, :], S(10), None, ALU.mult, None, ["bp"] + rw, [("bbf", di, 1)])
        op_stt(k, "dve", bbf[:, di, 1, :], bp[:, di, 1, :], S(9), bbf[:, di, 1, :], ALU.mult, ALU.add, ["bp", ("bbf", di, 1)] + rw, [("bbf", di, 1)])
        for ri in range(2):
            op_mm(k, pst[:, di * 2 + ri, :], bbf[:, di, ri, :], ident[:], True, True, [("bbf", di, ri), "ident"], ["pst"])
            op_copy(k, "dve", bbT[di][ri][:], pst[:, di * 2 + ri, :], ["pst"], [("bbT", di, ri)])
    jfi = k.sb("jfi", [128, T], I32)
    jf = k.sb("jf", [128, T], F32)
    tang = k.sb("tang", [128, T], F32)
    ttmp = k.sb("ttmp", [128, T], F32)
    tti = k.sb("tti", [128, T], I32)
    p.op("pool", lambda: nc.gpsimd.iota(jfi[:], pattern=[[1, T]], base=0, channel_multiplier=0), (), ["jfi"])
    op_copy(k, "dve", jf[:], jfi[:], ["jfi"], ["jf"])
    tab = [[k.sb("tab%d%d" % (di, cs), [128, T], F32) for cs in range(2)] for di in range(2)]
    for di in range(2):
        op_ts(k, "dve", tang[:], jf[:], col(16 * di + 2), None, ALU.mult, None, ["jf", "s5sc"], ["tang"])
        emit_sin(k, tab[di][1][:], tang[:], 0.0, ttmp[:], tti[:], ["tang"], [("tab", di, 1)], "s5t")
        emit_sin(k, tab[di][0][:], tang[:], 0.5 * np.pi, ttmp[:], tti[:], ["tang"], [("tab", di, 0)], "s5t")
    segs = [(0, CTX)] + [(CTX + i * T, CTX + (i + 1) * T) for i in range(SEQ // T)]
    ub = [k.sb("ub%d" % i, [32, T], F32) for i in range(2)]
    bur = [k.ps("bur%d" % i, [128, 512], F32) for i in range(2)]
    bui = [k.ps("bui%d" % i, [128, 512], F32) for i in range(2)]
    yps = [k.ps("yps%d" % i, [32, 512], F32) for i in range(2)]
    W = {n: k.sb("s5" + n, [128, T], F32) for n in ("br", "bi", "pr", "pi", "hr", "hi", "t1", "t2")}
    yst = [k.sb("yst%d" % i, [32, T], F32) for i in range(2)]
    carry = k.sb("carry", [128, 4], F32)
    it = 0
    for di in range(2):
        order = segs if di == 0 else [segs[0]] + segs[:0:-1]
        cosT, sinT = tab[di]
        rcol = col(16 * di + 1)
        cth, sth = cosT[:, 1:2], sinT[:, 1:2]
        for si, (t0, t1) in enumerate(order):
            n = t1 - t0
            b = it % 2
            it += 1
            first = si == 0
            fw = (lambda ap: ap) if di == 0 else rev_ap
            cv = cosT[:, 0:n] if di == 0 else rev_ap(cosT[:, 0:n])
            sv = sinT[:, 0:n] if di == 0 else rev_ap(sinT[:, 0:n])
            tabr = [("tab", di, 0), ("tab", di, 1)]
            p.dma("sp", ub[b][:, :n], u_d[:, t0:t1], writes=[("ub", b)])
            nb = (n + 511) // 512
            for j in range(nb):
                c0, c1 = j * 512, min(n, (j + 1) * 512)
                w = c1 - c0
                op_mm(k, bur[j][:, :w], bbT[di][0][:], ub[b][:, c0:c1], True, True, [("bbT", di, 0), ("ub", b)], [("bur", j)])
                op_mm(k, bui[j][:, :w], bbT[di][1][:], ub[b][:, c0:c1], True, True, [("bbT", di, 1), ("ub", b)], [("bui", j)])
                op_tt(k, "dve", W["t1"][:, c0:c1], bur[j][:, :w], cv[:, c0:c1], ALU.mult, [("bur", j)] + tabr, [("t1", j)])
                op_tt(k, "dve", W["t2"][:, c0:c1], bui[j][:, :w], sv[:, c0:c1], ALU.mult, [("bui", j)] + tabr, [("t2", j)])
                op_tt(k, "pool", W["br"][:, c0:c1], W["t1"][:, c0:c1], W["t2"][:, c0:c1], ALU.add, [("t1", j), ("t2", j)], [("br", j)])
                op_tt(k, "dve", W["t1"][:, c0:c1], bui[j][:, :w], cv[:, c0:c1], ALU.mult, [("bui", j)] + tabr, [("t1", j)])
                op_tt(k, "dve", W["t2"][:, c0:c1], bur[j][:, :w], sv[:, c0:c1], ALU.mult, [("bur", j)] + tabr, [("t2", j)])
                op_tt(k, "pool", W["bi"][:, c0:c1], W["t1"][:, c0:c1], W["t2"][:, c0:c1], ALU.subtract, [("t1", j), ("t2", j)], [("bi", j)])
            rb = bcast_cols(rcol, n)
            ire = 0.0 if first else carry[:, 2:3]
            iim = 0.0 if first else carry[:, 3:4]
            op_scan(k, "dve", fw(W["pr"][:, :n]), rb, fw(W["br"][:, :n]), ire, ["br", "s5sc", "carry"], ["pr"])
            op_scan(k, "dve", fw(W["pi"][:, :n]), rb, fw(W["bi"][:, :n]), iim, ["bi", "s5sc", "carry"], ["pi"])
            op_tt(k, "dve", W["t1"][:, :n], W["pr"][:, :n], cv, ALU.mult, ["pr"] + tabr, ["t1"])
            op_tt(k, "pool", W["t2"][:, :n], W["pi"][:, :n], sv, ALU.mult, ["pi"] + tabr, ["t2"])
            op_tt(k, "dve", W["hr"][:, :n], W["t1"][:, :n], W["t2"][:, :n], ALU.subtract, ["t1", "t2"], ["hr"])
            op_tt(k, "pool", W["t1"][:, :n], W["pr"][:, :n], sv, ALU.mult, ["pr"] + tabr, ["t1"])
            op_tt(k, "dve", W["t2"][:, :n], W["pi"][:, :n], cv, ALU.mult, ["pi"] + tabr, ["t2"])
            op_tt(k, "pool", W["hi"][:, :n], W["t1"][:, :n], W["t2"][:, :n], ALU.add, ["t1", "t2"], ["hi"])
            lc = n - 1 if di == 0 else 0
            op_tt(k, "dve", carry[:, 0:1], W["hr"][:, lc:lc + 1], cth, ALU.mult, ["hr"] + tabr, ["carry0"])
            op_tt(k, "dve", carry[:, 1:2], W["hi"][:, lc:lc + 1], sth, ALU.mult, ["hi"] + tabr, ["carry1"])
            op_tt(k, "dve", carry[:, 2:3], carry[:, 0:1], carry[:, 1:2], ALU.subtract, ["carry0", "carry1", "pr", "pi"], ["carry"])
            op_tt(k, "dve", carry[:, 0:1], W["hr"][:, lc:lc + 1], sth, ALU.mult, ["hr", "carry"] + tabr, ["carry0"])
            op_tt(k, "dve", carry[:, 1:2], W["hi"][:, lc:lc + 1], cth, ALU.mult, ["hi", "carry"] + tabr, ["carry1"])
            op_tt(k, "dve", carry[:, 3:4], carry[:, 0:1], carry[:, 1:2], ALU.add, ["carry0", "carry1"], ["carry"])
            if di == 1:
                p.dma("pool", yst[b][:, :n], y_d[:, t0:t1], reads=[("yout", t0)], writes=[("yst", b)])
            for j in range(nb):
                c0, c1 = j * 512, min(n, (j + 1) * 512)
                w = c1 - c0
                op_mm(k, yps[j][:, :w], ct[:, di, 0, :], W["hr"][:, c0:c1], True, False, ["ct", "hr"], [("yps", j)])
                op_mm(k, yps[j][:, :w], ct[:, di, 1, :], W["hi"][:, c0:c1], False, True, ["ct", "hi"], [("yps", j)])
                if di == 0:
                    op_copy(k, "act", yst[b][:, c0:c1], yps[j][:, :w], [("yps", j)], [("yst", b)])
                else:
                    op_tt(k, "dve", yst[b][:, c0:c1], yst[b][:, c0:c1], yps[j][:, :w], ALU.add, [("yps", j), ("yst", b)], [("yst", b)])
            p.dma("sp", y_d[:, t0:t1], yst[b][:, :n], reads=[("yst", b)], writes=[("yout", t0)])
    return k.finish(["yout"])


def s5_inputs(core, zu_all, lam_re, lam_im, lstep, b_re, b_im, c_re, c_im):
    g0 = 2 * core
    lam = np.zeros((128, 6), np.float32)
    bp = np.zeros((128, 2, 2, 32), np.float32)
    ct = np.zeros((128, 2, 2, 32), np.float32)
    for di in range(2):
        for gl in range(2):
            g = g0 + gl
            sl = slice(gl * 64, (gl + 1) * 64)
            lam[sl, di] = lam_re[di, g]
            lam[sl, 2 + di] = lam_im[di, g]
            lam[sl, 4 + di] = lstep[di, g]
            bp[sl, di, 0, gl * 16:(gl + 1) * 16] = b_re[di, g]
            bp[sl, di, 1, gl * 16:(gl + 1) * 16] = b_im[di, g]
            ct[sl, di, 0, gl * 16:(gl + 1) * 16] = c_re[di, g].T
            ct[sl, di, 1, gl * 16:(gl + 1) * 16] = c_im[di, g].T
    return dict(uT=np.ascontiguousarray(zu_all[32 * core:32 * core + 32]), lam=lam, bpad=bp, ctp=ct)


LRU_T = 1024


def build_lru():
    k = K()
    nc, p = k.nc, k.p
    T = LRU_T
    x_d = k.din("lxT", [32, NSEQ])
    y_d = k.din("lyT", [32, NSEQ])
    par_d = k.din("lpar", [32, 12])
    w_d = k.din("lw", [32, 2, 2, 32])
    o_d = k.dout("lo", [32, NSEQ])
    par = k.sb("lpar", [32, 12], F32)
    w = k.sb("lw", [32, 2, 2, 32], F32)
    cl = k.sb("lcl", [32, 2], F32)
    p.dma("sp", par[:], par_d, writes=["lpar"])
    p.dma("sp", w[:], w_d, writes=["lw"])
    op_act(k, cl[:], par[:, 9:11], AF.Exp, ["lpar"], ["lcl"], scale=-1.0)
    op_ts(k, "dve", cl[:], cl[:], 1.0, None, ALU.add, None, ["lcl"], ["lcl"])
    op_act(k, cl[:], cl[:], AF.Ln, ["lcl"], ["lcl"])
    op_ts(k, "dve", cl[:], cl[:], -8.0, None, ALU.mult, None, ["lcl"], ["lcl"])
    segs = [(0, CTX, 0, CTX)] + [(CTX + i * T, CTX + (i + 1) * T, CTX, NSEQ) for i in range(SEQ // T)]
    xs = [k.sb("lxs%d" % i, [32, T + 3], F32) for i in range(2)]
    W = {n: k.sb("l" + n, [32, T], F32) for n in ("xc", "r", "i", "a", "q", "b", "h")}
    ys = [k.sb("lys%d" % i, [32, T], F32) for i in range(2)]
    hf = [k.sb("lhf%d" % i, [32, T], F32) for i in range(2)]
    pr = [k.ps("lpr%d" % i, [32, 512], F32) for i in range(2)]
    pi = [k.ps("lpi%d" % i, [32, 512], F32) for i in range(2)]
    carry = k.sb("lcarry", [32, 1], F32)
    it = 0
    for di in range(2):
        order = segs if di == 0 else [segs[0]] + segs[:0:-1]
        for si, (t0, t1, lo, hi) in enumerate(order):
            n = t1 - t0
            b = it % 2
            it += 1
            fw = (lambda ap: ap) if di == 0 else rev_ap
            a0, a1 = max(lo, t0 - 2), min(hi, t1 + 1)
            if a0 > t0 - 2:
                op_memset(k, "pool", xs[b][:, 0:2], 0.0, [("lxs", b)])
            if a1 < t1 + 1:
                op_memset(k, "pool", xs[b][:, n + 2:n + 3], 0.0, [("lxs", b)])
            p.dma("sp", xs[b][:, a0 - (t0 - 2):a1 - (t0 - 2)], x_d[:, a0:a1], writes=[("lxs", b)])
            X = xs[b]
            op_ts(k, "dve", W["xc"][:, :n], X[:, 0:n], par[:, 0:1], par[:, 4:5], ALU.mult, ALU.add, [("lxs", b), "lpar"], ["xc"])
            for j in range(1, 4):
                op_stt(k, "dve", W["xc"][:, :n], X[:, j:j + n], par[:, j:j + 1], W["xc"][:, :n], ALU.mult, ALU.add, [("lxs", b), "lpar", "xc"], ["xc"])
            nb = (n + 511) // 512
            for j in range(nb):
                c0, c1 = j * 512, min(n, (j + 1) * 512)
                wd = c1 - c0
                op_mm(k, pr[j][:, :wd], w[:, di, 0, :], W["xc"][:, c0:c1], True, True, ["lw", "xc"], [("lpr", j)])
                op_mm(k, pi[j][:, :wd], w[:, di, 1, :], W["xc"][:, c0:c1], True, True, ["lw", "xc"], [("lpi", j)])
                op_act(k, W["r"][:, c0:c1], pr[j][:, :wd], AF.Sigmoid, [("lpr", j), "lpar"], [("r", j)], bias=par[:, 5 + di:6 + di])
                op_act(k, W["i"][:, c0:c1], pi[j][:, :wd], AF.Sigmoid, [("lpi", j), "lpar"], [("i", j)], bias=par[:, 7 + di:8 + di])
            op_act(k, W["a"][:, :n], W["r"][:, :n], AF.Exp, ["r", "lcl"], ["a"], scale=cl[:, di:di + 1])
            op_tt(k, "dve", W["q"][:, :n], W["a"][:, :n], W["a"][:, :n], ALU.mult, ["a"], ["q"])
            op_ts(k, "dve", W["q"][:, :n], W["q"][:, :n], -1.0, 1.0, ALU.mult, ALU.add, ["q"], ["q"])
            op_act(k, W["q"][:, :n], W["q"][:, :n], AF.Sqrt, ["q"], ["q"])
            op_tt(k, "pool", W["b"][:, :n], W["i"][:, :n], W["xc"][:, :n], ALU.mult, ["i", "xc"], ["b"])
            op_tt(k, "dve", W["b"][:, :n], W["b"][:, :n], W["q"][:, :n], ALU.mult, ["b", "q"], ["b"])
            init = 0.0 if si == 0 else carry[:, 0:1]
            op_scan(k, "dve", fw(W["h"][:, :n]), fw(W["a"][:, :n]), fw(W["b"][:, :n]), init, ["a", "b", "lcarry"], ["h"])
            lc = n - 1 if di == 0 else 0
            op_copy(k, "dve", carry[:, 0:1], W["h"][:, lc:lc + 1], ["h"], ["lcarry"])
            if di == 0:
                p.dma("pool", o_d[:, t0:t1], W["h"][:, :n], reads=["h"], writes=[("lout", t0)])
            else:
                p.dma("pool", hf[b][:, :n], o_d[:, t0:t1], reads=[("lout", t0)], writes=[("lhf", b)])
                p.dma("sp", ys[b][:, :n], y_d[:, t0:t1], writes=[("lys", b)])
                op_act(k, ys[b][:, :n], ys[b][:, :n], AF.Gelu_apprx_tanh, [("lys", b)], [("lys", b)])
                op_tt(k, "pool", hf[b][:, :n], hf[b][:, :n], W["h"][:, :n], ALU.add, [("lhf", b), "h"], [("lhf", b)])
                op_tt(k, "dve", hf[b][:, :n], hf[b][:, :n], ys[b][:, :n], ALU.mult, [("lhf", b), ("lys", b)], [("lhf", b)])
                p.dma("sp", o_d[:, t0:t1], hf[b][:, :n], reads=[("lhf", b)], writes=[("lout", t0)])
    return k.finish(["lout"])


def lru_inputs(core, zx_all, zy_all, conv_w, conv_b, wr, br, wi, bi, lam):
    ch = slice(32 * core, 32 * core + 32)
    par = np.zeros((32, 12), np.float32)
    par[:, 0:4] = conv_w[:, ch].T
    par[:, 4] = conv_b[ch]
    for di in range(2):
        par[:, 5 + di] = br[di, ch]
        par[:, 7 + di] = bi[di, ch]
        par[:, 9 + di] = lam[di, ch]
    w = np.zeros((32, 2, 2, 32), np.float32)
    for di in range(2):
        w[:, di, 0, :] = wr[di, core]
        w[:, di, 1, :] = wi[di, core]
    return dict(lxT=np.ascontiguousarray(zx_all[ch]), lyT=np.ascontiguousarray(zy_all[ch]), lpar=par, lw=w)


GLA_C = 64
NCHUNK = NSEQ // GLA_C


def build_gla():
    k = K()
    nc, p = k.nc, k.p
    q_d = k.din("gq", [32, NSEQ])
    k_d = k.din("gk", [32, NSEQ])
    a_d = k.din("ga", [32, NSEQ])
    vt_d = k.din("gvt", [64, NCHUNK, 32])
    kt_d = k.din("gkt", [64, NCHUNK, 32])
    w_d = k.din("gw", [33, 2, 32])
    o_d = k.dout("go", [32, NSEQ])
    ioi = k.sb("gioi", [64, 64], I32)
    iof = k.sb("giof", [64, 64], F32)
    p.op("pool", lambda: nc.gpsimd.iota(ioi[:], pattern=[[1, 64]], base=0, channel_multiplier=-1), (), ["gioi"])
    op_copy(k, "dve", iof[:], ioi[:], ["gioi"], ["giof"])
    msk = {}
    for nm, cmp in (("ge", ALU.is_ge), ("le", ALU.is_le), ("lt", ALU.is_lt), ("gt", ALU.is_gt)):
        msk[nm] = k.sb("gm" + nm, [64, 64], F32)
        op_ts(k, "dve", msk[nm][:], iof[:], 0.0, None, cmp, None, ["giof"], ["gm" + nm])
    amask, triI, triS = [], [], []
    for di in range(2):
        am = k.sb("gam%d" % di, [64, 8, 64], F32)
        src = msk["ge"] if di == 0 else msk["le"]
        for n in range(8):
            op_copy(k, "dve", am[:, n, :], src[:], ["gmge", "gmle"], ["gam%d" % di])
        ti = k.sb("gti%d" % di, [64, 64], F32)
        ts_ = k.sb("gts%d" % di, [64, 64], F32)
        op_ts(k, "dve", ti[:], src[:], -1.0 / 16, None, ALU.mult, None, ["gmge", "gmle"], ["gti%d" % di])
        op_ts(k, "dve", ts_[:], (msk["lt"] if di == 0 else msk["gt"])[:], -1.0 / 16, None, ALU.mult, None, ["gmlt", "gmgt"], ["gts%d" % di])
        amask.append(am); triI.append(ti); triS.append(ts_)
    w = k.sb("gw", [33, 2, 32], F32)
    p.dma("sp", w[:], w_d, writes=["gw"])
    NB = 2
    a1 = [k.sb("ga1%d" % i, [33, 512], F32) for i in range(NB)]
    qb = [k.sb("gqb%d" % i, [32, 512], F32) for i in range(NB)]
    kb = [k.sb("gkb%d" % i, [32, 512], F32) for i in range(NB)]
    vt = [k.sb("gvt%d" % i, [64, 8, 32], F32) for i in range(NB)]
    kt = [k.sb("gkt%d" % i, [64, 8, 32], F32) for i in range(NB)]
    of = [k.sb("gof%d" % i, [32, 512], F32) for i in range(NB)]
    for i in range(NB):
        op_memset(k, "pool", a1[i][32:33, :], 1.0, [("ga1", i, "one")])
    L = k.sb("gL", [64, 8, 32], F32)
    kd = k.sb("gkd", [64, 8, 32], F32)
    eb = k.sb("geb", [32, 8, 64], F32)
    enb = k.sb("genb", [32, 8, 64], F32)
    qe = k.sb("gqe", [32, 512], F32)
    ke = k.sb("gke", [32, 512], F32)
    attm = k.sb("gattm", [64, 8, 64], F32)
    Sl = k.sb("gS", [32, 9, 32], F32)
    pz = k.ps("gpz", [64, 8, 32], F32)
    pg = k.ps("gpg", [64, 8, 32], F32)
    pb = k.ps("gpb", [32, 8, 64], F32)
    pa = k.ps("gpa", [64, 8, 64], F32)
    pkv = k.ps("gpkv", [32, 8, 32], F32)
    po = k.ps("gpo", [32, 8, 64], F32)
    blocks = [(0, 4)] + [(4 + 8 * i, 8) for i in range(32)]
    it = 0
    for di in range(2):
        order = blocks if di == 0 else [blocks[0]] + blocks[:0:-1]
        for bi_, (n0, nch) in enumerate(order):
            b = it % NB
            it += 1
            t0, n = n0 * 64, nch * 64
            t1 = t0 + n
            p.dma("sp", a1[b][0:32, :n], a_d[:, t0:t1], writes=[("ga1", b, "a")])
            p.dma("pool", qb[b][:, :n], q_d[:, t0:t1], writes=[("gqb", b)])
            p.dma("sp", kb[b][:, :n], k_d[:, t0:t1], writes=[("gkb", b)])
            p.dma("pool", vt[b][:, :nch, :], vt_d[:, n0:n0 + nch, :], writes=[("gvt", b)])
            p.dma("sp", kt[b][:, :nch, :], kt_d[:, n0:n0 + nch, :], writes=[("gkt", b)])
            if di == 1:
                p.dma("pool", of[b][:, :n], o_d[:, t0:t1], reads=[("gout", t0)], writes=[("gof", b)])
            if bi_ == 0:
                op_memset(k, "dve", Sl[:, 0, :], 0.0, [("gS", 0)])
            for c in range(nch):
                op_mm(k, pz[:, c, :], a1[b][:, c * 64:(c + 1) * 64], w[:, di, :], True, True, [("ga1", b), "gw"], ["gpz"])
            op_act(k, L[:, :nch, :], pz[:, :nch, :], AF.Exp, ["gpz"], ["gL"], scale=-1.0)
            op_ts(k, "dve", L[:, :nch, :], L[:, :nch, :], 1.0, None, ALU.add, None, ["gL"], ["gL"])
            op_act(k, L[:, :nch, :], L[:, :nch, :], AF.Ln, ["gL"], ["gL"])
            op_mm(k, pg[:, :nch, :], triS[di][:], L[:, :nch, :], True, True, ["gts%d" % di, "gL"], ["gpg"])
            op_act(k, kd[:, :nch, :], pg[:, :nch, :], AF.Exp, ["gpg"], ["gkd"])
            op_tt(k, "dve", kd[:, :nch, :], kd[:, :nch, :], kt[b][:, :nch, :], ALU.mult, ["gkd", ("gkt", b)], ["gkd"])
            for c in range(nch):
                op_mm(k, pb[:, c, :], L[:, c, :], triI[di][:], True, True, ["gL", "gti%d" % di], ["gpb"])
            op_act(k, eb[:, :nch, :], pb[:, :nch, :], AF.Exp, ["gpb"], ["geb"])
            op_act(k, enb[:, :nch, :], pb[:, :nch, :], AF.Exp, ["gpb"], ["genb"], scale=-1.0)
            ebf = eb[:].rearrange("p c j -> p (c j)")
            enbf = enb[:].rearrange("p c j -> p (c j)")
            op_stt(k, "dve", qe[:, :n], ebf[:, :n], GLA_C_SCALE, qb[b][:, :n], ALU.mult, ALU.mult, ["geb", ("gqb", b)], ["gqe"])
            op_tt(k, "pool", ke[:, :n], enbf[:, :n], kb[b][:, :n], ALU.mult, ["genb", ("gkb", b)], ["gke"])
            for c in range(nch):
                op_mm(k, pa[:, c, :], ke[:, c * 64:(c + 1) * 64], qe[:, c * 64:(c + 1) * 64], True, True, ["gke", "gqe"], ["gpa"])
            op_tt(k, "dve", attm[:, :nch, :], pa[:, :nch, :], amask[di][:, :nch, :], ALU.mult, ["gpa", "gam%d" % di], ["gattm"])
            for c in range(nch):
                op_mm(k, pkv[:, c, :], kd[:, c, :], vt[b][:, c, :], True, True, ["gkd", ("gvt", b)], ["gpkv"])
            cho = list(range(nch)) if di == 0 else list(range(nch - 1, -1, -1))
            lastj = 63 if di == 0 else 0
            for i, c in enumerate(cho):
                op_stt(k, "dve", Sl[:, i + 1, :], Sl[:, i, :], eb[:, c, lastj:lastj + 1], pkv[:, c, :], ALU.mult, ALU.add,
                       [("gS", i), "geb", "gpkv"], [("gS", i + 1)])
            for i, c in enumerate(cho):
                op_mm(k, po[:, c, :], vt[b][:, c, :], attm[:, c, :], True, False, [("gvt", b), "gattm"], ["gpo"])
                op_mm(k, po[:, c, :], Sl[:, i, :], qe[:, c * 64:(c + 1) * 64], False, True, [("gS", i), "gqe"], ["gpo"])
            pof = po[:].rearrange("p c j -> p (c j)")
            if di == 0:
                op_copy(k, "act", of[b][:, :n], pof[:, :n], ["gpo"], [("gof", b)])
            else:
                op_tt(k, "dve", of[b][:, :n], of[b][:, :n], pof[:, :n], ALU.add, ["gpo", ("gof", b)], [("gof", b)])
            p.dma("sp", o_d[:, t0:t1], of[b][:, :n], reads=[("gof", b)], writes=[("gout", t0)])
            op_copy(k, "dve", Sl[:, 0, :], Sl[:, nch, :], [("gS", nch)], [("gS", 0)])
    return k.finish(["gout"])


GLA_C_SCALE = 32 ** -0.5


def gla_inputs(core, zq, zk, zv, zg_a, wa2, ba):
    h, vh = core // 2, core % 2
    qT = np.ascontiguousarray(zq[32 * h:32 * h + 32])
    kT = np.ascontiguousarray(zk[32 * h:32 * h + 32])
    vT = zv[64 * h + 32 * vh:64 * h + 32 * vh + 32]
    tok = lambda xT: np.ascontiguousarray(xT.T.reshape(NCHUNK, 64, 32).transpose(1, 0, 2))
    w = np.zeros((33, 2, 32), np.float32)
    for di in range(2):
        w[16 * di:16 * di + 16, di, :] = wa2[di][:, 32 * h:32 * h + 32]
        w[32, di, :] = ba[di][32 * h:32 * h + 32]
    return dict(gq=qT, gk=kT, ga=np.ascontiguousarray(zg_a), gvt=tok(vT), gkt=tok(kT), gw=w)


NKT = NSEQ // 128
QH = SEQ // 2
LN1E4 = float(np.log(10000.0))


def build_da(lam_init, nqb=QH // 512):
    k = K()
    nc, p = k.nc, k.p
    q_d = k.din("dq", [64, QH])
    k_d = k.din("dk", [64, NSEQ])
    vt_d = k.din("dvt", [128, NKT, 64])
    qc_d = k.din("dqc", [64, CTX])
    par_d = k.din("dpar", [64, 4])
    lam_d = k.din("dlam", [32, 4])
    y_d = k.dout("dy", [64, QH])
    yc_d = k.dout("dyc", [64, CTX])
    emit_consts(k)
    par = k.sb("dpar", [64, 4], F32)
    lp = k.sb("dlp", [32, 4], F32)
    p.dma("sp", par[:], par_d, writes=["dpar"])
    p.dma("sp", lp[:], lam_d, writes=["dlp"])
    sc = k.sb("dsc", [64, 16], F32)
    col = lambda i: sc[:, i:i + 1]
    R_ = ["dsc"]
    pr2 = k.sb("dpr2", [32, 2], F32)
    op_tt(k, "dve", pr2[:, 0:1], lp[:, 0:1], lp[:, 1:2], ALU.mult, ["dlp"], ["dpr2"])
    op_tt(k, "dve", pr2[:, 1:2], lp[:, 2:3], lp[:, 3:4], ALU.mult, ["dlp"], ["dpr2"])
    pS = [k.ps("dpS%d" % i, [128, 2, 512], F32) for i in range(3)]
    pm = pS[0][:, 0, :]
    op_mm(k, pm[0:64, 0:2], k.ones[0:32, 0:64], pr2[:], True, True, ["ones", "dpr2"], [("dpS", 0)])
    op_act(k, sc[:, 3:5], pm[0:64, 0:2], AF.Exp, [("dpS", 0)], R_)
    op_tt(k, "dve", col(0), col(4), col(3), ALU.subtract, R_, R_)
    op_ts(k, "dve", col(0), col(0), -float(lam_init), None, ALU.add, None, R_, R_)
    op_ts(k, "dve", col(1), par[:, 2:3], 1.0 - float(lam_init), None, ALU.mult, None, ["dpar"], R_)
    op_ts(k, "dve", col(2), par[:, 0:1], 32 ** -0.5, None, ALU.mult, None, ["dpar"], R_)
    pi_ = k.sb("dpi", [64, 4], I32)
    p.op("pool", lambda: nc.gpsimd.iota(pi_[:, 0:1], pattern=[[0, 1]], base=0, channel_multiplier=1), (), ["dpi"])
    op_ts(k, "dve", pi_[:, 1:2], pi_[:, 0:1], 7, None, ALU.bitwise_and, None, ["dpi"], ["dpi"])
    op_ts(k, "dve", pi_[:, 2:3], pi_[:, 0:1], 4, 1, ALU.logical_shift_right, ALU.bitwise_and, ["dpi"], ["dpi"])
    op_copy(k, "dve", sc[:, 5:7], pi_[:, 1:3], ["dpi"], R_)
    op_act(k, col(7), col(5), AF.Exp, R_, R_, scale=-LN1E4 / 8.0)
    op_tt(k, "dve", col(9), col(7), col(6), ALU.mult, R_, R_)
    op_tt(k, "dve", col(8), col(7), col(9), ALU.subtract, R_, R_)
    jfi = k.sb("djfi", [64, 256], I32)
    jf = k.sb("djf", [64, 256], F32)
    ang = k.sb("dang", [64, 256], F32)
    ttmp = k.sb("dttmp", [64, 256], F32)
    tti = k.sb("dtti", [64, 256], I32)
    p.op("pool", lambda: nc.gpsimd.iota(jfi[:], pattern=[[1, 256]], base=0, channel_multiplier=0), (), ["djfi"])
    op_copy(k, "dve", jf[:], jfi[:], ["djfi"], ["djf"])
    T = {n: k.sb("dT" + n, [64, 256], F32) for n in ("crk", "srk", "crq", "srq", "cc", "sc")}

    def table(cn, sn, n, scale_col, add_col):
        if add_col is None:
            op_ts(k, "dve", ang[:, :n], jf[:, :n], scale_col, None, ALU.mult, None, ["djf"] + R_, ["dang"])
        else:
            op_ts(k, "dve", ang[:, :n], jf[:, :n], add_col, scale_col, ALU.add, ALU.mult, ["djf", "dpar"] + R_, ["dang"])
        emit_sin(k, T[sn][:, :n], ang[:, :n], 0.0, ttmp[:, :n], tti[:, :n], ["dang"], ["dT" + sn], "dts")
        emit_sin(k, T[cn][:, :n], ang[:, :n], 0.5 * np.pi, ttmp[:, :n], tti[:, :n], ["dang"], ["dT" + cn], "dts")
    table("crk", "srk", 256, col(8), None)
    table("crq", "srq", 128, col(8), par[:, 3:4])
    table("cc", "sc", 64, col(9), None)
    op_ts(k, "dve", T["cc"][:, :64], T["cc"][:, :64], -1.0, None, ALU.add, None, ["dTcc"], ["dTcc"])
    ri = k.sb("dri", [64, 64], I32)
    rf = k.sb("drf", [64, 64], F32)
    mi = k.sb("dmi", [64, 64], I32)
    mf = k.sb("dmf", [64, 64], F32)
    e1 = k.sb("de1", [64, 64], F32)
    RmT = k.sb("dRmT", [64, 64], F32)
    p.op("pool", lambda: nc.gpsimd.iota(ri[:], pattern=[[1, 64]], base=0, channel_multiplier=-1), (), ["dri"])
    p.op("pool", lambda: nc.gpsimd.iota(mi[:], pattern=[[1, 64]], base=0, channel_multiplier=0), (), ["dmi"])
    op_copy(k, "dve", rf[:], ri[:], ["dri"], ["drf"])
    op_ts(k, "dve", mi[:], mi[:], 3, 1, ALU.logical_shift_right, ALU.bitwise_and, ["dmi"], ["dmi"])
    op_copy(k, "dve", mf[:], mi[:], ["dmi"], ["dmf"])
    op_ts(k, "dve", e1[:], rf[:], -8.0, None, ALU.is_equal, None, ["drf"], ["de1"])
    op_ts(k, "dve", RmT[:], rf[:], 8.0, None, ALU.is_equal, None, ["drf"], ["dRmT"])
    op_tt(k, "dve", RmT[:], RmT[:], mf[:], ALU.mult, ["dRmT", "dmf"], ["dRmT"])
    op_ts(k, "dve", mf[:], mf[:], -1.0, 1.0, ALU.mult, ALU.add, ["dmf"], ["dmf"])
    op_tt(k, "dve", e1[:], e1[:], mf[:], ALU.mult, ["de1", "dmf"], ["de1"])
    op_tt(k, "dve", RmT[:], RmT[:], e1[:], ALU.subtract, ["dRmT", "de1"], ["dRmT"])
    blk = k.sb("dblk", [64, 64], F32)
    op_memset(k, "dve", blk[:], 0.0, ["dblk"])
    op_memset(k, "dve", blk[0:32, 0:32], 1.0 / 32, ["dblk"])
    op_memset(k, "dve", blk[32:64, 32:64], 1.0 / 32, ["dblk"])
    o64 = k.sb("do64", [64, 64], F32)
    op_memset(k, "dve", o64[:], 1.0 / 64, ["do64"])
    sel = k.sb("dsel", [65, 64], F32)
    op_memset(k, "dve", sel[:], 0.0, ["dsel"])
    op_memset(k, "dve", sel[64:65, :], 1.0, ["dsel"])
    kh = k.sb("dkh", [64, NSEQ], BF16)
    qh = k.sb("dqh", [64, QH], BF16)
    qch = k.sb("dqch", [64, CTX], BF16)
    vaug = k.sb("dvaug", [128, NKT, 128], BF16)
    op_memset(k, "pool", vaug[:, :, 64:128], 0.0, [("dvaug", "one")])
    op_memset(k, "pool", vaug[:, :, 64:65], 1.0, [("dvaug", "one")])
    vst = [k.sb("dvst%d" % i, [128, 13, 64], F32) for i in range(2)]
    for i in range(10):
        b = i % 2
        p.dma(k.dq(), vst[b][:], vt_d[:, 13 * i:13 * i + 13, :], writes=[("dvst", b)])
        op_copy(k, "pool", vaug[:, 13 * i:13 * i + 13, 0:64], vst[b][:], [("dvst", b)], [("dvaug", i)])
    xs = [k.sb("dxs%d" % i, [64, 512], F32) for i in range(2)]
    W = {n: k.sb("dw" + n, [64, 512], F32) for n in ("sq", "rs", "xn", "cb", "sb", "t1")}
    bankA = k.ps("dbA", [128, 512], F32)
    bankB = k.ps("dbB", [128, 512], F32)
    pss = bankA[0:64, :]
    prx = bankB[0:64, :]
    it = 0

    def prep(src_d, c0, n, gcol, dst, dname, rope, crn, srn, a0):
        nonlocal it
        b = it % 2
        it += 1
        p.dma(k.dq(), xs[b][:, :n], src_d[:, c0:c0 + n], writes=[("dxs", b)])
        op_tt(k, "pool", W["sq"][:, :n], xs[b][:, :n], xs[b][:, :n], ALU.mult, [("dxs", b)], ["dwsq"])
        op_mm(k, pss[:, :n], blk[:], W["sq"][:, :n], True, True, ["dblk", "dwsq"], ["dbA"])
        op_ts(k, "dve", W["rs"][:, :n], pss[:, :n], EPS, None, ALU.add, None, ["dbA"], ["dwrs"])
        op_act(k, W["rs"][:, :n], W["rs"][:, :n], AF.Sqrt, ["dwrs"], ["dwrs"])
        k.p.op("dve", (lambda o=W["rs"][:, :n]: nc.vector.reciprocal(out=o, in_=o)), ["dwrs"], ["dwrs"])
        if not rope:
            op_stt(k, "dve", dst, xs[b][:, :n], gcol, W["rs"][:, :n], ALU.mult, ALU.mult, [("dxs", b), "dwrs", "dpar"] + R_, [dname])
            return
        op_stt(k, "dve", W["xn"][:, :n], xs[b][:, :n], gcol, W["rs"][:, :n], ALU.mult, ALU.mult, [("dxs", b), "dwrs", "dpar"] + R_, ["dwxn"])
        op_mm(k, prx[:, :n], RmT[:], W["xn"][:, :n], True, True, ["dRmT", "dwxn"], ["dbB"])
        na = n // 64
        v3 = lambda ap: ap.rearrange("p (a b) -> p a b", b=64)
        bc_a = lambda t2: bass.AP(t2.tensor, t2[:, a0:a0 + 1].offset, [list(t2.ap[0]), [1, na], [0, 64]])
        bc_b = lambda t2: bass.AP(t2.tensor, t2[:, 0:1].offset, [list(t2.ap[0]), [0, na], [1, 64]])
        op_tt(k, "pool", v3(W["cb"][:, :n]), bc_a(T[crn][:]), bc_b(T["cc"][:]), ALU.add, ["dT" + crn, "dTcc"], ["dwcb"])
        op_tt(k, "pool", v3(W["sb"][:, :n]), bc_a(T[srn][:]), bc_b(T["sc"][:]), ALU.add, ["dT" + srn, "dTsc"], ["dwsb"])
        op_tt(k, "dve", W["cb"][:, :n], W["cb"][:, :n], W["xn"][:, :n], ALU.mult, ["dwcb", "dwxn"], ["dwcb"])
        op_tt(k, "dve", W["sb"][:, :n], W["sb"][:, :n], prx[:, :n], ALU.mult, ["dwsb", "dbB"], ["dwsb"])
        op_tt(k, "dve", dst, W["cb"][:, :n], W["sb"][:, :n], ALU.add, ["dwcb", "dwsb"], [dname])

    prep(k_d, 0, CTX, par[:, 1:2], kh[:, 0:CTX], ("dkh", "c"), False, None, None, 0)
    for i in range(SEQ // 512):
        prep(k_d, CTX + 512 * i, 512, par[:, 1:2], kh[:, CTX + 512 * i:CTX + 512 * (i + 1)], ("dkh", i), True, "crk", "srk", 8 * i)
    prep(qc_d, 0, CTX, col(2), qch[:], "dqch", False, None, None, 0)
    for i in range(nqb):
        prep(q_d, 512 * i, 512, col(2), qh[:, 512 * i:512 * (i + 1)], ("dqh", i), True, "crq", "srq", 8 * i)
    pO = [bankA, bankB]
    pOn = ["dbA", "dbB"]
    Pb = [k.sb("dP%d" % i, [128, 2, 512], BF16) for i in range(3)]
    osb = [k.sb("dosb%d" % c, [65, 512], F32) for c in range(2)]
    F = {n: k.sb("df" + n, [64, 512], F32) for n in ("rd", "o0", "o1", "o", "sq", "rs")}

    def attend(qsrc, qname, n, kts, out_d, oc0):
        nk = len(kts)

        def scores(ki):
            kt = kts[ki]
            b = ki % 3
            kname = ("dkh", "c") if kt < 2 else ("dkh", (kt - 2) // 4)
            for c in range(2):
                rows = slice(32 * c, 32 * c + 32)
                op_mm(k, pS[b][:, c, :n], kh[rows, kt * 128:(kt + 1) * 128], qsrc[rows, :n], True, True, [kname, qname], [("dpS", b)])
            op_act(k, Pb[b][:, :, :n], pS[b][:, :, :n], AF.Exp, [("dpS", b)], [("dP", b)])

        def pv(ki):
            kt = kts[ki]
            b = ki % 3
            for c in range(2):
                op_mm(k, pO[c][:, :n], vaug[:, kt, :], Pb[b][:, c, :n], ki == 0, ki == nk - 1, ["dvaug", ("dP", b)], [pOn[c]])

        DEPTH = 2
        for ki in range(nk + DEPTH):
            if ki < nk:
                scores(ki)
            if ki >= DEPTH:
                pv(ki - DEPTH)
        for c in range(2):
            op_copy(k, "act", osb[c][:, :n], pO[c][0:65, :n], [pOn[c]], [("dosb", c)])
            op_mm(k, pm[0:64, :n], sel[:], osb[c][:, :n], True, True, ["dsel", ("dosb", c)], [("dpS", 0)])
            k.p.op("dve", (lambda o=F["rd"][:, :n], i_=pm[0:64, :n]: nc.vector.reciprocal(out=o, in_=i_)), [("dpS", 0)], ["dfrd"])
            op_tt(k, "dve", F["o%d" % c][:, :n], osb[c][0:64, :n], F["rd"][:, :n], ALU.mult, [("dosb", c), "dfrd"], ["dfo%d" % c])
        op_stt(k, "dve", F["o"][:, :n], F["o1"][:, :n], col(0), F["o0"][:, :n], ALU.mult, ALU.add, ["dfo0", "dfo1"] + R_, ["dfo"])
        op_tt(k, "pool", F["sq"][:, :n], F["o"][:, :n], F["o"][:, :n], ALU.mult, ["dfo"], ["dfsq"])
        op_mm(k, pm[0:64, :n], o64[:], F["sq"][:, :n], True, True, ["do64", "dfsq"], [("dpS", 0)])
        op_ts(k, "dve", F["rs"][:, :n], pm[0:64, :n], EPS, None, ALU.add, None, [("dpS", 0)], ["dfrs"])
        op_act(k, F["rs"][:, :n], F["rs"][:, :n], AF.Sqrt, ["dfrs"], ["dfrs"])
        k.p.op("dve", (lambda o=F["rs"][:, :n]: nc.vector.reciprocal(out=o, in_=o)), ["dfrs"], ["dfrs"])
        op_stt(k, "dve", F["o"][:, :n], F["o"][:, :n], col(1), F["rs"][:, :n], ALU.mult, ALU.mult, ["dfo", "dfrs"] + R_, ["dfo"])
        p.dma("sp", out_d[:, oc0:oc0 + n], F["o"][:, :n], reads=["dfo"], writes=[("dout", id(out_d), oc0)])

    attend(qch, "dqch", CTX, [0, 1], yc_d, 0)
    for i in range(nqb):
        attend(qh[:, 512 * i:512 * (i + 1)], ("dqh", i), 512, list(range(NKT)), y_d, 512 * i)
    return k.finish(["dout"])


def da_inputs(core, zq, zk, zv, zqc, q_norm, k_norm, out_norm, da_lam):
    h, qh = core // 2, core % 2
    rows = slice(64 * h, 64 * h + 64)
    par = np.zeros((64, 4), np.float32)
    par[:, 0] = np.tile(q_norm, 2)
    par[:, 1] = np.tile(k_norm, 2)
    par[:, 2] = out_norm
    par[:, 3] = 128.0 * qh
    vt = np.ascontiguousarray(zv[rows].T.reshape(NKT, 128, 64).transpose(1, 0, 2))
    return dict(dq=np.ascontiguousarray(zq[rows, QH * qh:QH * (qh + 1)]), dk=np.ascontiguousarray(zk[rows]),
                dvt=vt, dqc=np.ascontiguousarray(zqc[rows]), dpar=par, dlam=np.ascontiguousarray(np.asarray(da_lam, np.float32).T))


NL = 1024
C_HL, C_HR, C_CTX0 = NL, NL + 1, NL + 2
NFF = D_FF // 128


class Banks:
    def __init__(self, k, n):
        self.b = [(k.ps("bank%d" % i, [128, 512], F32), "bank%d" % i) for i in range(n)]
        self.i = 0

    def next(self):
        self.i += 1
        return self.b[self.i % len(self.b)]


class WStream:
    def __init__(self, k, tag, nk, mw, nbuf=2):
        self.k, self.tag, self.nbuf = k, tag, nbuf
        self.st = [k.sb(tag + "s%d" % i, [128, nk, mw], F32) for i in range(nbuf)]
        self.bf = [k.sb(tag + "b%d" % i, [128, nk, mw], BF16) for i in range(nbuf)]
        self.i = 0

    def load(self, view, nk=None, mw=None):
        k = self.k
        b = self.i % self.nbuf
        self.i += 1
        st, bf = self.st[b], self.bf[b]
        nk = nk or st.shape[1]
        mw = mw or st.shape[2]
        k.p.dma(k.dq(), st[:, :nk, :mw], view, writes=[(self.tag + "s", b)])
        op_copy(k, "pool", bf[:, :nk, :mw], st[:, :nk, :mw], [(self.tag + "s", b)], [(self.tag + "b", b)])
        return bf, (self.tag + "b", b)


def build_phaseC(with_ctx):
    k = K()
    nc, p = k.nc, k.p
    NTS = NL + 2
    nsec = 3 if with_ctx else 2
    x_d = k.din("xT", [nsec, D, NTS])
    cvec_d = k.din("cvec", [128, 8, 2])
    wada_d = k.din("wada", [D, 6 * D])
    bada_d = k.din("bada", [128, 48])
    g1_d = k.din("g1", [128, 8])
    g2_d = k.din("g2", [128, 8])
    wg_d = k.din("wg", [D, 4 * D])
    u_d = k.din("bu", [nsec, 256, NTS])
    ys_d = k.din("bys", [nsec, 256, NTS])
    yb_d = k.din("byb", [nsec, 256, NTS])
    yc_d = k.din("byc", [nsec, 256, NTS])
    go_d = k.din("bgo", [nsec, 256, NTS])
    gg_d = k.din("bgg", [nsec, 256, NTS])
    sp_d = k.din("spar", [128, 8])
    wglu_d = k.din("wglu", [256, 256])
    wbr_d = k.din("wbr", [4, 256, D])
    wo_d = k.din("wo", [D, D])
    wup_d = k.din("wup", [D, 2 * D_FF])
    fcw_d = k.din("fcw", [128, NFF, 4])
    wdn_d = k.din("wdn", [D_FF, D])
    xo_d = k.dout("xo", [2, D, NL])
    co_d = k.dout("co", [D, CTX]) if with_ctx else None

    emit_consts(k)
    k.modname = "Cmod"
    mod = emit_adaln(k, wada_d, bada_d, cvec_d, [0, 1, 2, 3, 4, 5], "C")
    banks = Banks(k, 7)
    spar = k.sb("spar", [128, 8], F32)
    fcw = k.sb("fcw", [128, NFF, 4], F32)
    p.dma("sp", spar[:], sp_d, writes=["spar"])
    p.dma("sp", fcw[:], fcw_d, writes=["fcw"])
    blk64 = k.sb("blk64", [128, 128], F32)
    op_memset(k, "dve", blk64[:], 0.0, ["blk64"])
    op_memset(k, "dve", blk64[0:64, 0:64], 1.0 / 64, ["blk64"])
    op_memset(k, "dve", blk64[64:128, 64:128], 1.0 / 64, ["blk64"])
    wglu = WStream(k, "wglu", 2, 256, nbuf=1)
    wglu_bf, wglu_n = wglu.load(wglu_d.rearrange("(k p) m -> p k m", p=128))

    xT = k.sb("xT_sb", [128, 8, NTS], F32)
    hnT = k.sb("hnT", [128, 8, NTS], BF16)
    yT = k.sb("yT", [128, 8, NTS], BF16)
    mg = k.sb("mg", [128, 8, NTS], BF16)
    ws_g = WStream(k, "wsg", 8, 128, nbuf=4)
    ws_b = WStream(k, "wsb", 2, 128, nbuf=3)
    ws_o = ws_g
    ws_u = ws_g
    ws_d = WStream(k, "wsd", 1, 1024, nbuf=4)
    stg = [k.sb("stg%d" % i, [128, NTS], F32) for i in range(3)]
    stg_i = [0]

    def stage():
        stg_i[0] += 1
        b = stg_i[0] % 3
        return stg[b], ("stg", b)

    zf = k.sb("zf", [128, 2, NTS], F32)
    zb = k.sb("zb", [128, 2, NTS], BF16)
    acc = k.sb("acc", [128, NTS], F32)
    sig = [k.sb("sig%d" % i, [128, 512], F32) for i in range(2)]
    tmpm = [k.sb("tmpm%d" % i, [128, 512], F32) for i in range(2)]
    gts = k.sb("gts", [128, NL + 2], F32)
    cvt = acc
    asb = stg[0]
    hT = [k.sb("hT%d" % i, [128, NL], BF16) for i in range(4)]

    for sec in range(nsec):
        sfx = "s%d" % sec
        isctx = sec == 2
        if isctx:
            blocks = [(0, CTX, 1)]
            oblocks = [(0, CTX, 1, 0)]
            op_memset(k, "dve", gts[:, 0:1], 0.0, ["gts"])
            op_memset(k, "dve", gts[:, CTX + 1:CTX + 2], 0.0, ["gts"])
        else:
            blocks = [(0, 512, 0), (512, 1024, 0), (NL, NL + 2, 0)]
            oblocks = [(0, 512, 0, 0), (512, 1024, 0, 512)]
        xv = x_d[sec].rearrange("(k p) t -> p k t", p=128)
        for kk in range(8):
            p.dma(k.dq(), xT[:, kk, :], xv[:, kk, :], writes=[("xT", kk)])
        emit_norm_mod(k, xT, NTS, blocks, g1_d, mod, 0, 1, hnT, "n1" + sfx, "xT", banks.b)
        hn_name = "n1" + sfx + "hnT"
        for j in range(2):
            su, sun = stage()
            sy, syn = stage()
            p.dma(k.dq(), su[:], u_d[sec, 128 * j:128 * (j + 1), :], writes=[sun])
            p.dma(k.dq(), sy[:], ys_d[sec, 128 * j:128 * (j + 1), :], writes=[syn])
            op_stt(k, "dve", sy[:], su[:], spar[:, j:j + 1], sy[:], ALU.mult, ALU.add, [sun, syn, "spar"], [syn])
            op_act(k, zf[:, j, :], sy[:], AF.Gelu_apprx_tanh, [syn], [("zf", j)])
            op_copy(k, "pool", zb[:, j, :], zf[:, j, :], [("zf", j)], [("zb", j)])
        for j in range(2):
            for (c0, c1, v) in blocks:
                w = c1 - c0
                pt, pn = banks.next()
                for kk in range(2):
                    op_mm(k, pt[:, :w], wglu_bf[:, kk, 128 * j:128 * (j + 1)], zb[:, kk, c0:c1], kk == 0, kk == 1, [wglu_n, ("zb", kk)], [pn])
                b = c0 // 512 % 2
                op_act(k, sig[b][:, :w], pt[:, :w], AF.Sigmoid, [pn], [("sig", b)])
                op_tt(k, "dve", yT[:, j, c0:c1], sig[b][:, :w], zf[:, j, c0:c1], ALU.mult, [("sig", b), ("zf", j)], [("yT", j, c0)])
        for bi_, src in ((1, yb_d), (2, yc_d)):
            for j in range(2):
                s_, sn_ = stage()
                p.dma(k.dq(), s_[:], src[sec, 128 * j:128 * (j + 1), :], writes=[sn_])
                op_copy(k, "pool", yT[:, 2 * bi_ + j, :], s_[:], [sn_], [("yT", 2 * bi_ + j)])
        for j in range(2):
            so, son = stage()
            sg, sgn = stage()
            p.dma(k.dq(), so[:], go_d[sec, 128 * j:128 * (j + 1), :], writes=[son])
            p.dma(k.dq(), sg[:], gg_d[sec, 128 * j:128 * (j + 1), :], writes=[sgn])
            op_act(k, sg[:], sg[:], AF.Silu, [sgn], [sgn])
            for (c0, c1, v) in blocks:
                w = c1 - c0
                b = c0 // 512 % 2
                op_tt(k, "pool", tmpm[b][:, :w], so[:, c0:c1], so[:, c0:c1], ALU.mult, [son], [("tmpm", b)])
                pt, pn = banks.next()
                op_mm(k, pt[:, :w], blk64[:], tmpm[b][:, :w], True, True, ["blk64", ("tmpm", b)], [pn])
                op_ts(k, "dve", sig[b][:, :w], pt[:, :w], EPS, None, ALU.add, None, [pn], [("sig", b)])
                op_act(k, sig[b][:, :w], sig[b][:, :w], AF.Sqrt, [("sig", b)], [("sig", b)])
                k.p.op("dve", (lambda o=sig[b][:, :w]: nc.vector.reciprocal(out=o, in_=o)), [("sig", b)], [("sig", b)])
                op_stt(k, "dve", tmpm[b][:, :w], so[:, c0:c1], spar[:, 2 + j:3 + j], sig[b][:, :w], ALU.mult, ALU.mult, [son, "spar", ("sig", b), ("tmpm", b)], [("tmpm", b)])
                op_tt(k, "dve", yT[:, 6 + j, c0:c1], tmpm[b][:, :w], sg[:, c0:c1], ALU.mult, [("tmpm", b), sgn], [("yT", 6 + j, c0)])
        wgv = wg_d.rearrange("(k p) m -> p k m", p=128)
        for m in range(8):
            for n in range(4):
                wgb, wgn = ws_g.load(wgv[:, :, n * D + m * 128:n * D + (m + 1) * 128])
                wbb, wbn = ws_b.load(wbr_d[n].rearrange("(k p) m -> p k m", p=128)[:, :, m * 128:(m + 1) * 128])
                for (c0, c1, v) in blocks:
                    w = c1 - c0
                    b = (c0 // 512 + n) % 2
                    ptg, png = banks.next()
                    for kk in range(8):
                        op_mm(k, ptg[:, :w], wgb[:, kk, :], hnT[:, kk, c0:c1], kk == 0, kk == 7, [wgn, (hn_name, kk)], [png])
                    ptp, pnp = banks.next()
                    for kk in range(2):
                        op_mm(k, ptp[:, :w], wbb[:, kk, :], yT[:, 2 * n + kk, c0:c1], kk == 0, kk == 1, [wbn, ("yT", 2 * n + kk)], [pnp])
                    op_act(k, sig[b][:, :w], ptg[:, :w], AF.Sigmoid, [png], [("sig", b)])
                    if n == 0:
                        op_tt(k, "dve", acc[:, c0:c1], sig[b][:, :w], ptp[:, :w], ALU.mult, [("sig", b), pnp], [("acc", c0)])
                    else:
                        op_tt(k, "dve", tmpm[b][:, :w], sig[b][:, :w], ptp[:, :w], ALU.mult, [("sig", b), pnp], [("tmpm", b)])
                        dst = mg[:, m, c0:c1] if n == 3 else acc[:, c0:c1]
                        dn = ("mg", m, c0) if n == 3 else ("acc", c0)
                        op_tt(k, "dve", dst, acc[:, c0:c1], tmpm[b][:, :w], ALU.add, [("acc", c0), ("tmpm", b)], [dn])
        wov = wo_d.rearrange("(k p) m -> p k m", p=128)
        for m in range(8):
            wob, won = ws_o.load(wov[:, :, m * 128:(m + 1) * 128])
            for (c0, c1, v) in blocks:
                w = c1 - c0
                pt, pn = banks.next()
                for kk in range(8):
                    op_mm(k, pt[:, :w], wob[:, kk, :], mg[:, kk, c0:c1], kk == 0, kk == 7, [won, ("mg", kk)], [pn])
                op_stt(k, "dve", xT[:, m, c0:c1], pt[:, :w], mod[:, 16 + m, v:v + 1], xT[:, m, c0:c1], ALU.mult, ALU.add,
                       [pn, "Cmod", ("xT", m)], [("xT", m)])
        emit_norm_mod(k, xT, NTS, blocks, g2_d, mod, 3, 4, hnT, "n2" + sfx, "xT", banks.b)
        hn2 = "n2" + sfx + "hnT"
        wuv = wup_d.rearrange("(k p) m -> p k m", p=128)
        for g in range(NFF // 2):
          wds = []
          for f in (2 * g, 2 * g + 1):
            wab, wan = ws_u.load(wuv[:, :, f * 128:(f + 1) * 128])
            wtb, wtn = ws_u.load(wuv[:, :, D_FF + f * 128:D_FF + (f + 1) * 128])
            wdb, wdn_ = ws_d.load(wdn_d[f * 128:(f + 1) * 128, :].rearrange("p (o m) -> p o m", o=1))
            hb = f % 4
            wds.append((wdb, wdn_))
            for (c0, c1, v) in blocks:
                w = c1 - c0
                pta, pna = banks.next()
                for kk in range(8):
                    op_mm(k, pta[:, :w], wab[:, kk, :], hnT[:, kk, c0:c1], kk == 0, kk == 7, [wan, (hn2, kk)], [pna])
                op_copy(k, "act", asb[:, c0:c1], pta[:, :w], [pna], [("stg", 0, c0)])
                ptt, pnt = banks.next()
                for kk in range(8):
                    op_mm(k, ptt[:, :w], wtb[:, kk, :], hnT[:, kk, c0:c1], kk == 0, kk == 7, [wtn, (hn2, kk)], [pnt])
                if c0 < NL:
                    op_copy(k, "act", gts[:, 1 + c0:1 + c1], ptt[:, :w], [pnt], [("gts", "l", c0)])
                else:
                    op_tt(k, "dve", gts[:, 0:1], ptt[:, 0:1], spar[:, 4 + 2 * sec:5 + 2 * sec], ALU.mult, [pnt, "spar"], [("gts", "hl")])
                    op_tt(k, "dve", gts[:, NL + 1:NL + 2], ptt[:, 1:2], spar[:, 5 + 2 * sec:6 + 2 * sec], ALU.mult, [pnt, "spar"], [("gts", "hr")])
            segs = [(0, CTX if isctx else NL, 0, 0)]
            for (g0, n, oc, ac) in segs:
                op_ts(k, "dve", cvt[:, oc:oc + n], gts[:, g0:g0 + n], fcw[:, f, 0:1], fcw[:, f, 3:4], ALU.mult, ALU.add, ["gts", "fcw"], [("acc", "cv", oc)])
                op_stt(k, "dve", cvt[:, oc:oc + n], gts[:, g0 + 1:g0 + 1 + n], fcw[:, f, 1:2], cvt[:, oc:oc + n], ALU.mult, ALU.add, ["gts", "fcw", ("acc", "cv", oc)], [("acc", "cv", oc)])
                op_stt(k, "dve", cvt[:, oc:oc + n], gts[:, g0 + 2:g0 + 2 + n], fcw[:, f, 2:3], cvt[:, oc:oc + n], ALU.mult, ALU.add, ["gts", "fcw", ("acc", "cv", oc)], [("acc", "cv", oc)])
                op_act(k, cvt[:, oc:oc + n], cvt[:, oc:oc + n], AF.Gelu_apprx_tanh, [("acc", "cv", oc)], [("acc", "cv", oc)])
                op_tt(k, "dve", hT[hb][:, oc:oc + n], cvt[:, oc:oc + n], asb[:, ac:ac + n], ALU.mult, [("acc", "cv", oc), ("stg", 0)], [("hT", hb, oc)])
          for m in range(8):
              for (c0, c1, v, xc0) in oblocks:
                  w = c1 - c0
                  pt, pn = banks.next()
                  for fi, f in enumerate((2 * g, 2 * g + 1)):
                      wdb, wdn_ = wds[fi]
                      op_mm(k, pt[:, :w], wdb[:, 0, m * 128:(m + 1) * 128], hT[f % 4][:, c0:c1], fi == 0, fi == 1, [wdn_, ("hT", f % 4)], [pn])
                  if (m + c0 // 512) % 2 == 1:
                      op_act(k, tmpm[m % 2][:, :w], pt[:, :w], AF.Identity, [pn, "Cmod"], [("tmpm", m % 2)], scale=mod[:, 40 + m, v:v + 1])
                      op_tt(k, "pool", xT[:, m, xc0:xc0 + w], xT[:, m, xc0:xc0 + w], tmpm[m % 2][:, :w], ALU.add,
                            [("tmpm", m % 2), ("xT", m)], [("xT", m)])
                  else:
                      op_stt(k, "dve", xT[:, m, xc0:xc0 + w], pt[:, :w], mod[:, 40 + m, v:v + 1], xT[:, m, xc0:xc0 + w], ALU.mult, ALU.add,
                             [pn, "Cmod", ("xT", m)], [("xT", m)])
        if not isctx:
            xov = xo_d[sec].rearrange("(k p) t -> p k t", p=128)
            for kk in range(8):
                p.dma(k.dq(), xov[:, kk, :], xT[:, kk, 0:NL], reads=[("xT", kk)], writes=[("xout", sec, kk)])
        else:
            cov = co_d.rearrange("(k p) t -> p k t", p=128)
            for kk in range(8):
                p.dma(k.dq(), cov[:, kk, :], xT[:, kk, 0:CTX], reads=[("xT", kk)], writes=[("cout", kk)])
    return k.finish(["xout", "cout"] if with_ctx else ["xout"])


def _sec_cols(arrT, core, sec, off):
    s0 = TL * core + NL * sec
    out = np.zeros((arrT.shape[0], NL + 2), np.float32)
    out[:, :NL] = arrT[:, off + s0:off + s0 + NL]
    if s0 > 0:
        out[:, NL] = arrT[:, off + s0 - 1]
    if s0 + NL < SEQ:
        out[:, NL + 1] = arrT[:, off + s0 + NL]
    return out


def _ctx_cols(arrT):
    out = np.zeros((arrT.shape[0], NL + 2), np.float32)
    out[:, :CTX] = arrT[:, :CTX]
    return out


def phaseC_inputs(core, with_ctx, xT_all, cT_all, br, c, c_ctx, w_ada_l, b_ada_l, g1_l, g2_l, w_in_l, ssm_d, w_glu, gla_g,
                  w_branch, w_out, w_up, fcw, fcb, w_down):
    nsec = 3 if with_ctx else 2
    def sec_stack(aT, off, ctxT):
        parts = [_sec_cols(aT, core, s, off) for s in range(2)]
        if with_ctx:
            parts.append(_ctx_cols(ctxT))
        return np.ascontiguousarray(np.stack(parts))
    d = dict(xT=sec_stack(xT_all, 0, cT_all))
    for nm, key in (("bu", "u"), ("bys", "ys"), ("byb", "yb"), ("byc", "yc"), ("bgo", "go"), ("bgg", "gg")):
        d[nm] = sec_stack(br[key], CTX, br[key])
    spar = np.zeros((128, 8), np.float32)
    spar[:, 0:2] = np.asarray(ssm_d, np.float32).reshape(2, 128).T
    spar[:, 2:4] = np.tile(np.asarray(gla_g, np.float32), 2)[:, None]
    for s in range(2):
        s0 = TL * core + NL * s
        spar[:, 4 + 2 * s] = 1.0 if s0 > 0 else 0.0
        spar[:, 5 + 2 * s] = 1.0 if s0 + NL < SEQ else 0.0
    fc = np.zeros((128, NFF, 4), np.float32)
    fc[:, :, 0:3] = np.asarray(fcw, np.float32).T.reshape(NFF, 128, 3).transpose(1, 0, 2)
    fc[:, :, 3] = np.asarray(fcb, np.float32).reshape(NFF, 128).T
    cvec = np.ascontiguousarray(np.stack([_ft(np.asarray(c).reshape(-1)), _ft(np.asarray(c_ctx).reshape(-1))], axis=-1))
    d.update(cvec=cvec, wada=np.ascontiguousarray(w_ada_l), bada=_ft(b_ada_l), g1=_ft(g1_l), g2=_ft(g2_l),
             wg=np.ascontiguousarray(w_in_l[:, IN_MIX:]), spar=spar, wglu=np.ascontiguousarray(w_glu),
             wbr=np.ascontiguousarray(w_branch), wo=np.ascontiguousarray(w_out), wup=np.ascontiguousarray(w_up),
             fcw=fc, wdn=np.ascontiguousarray(w_down))
    return d


_PIECES = (('ssm_u', 256), ('lru_x', 256), ('lru_y', 256), ('da_q', 256), ('da_k', 256), ('da_v', 256),
           ('gla_q', 128), ('gla_k', 128), ('gla_v', 256), ('gla_g', 256), ('gla_a', 32))


def _spmd(nc, in_maps):
    res = run_bass_kernel_spmd(nc, in_maps, core_ids=list(range(NCORE)))
    return res.results


def kernel(x, c, ctx, c_ctx, w_ada, b_ada, norm1_g, norm2_g, w_in,
           ssm_lam_re, ssm_lam_im, ssm_log_step, ssm_b_re, ssm_b_im, ssm_c_re, ssm_c_im, ssm_d, ssm_w_glu,
           lru_conv_w, lru_conv_b, lru_wr, lru_br, lru_wi, lru_bi, lru_lam,
           da_q_norm, da_k_norm, da_lam, da_out_norm,
           gla_wa2, gla_ba, gla_out_norm,
           w_branch, w_out, w_up, ffn_conv_w, ffn_conv_b, w_down):
    A = lambda a: np.asarray(a, dtype=np.float32)
    xT = np.ascontiguousarray(A(x)[0].T)
    cT = np.ascontiguousarray(A(ctx)[0].T)
    depth = A(w_in).shape[0]
    for l in range(depth):
        with_ctx = l < depth - 1
        zs = run_phaseA(xT, cT, A(c), A(c_ctx), A(w_ada)[l], A(b_ada)[l], A(norm1_g)[l], A(w_in)[l])
        z_all = np.concatenate([zs[0][:, TL:]] + [zs[i][:, :TL] for i in range(NCORE)], axis=1)
        z = {}
        o = 0
        for nm, sz in _PIECES:
            z[nm] = z_all[o:o + sz]
            o += sz
        r = _spmd(build_s5(), [s5_inputs(i, z['ssm_u'], A(ssm_lam_re)[l], A(ssm_lam_im)[l], A(ssm_log_step)[l],
                                         A(ssm_b_re)[l], A(ssm_b_im)[l], A(ssm_c_re)[l], A(ssm_c_im)[l]) for i in range(NCORE)])
        ys = np.concatenate([q["ys"] for q in r], axis=0)
        r = _spmd(build_lru(), [lru_inputs(i, z['lru_x'], z['lru_y'], A(lru_conv_w)[l], A(lru_conv_b)[l], A(lru_wr)[l],
                                           A(lru_br)[l], A(lru_wi)[l], A(lru_bi)[l], A(lru_lam)[l]) for i in range(NCORE)])
        yb = np.concatenate([q["lo"] for q in r], axis=0)
        r = _spmd(build_gla(), [gla_inputs(i, z['gla_q'], z['gla_k'], z['gla_v'], z['gla_a'], A(gla_wa2)[l], A(gla_ba)[l])
                                for i in range(NCORE)])
        go = np.concatenate([q["go"] for q in r], axis=0)
        lam_init = 0.8 - 0.6 * float(np.exp(-0.3 * l))
        zq_lat = np.ascontiguousarray(z['da_q'][:, CTX:])
        zq_ctx = np.ascontiguousarray(z['da_q'][:, :CTX])
        r = _spmd(build_da(lam_init), [da_inputs(i, zq_lat, z['da_k'], z['da_v'], zq_ctx, A(da_q_norm)[l], A(da_k_norm)[l],
                                                 A(da_out_norm)[l], A(da_lam)[l]) for i in range(NCORE)])
        yc = np.zeros((256, NSEQ), np.float32)
        for i in range(NCORE):
            h, qh = i // 2, i % 2
            yc[64 * h:64 * h + 64, CTX + QH * qh:CTX + QH * (qh + 1)] = r[i]["dy"]
            if qh == 0:
                yc[64 * h:64 * h + 64, :CTX] = r[i]["dyc"]
        br = dict(u=z['ssm_u'], ys=ys, yb=yb, yc=yc, go=go, gg=z['gla_g'])
        r = _spmd(build_phaseC(with_ctx), [phaseC_inputs(i, with_ctx, xT, cT, br, A(c), A(c_ctx), A(w_ada)[l], A(b_ada)[l],
                                                         A(norm1_g)[l], A(norm2_g)[l], A(w_in)[l], A(ssm_d)[l], A(ssm_w_glu)[l],
                                                         A(gla_out_norm)[l], A(w_branch)[l], A(w_out)[l], A(w_up)[l],
                                                         A(ffn_conv_w)[l], A(ffn_conv_b)[l], A(w_down)[l]) for i in range(NCORE)])
        xT = np.concatenate([np.concatenate([q["xo"][0], q["xo"][1]], axis=1) for q in r], axis=1)
        if with_ctx:
            cT = np.ascontiguousarray(r[0]["co"])
    return np.ascontiguousarray(xT.T)[None].astype(np.float32)
```

```python
import numpy as np
from contextlib import ExitStack
import concourse.bass as bass
import concourse.mybir as mybir
from concourse.bass_utils import run_bass_kernel_spmd

F32 = mybir.dt.float32
BF16 = mybir.dt.bfloat16
I32 = mybir.dt.int32
ALU = mybir.AluOpType
AF = mybir.ActivationFunctionType

D = 1024
SEQ = 16384
NCORE = 8
TL = SEQ // NCORE
CTX = 256
EPS = 1e-6
IN_MIX = 2336
D_FF = 2816


class Prog:
    SEM_MAX = 30000
    DMA_POOL = 16

    def __init__(self, nc):
        self.nc = nc
        self.eng = {"pe": nc.tensor, "act": nc.scalar, "dve": nc.vector,
                    "pool": nc.gpsimd, "sp": nc.sync}
        self.ops = []

    def op(self, eng, fn, reads=(), writes=(), dma=False):
        norm = lambda bs: tuple(b if isinstance(b, tuple) else (b,) for b in bs)
        self.ops.append(dict(eng=eng, fn=fn, reads=norm(reads), writes=norm(writes), dma=dma))

    def dma(self, eng, out, in_, reads=(), writes=(), **kw):
        e = self.eng[eng]
        self.op(eng, lambda: e.dma_start(out=out, in_=in_, **kw), reads, writes, dma=True)

    def emit(self, stack):
        nc = self.nc
        ops = self.ops
        n = len(ops)
        last_w, readers, desc = {}, {}, {}

        def related(p):
            out = [p[:i] for i in range(1, len(p) + 1)]
            out.extend(desc.get(p, ()))
            return out

        def register(p):
            for i in range(1, len(p)):
                desc.setdefault(p[:i], set()).add(p)

        deps = [set() for _ in range(n)]
        for i, o in enumerate(ops):
            for b in o["reads"]:
                for q in related(b):
                    if q in last_w:
                        deps[i].add(last_w[q])
            for b in o["writes"]:
                for q in related(b):
                    if q in last_w:
                        deps[i].add(last_w[q])
                    for r in readers.get(q, ()):
                        if r != i:
                            deps[i].add(r)
            for b in o["reads"]:
                register(b)
                readers.setdefault(b, []).append(i)
            for b in o["writes"]:
                register(b)
                for q in list(desc.get(b, ())):
                    last_w.pop(q, None)
                    readers.pop(q, None)
                last_w[b] = i
                readers[b] = []
            deps[i].discard(i)

        def stream(o):
            return ("d:" if o["dma"] else "c:") + o["eng"]

        needed = [False] * n
        for i, o in enumerate(ops):
            si = stream(o)
            keep = {}
            for d in deps[i]:
                sd = stream(ops[d])
                if sd == si and sd == "c:pe":
                    continue
                if sd.startswith("d:"):
                    keep[(sd, d)] = d
                elif sd not in keep or d > keep[sd]:
                    keep[sd] = d
            deps[i] = keep
            for d in keep.values():
                needed[d] = True
        P = self.DMA_POOL
        cnt = {}
        semval = [None] * n
        dma_prev = [None] * n
        for i, o in enumerate(ops):
            if o["fn"] is None:
                continue
            s = stream(o)
            if o["dma"]:
                c = cnt.get(s, 0)
                cnt[s] = c + 1
                slot, m = c % P, c // P + 1
                semval[i] = ((s, slot), 16 * m)
                if m > 1:
                    dma_prev[i] = ((s, slot), 16 * (m - 1))
                continue
            if not needed[i]:
                continue
            c = cnt.get(s, 0) + 1
            cnt[s] = c
            semval[i] = ((s, "e%d" % ((c - 1) // self.SEM_MAX)), (c - 1) % self.SEM_MAX + 1)
        sems = {}

        def get_sem(k):
            if k not in sems:
                sems[k] = stack.enter_context(nc.semaphore(("s_%s_%s" % k).replace(":", "_")))
            return sems[k]

        waited = {}

        def do_wait(engname, k, v):
            kk = (engname, k)
            if waited.get(kk, -1) >= v:
                return
            waited[kk] = v
            self.eng[engname].wait_ge(get_sem(k), v)

        for i, o in enumerate(ops):
            for sd, d in sorted(deps[i].items(), key=lambda t: t[1]):
                k, v = semval[d]
                do_wait(o["eng"], k, v)
            if dma_prev[i] is not None:
                do_wait(o["eng"], *dma_prev[i])
            if o["fn"] is None:
                continue
            ins = o["fn"]()
            if semval[i] is not None:
                k, v = semval[i]
                ins.then_inc(get_sem(k), 16 if o["dma"] else 1)
        self.ops = []
        return cnt


class K:
    def __init__(self):
        self.nc = bass.Bass("TRN2", target_bir_lowering=False)
        self.p = Prog(self.nc)
        self.st = ExitStack()
        self._dq = 0
        self.psn = 0

    def din(self, name, shape, dt=F32):
        return self.nc.dram_tensor(name, list(shape), dt, kind="ExternalInput").ap()

    def dout(self, name, shape, dt=F32):
        return self.nc.dram_tensor(name, list(shape), dt, kind="ExternalOutput").ap()

    def sb(self, name, shape, dt=F32):
        return self.st.enter_context(self.nc.sbuf_tensor("sb_" + name, list(shape), dt))

    def ps(self, name, shape, dt=F32):
        return self.st.enter_context(self.nc.psum_tensor("ps_" + name, list(shape), dt))

    def dq(self):
        if getattr(self, "alt_dq", False):
            self._dq += 1
            return ("sp", "pool")[self._dq % 2]
        return "sp"

    def finish(self, out_bufs):
        self.p.op("sp", None, reads=out_bufs)
        self.p.emit(self.st)
        self.st.close()
        return self.nc


def rev_ap(ap2d):
    n = ap2d.shape[-1]
    last = ap2d[:, n - 1:n]
    return bass.AP(ap2d.tensor, last.offset, [list(ap2d.ap[0]), [-ap2d.ap[-1][0], n]])


def emit_consts(k):
    nc, p = k.nc, k.p
    ones = k.sb("ones", [128, 128], F32)
    p.op("dve", lambda: nc.vector.memset(ones[:], 1.0), writes=["ones"])
    k.ones = ones


def emit_adaln(k, wada, bada_t, cvec, vec_ids, tag):
    nc, p = k.nc, k.p
    nv = len(vec_ids)
    scv = k.sb(tag + "scv", [128, 8, 2], F32)
    bsb = k.sb(tag + "bsb", [128, 48], F32)
    mod = k.sb(tag + "mod", [128, nv * 8, 2], F32)
    psa = k.ps(tag + "psa", [128, nv * 8, 2], F32)
    p.dma("sp", scv[:], cvec, writes=[tag + "scv"])
    p.dma("sp", bsb[:], bada_t, writes=[tag + "bsb"])
    p.op("act", lambda: nc.scalar.activation(out=scv[:], in_=scv[:], func=AF.Silu),
         reads=[tag + "scv"], writes=[tag + "scv"])
    wv = wada.rearrange("(k p) m -> p k m", p=128)
    wst = [k.sb(tag + "wst%d" % i, [128, 8, 128], F32) for i in range(2)]
    it = 0
    for vi, v in enumerate(vec_ids):
        for jj in range(8):
            b = it % 2
            it += 1
            c0 = v * D + jj * 128
            j = vi * 8 + jj
            p.dma(k.dq(), wst[b][:], wv[:, :, c0:c0 + 128], writes=[(tag + "wst", b)])
            for kk in range(8):
                p.op("pe", (lambda b=b, kk=kk, j=j: nc.tensor.matmul(
                    psa[:, j, :], lhsT=wst[b][:, kk, :], rhs=scv[:, kk, :],
                    start=(kk == 0), stop=(kk == 7))),
                    reads=[(tag + "wst", b), tag + "scv"], writes=[tag + "psa"])
    v0 = vec_ids[0]
    for c in range(2):
        p.op("dve", (lambda c=c: nc.vector.tensor_tensor(
            out=mod[:, :, c], in0=psa[:, :, c], in1=bsb[:, v0 * 8:(v0 + nv) * 8], op=ALU.add)),
            reads=[tag + "psa", tag + "bsb"], writes=[(tag + "mod", "c%d" % c)])
    return mod


def emit_norm_mod(k, xT, ntok, blocks, gT, mod, sh_i, sc_i, hnT, tag, xname, psb):
    nc, p = k.nc, k.p
    gsb = k.sb(tag + "g", [128, 8], F32)
    A = k.sb(tag + "A", [128, 8, 2], F32)
    p.dma("sp", gsb[:], gT, writes=[tag + "g"])
    for v in range(2):
        p.op("dve", (lambda v=v: nc.vector.scalar_tensor_tensor(
            out=A[:, :, v], in0=mod[:, sc_i * 8:(sc_i + 1) * 8, v], scalar=1.0, in1=gsb[:],
            op0=ALU.add, op1=ALU.mult)),
            reads=[tag + "g", k.modname],
            writes=[(tag + "A", v)])
    if not hasattr(k, "_nm"):
        k._nm = (k.sb("nm_rstd", [128, ntok], F32),
                 [k.sb("nm_sq%d" % i, [128, 512], F32) for i in range(2)],
                 [k.sb("nm_tmp%d" % i, [128, 512], F32) for i in range(2)])
    rstd, sq, tmp = k._nm
    it = 0
    for bi, (c0, c1, v) in enumerate(blocks):
        w = c1 - c0
        pt, pn = psb[bi % len(psb)]
        for kk in range(8):
            b = it % 2
            it += 1
            p.op("act", (lambda b=b, kk=kk, c0=c0, c1=c1, w=w: nc.scalar.activation(
                out=sq[b][:, :w], in_=xT[:, kk, c0:c1], func=AF.Square)),
                reads=[(xname, kk)], writes=[("nm_sq", b)])
            p.op("pe", (lambda b=b, kk=kk, w=w, pt=pt: nc.tensor.matmul(
                pt[:, :w], lhsT=k.ones[:], rhs=sq[b][:, :w], start=(kk == 0), stop=(kk == 7))),
                reads=["ones", ("nm_sq", b)], writes=[pn])
        p.op("dve", (lambda c0=c0, c1=c1, w=w, pt=pt: nc.vector.tensor_scalar(
            out=rstd[:, c0:c1], in0=pt[:, :w], scalar1=1.0 / D, scalar2=EPS, op0=ALU.mult, op1=ALU.add)),
            reads=[pn], writes=[("nm_rstd", bi)])
        p.op("act", (lambda c0=c0, c1=c1: nc.scalar.activation(
            out=rstd[:, c0:c1], in_=rstd[:, c0:c1], func=AF.Sqrt)),
            reads=[("nm_rstd", bi)], writes=[("nm_rstd", bi)])
        p.op("dve", (lambda c0=c0, c1=c1: nc.vector.reciprocal(
            out=rstd[:, c0:c1], in_=rstd[:, c0:c1])),
            reads=[("nm_rstd", bi)], writes=[("nm_rstd", bi)])
        for kk in range(8):
            b = it % 2
            it += 1
            p.op("dve", (lambda b=b, kk=kk, c0=c0, c1=c1, w=w: nc.vector.tensor_tensor(
                out=tmp[b][:, :w], in0=xT[:, kk, c0:c1], in1=rstd[:, c0:c1], op=ALU.mult)),
                reads=[(xname, kk), ("nm_rstd", bi)], writes=[("nm_tmp", b)])
            p.op("act", (lambda b=b, kk=kk, c0=c0, c1=c1, w=w, v=v: nc.scalar.activation(
                out=hnT[:, kk, c0:c1], in_=tmp[b][:, :w], func=AF.Identity,
                scale=A[:, kk, v:v + 1], bias=mod[:, sh_i * 8 + kk, v:v + 1])),
                reads=[("nm_tmp", b), (tag + "A", v), k.modname],
                writes=[(tag + "hnT", kk, bi)])
    return rstd


def build_phaseA():
    k = K()
    k.alt_dq = True
    nc, p = k.nc, k.p
    NT = TL + CTX
    xT_d = k.din("xT", [D, TL])
    cT_d = k.din("cT", [D, CTX])
    cvec_d = k.din("cvec", [128, 8, 2])
    wada_d = k.din("wada", [D, 6 * D])
    bada_d = k.din("bada", [128, 48])
    g_d = k.din("g1", [128, 8])
    win_d = k.din("win", [D, IN_MIX])
    zT_d = k.dout("zT", [IN_MIX, NT])

    emit_consts(k)
    xT = k.sb("xT_sb", [128, 8, NT], F32)
    hnT = k.sb("hnT", [128, 8, NT], BF16)
    xv = xT_d.rearrange("(k p) t -> p k t", p=128)
    cv = cT_d.rearrange("(k p) t -> p k t", p=128)
    for kk in range(8):
        p.dma(k.dq(), xT[:, kk, 0:TL], xv[:, kk, :], writes=[("xT", kk)])
        p.dma(k.dq(), xT[:, kk, TL:NT], cv[:, kk, :], writes=[("xT", kk)])
    k.modname = "Amod"
    mod = emit_adaln(k, wada_d, bada_d, cvec_d, [0, 1], "A")
    banks = [(k.ps("bank%d" % i, [128, 512], F32), "bank%d" % i) for i in range(7)]
    blocks = [(i * 512, (i + 1) * 512, 0) for i in range(4)] + [(TL, NT, 1)]
    emit_norm_mod(k, xT, NT, blocks, g_d, mod, 0, 1, hnT, "n1", "xT", banks)
    wv = win_d.rearrange("(k p) m -> p k m", p=128)
    wst = [k.sb("wst%d" % i, [128, 8, 128], F32) for i in range(2)]
    wbf = [k.sb("wbf%d" % i, [128, 8, 128], BF16) for i in range(2)]
    ost = [k.sb("ost%d" % i, [128, NT], F32) for i in range(2)]
    nm = (IN_MIX + 127) // 128
    bi = 0
    for m in range(nm):
        c0 = m * 128
        mw = min(128, IN_MIX - c0)
        b = m % 2
        p.dma(k.dq(), wst[b][:, :, :mw], wv[:, :, c0:c0 + mw], writes=[("wst", b)])
        p.op("pool", (lambda b=b, mw=mw: nc.gpsimd.tensor_copy(out=wbf[b][:, :, :mw], in_=wst[b][:, :, :mw])),
             reads=[("wst", b)], writes=[("wbf", b)])
        for tb, (t0, t1, v) in enumerate(blocks):
            w = t1 - t0
            pt, pn = banks[bi % len(banks)]
            bi += 1
            for kk in range(8):
                p.op("pe", (lambda b=b, kk=kk, mw=mw, t0=t0, t1=t1, w=w, pt=pt: nc.tensor.matmul(
                    pt[:mw, :w], lhsT=wbf[b][:, kk, :mw], rhs=hnT[:, kk, t0:t1], start=(kk == 0), stop=(kk == 7))),
                    reads=[("wbf", b), ("n1hnT", kk, tb)], writes=[pn])
            if tb % 2 == 0:
                p.op("act", (lambda b=b, mw=mw, t0=t0, t1=t1, w=w, pt=pt: nc.scalar.copy(
                    out=ost[b][:mw, t0:t1], in_=pt[:mw, :w])), reads=[pn], writes=[("ost", b, tb)])
            else:
                p.op("dve", (lambda b=b, mw=mw, t0=t0, t1=t1, w=w, pt=pt: nc.vector.tensor_copy(
                    out=ost[b][:mw, t0:t1], in_=pt[:mw, :w])), reads=[pn], writes=[("ost", b, tb)])
        p.dma(k.dq(), zT_d[c0:c0 + mw, :], ost[b][:mw, :], reads=[("ost", b)], writes=[("zout", m)])
    return k.finish(["zout"])


def _ft(a):
    return np.ascontiguousarray(np.asarray(a, np.float32).reshape(-1, 128).T)


def run_phaseA(xT, cT, c, c_ctx, w_ada_l, b_ada_l, g_l, w_in_l):
    nc = build_phaseA()
    cvec = np.ascontiguousarray(np.stack([_ft(c.reshape(-1)), _ft(c_ctx.reshape(-1))], axis=-1))
    common = dict(cT=np.ascontiguousarray(cT), cvec=cvec, wada=np.ascontiguousarray(w_ada_l),
                  bada=_ft(b_ada_l), g1=_ft(g_l), win=np.ascontiguousarray(w_in_l[:, :IN_MIX]))
    in_maps = [dict(common, xT=np.ascontiguousarray(xT[:, i * TL:(i + 1) * TL])) for i in range(NCORE)]
    res = run_bass_kernel_spmd(nc, in_maps, core_ids=list(range(NCORE)))
    return [r["zT"] for r in res.results]


def _E(k, eng):
    return k.p.eng[eng]


def op_tt(k, eng, out, in0, in1, op, r, w):
    e = _E(k, eng)
    k.p.op(eng, lambda: e.tensor_tensor(out=out, in0=in0, in1=in1, op=op), r, w)


def op_ts(k, eng, out, in0, s1, s2, op0, op1, r, w):
    e = _E(k, eng)
    if op1 is None:
        k.p.op(eng, lambda: e.tensor_scalar(out=out, in0=in0, scalar1=s1, scalar2=None, op0=op0), r, w)
    else:
        k.p.op(eng, lambda: e.tensor_scalar(out=out, in0=in0, scalar1=s1, scalar2=s2, op0=op0, op1=op1), r, w)


def op_stt(k, eng, out, in0, scalar, in1, op0, op1, r, w):
    e = _E(k, eng)
    k.p.op(eng, lambda: e.scalar_tensor_tensor(out=out, in0=in0, scalar=scalar, in1=in1, op0=op0, op1=op1), r, w)


def op_act(k, out, in_, func, r, w, scale=1.0, bias=0.0):
    nc = k.nc
    k.p.op("act", lambda: nc.scalar.activation(out=out, in_=in_, func=func, bias=bias, scale=scale), r, w)


def op_copy(k, eng, out, in_, r, w):
    e = _E(k, eng)
    if eng == "act":
        k.p.op(eng, lambda: e.copy(out=out, in_=in_), r, w)
    else:
        k.p.op(eng, lambda: e.tensor_copy(out=out, in_=in_), r, w)


def op_mm(k, out, lhsT, rhs, start, stop, r, w):
    nc = k.nc
    k.p.op("pe", lambda: nc.tensor.matmul(out, lhsT=lhsT, rhs=rhs, start=start, stop=stop), r, w)


def op_scan(k, eng, out, d0, d1, init, r, w, op0=None, op1=None):
    e = _E(k, eng)
    op0 = op0 or ALU.mult
    op1 = op1 or ALU.add
    k.p.op(eng, lambda: e.tensor_tensor_scan(out=out, data0=d0, data1=d1, initial=init, op0=op0, op1=op1), r, w)


def op_memset(k, eng, ap, val, w):
    e = _E(k, eng)
    k.p.op(eng, lambda: e.memset(ap, val), (), w)


def bcast_cols(col_ap, n):
    return bass.AP(col_ap.tensor, col_ap.offset, [list(col_ap.ap[0]), [0, n]])


def emit_identity(k):
    nc = k.nc
    io = k.sb("ident_i", [128, 128], I32)
    ident = k.sb("ident", [128, 128], F32)
    k.p.op("pool", lambda: nc.gpsimd.iota(io[:], pattern=[[1, 128]], base=0, channel_multiplier=-1), (), ["ident_i"])
    op_copy(k, "dve", ident[:], io[:], ["ident_i"], ["ident"])
    op_ts(k, "dve", ident[:], ident[:], 0.0, None, ALU.is_equal, None, ["ident"], ["ident"])
    k.ident = ident
    return ident


TWO_PI = 6.283185307179586
C1_2PI = 6.28125
C2_2PI = TWO_PI - 6.28125
PI_SAFE = 3.1415925


def emit_sin(k, out, ang, shift, tmp, tmpi, r, w, tag):
    wn = [tag + "_t"]
    wi = [tag + "_i"]
    op_ts(k, "dve", tmp, ang, shift, 1.0 / TWO_PI, ALU.add, ALU.mult, r, wn)
    op_copy(k, "dve", tmpi, tmp, wn, wi)
    op_copy(k, "dve", tmp, tmpi, wi, wn)
    op_stt(k, "dve", out, tmp, -C1_2PI, ang, ALU.mult, ALU.add, wn + list(r), w)
    if shift != 0.0:
        op_ts(k, "dve", out, out, shift, None, ALU.add, None, w, w)
    op_stt(k, "dve", out, tmp, -C2_2PI, out, ALU.mult, ALU.add, wn + list(w), w)
    op_ts(k, "dve", out, out, -PI_SAFE, PI_SAFE, ALU.max, ALU.min, w, w)
    op_act(k, out, out, AF.Sin, w, w)


class Banks:
    def __init__(self, k, n, items=None):
        self.b = items if items is not None else [(k.ps("bank%d" % i, [128, 512], F32), "bank%d" % i) for i in range(n)]
        self.i = 0

    def next(self):
        self.i += 1
        return self.b[self.i % len(self.b)]

    def sub(self, lo, hi):
        return Banks(None, 0, self.b[lo:hi])


NSEQ = CTX + SEQ
S5_T = 1024


def build_s5():
    k = K()
    rec_s5(k, Banks(k, 3))
    return k.finish(["yout"])


def rec_s5(k, bk):
    nc, p = k.nc, k.p
    T = S5_T
    u_d = k.din("uT", [32, NSEQ])
    lam_d = k.din("lam", [128, 6])
    bp_d = k.din("bpad", [128, 2, 2, 32])
    ct_d = k.din("ctp", [128, 2, 2, 32])
    y_d = k.dout("ys", [32, NSEQ])
    if not hasattr(k, "ones"):
        emit_consts(k)
    ident = emit_identity(k)
    lam = k.sb("lam", [128, 6], F32)
    bp = k.sb("bp", [128, 2, 2, 32], F32)
    ct = k.sb("ct", [128, 2, 2, 32], F32)
    p.dma("sp", lam[:], lam_d, writes=["lam"])
    p.dma("sp", bp[:], bp_d, writes=["bp"])
    p.dma("sp", ct[:], ct_d, writes=["ct"])
    for di in range(2):
        op_ts(k, "dve", ct[:, di, 1, :], ct[:, di, 1, :], -1.0, None, ALU.mult, None, ["ct"], ["ct"])
    sc = k.sb("s5sc", [128, 40], F32)
    sci = k.sb("s5sci", [128, 8], I32)
    col = lambda i: sc[:, i:i + 1]
    bbT = [[k.sb("bbT%d%d" % (di, ri), [32, 128], F32) for ri in range(2)] for di in range(2)]
    bbf = k.sb("bbf", [128, 2, 2, 32], F32)
    pst_t, pst_n = bk.next()
    pst = pst_t[0:32, :].rearrange("p (a b) -> p a b", b=128)
    for di in range(2):
        b0 = 16 * di
        S = lambda i: col(b0 + i)
        rw = ["s5sc"]
        op_act(k, S(0), lam[:, 4 + di:5 + di], AF.Exp, ["lam"], rw)
        op_tt(k, "dve", S(1), lam[:, di:di + 1], S(0), ALU.mult, ["lam"] + rw, rw)
        op_act(k, S(1), S(1), AF.Exp, rw, rw)
        op_tt(k, "dve", S(2), lam[:, 2 + di:3 + di], S(0), ALU.mult, ["lam"] + rw, rw)
        emit_sin(k, S(4), S(2), 0.0, S(13), sci[:, 0:1], rw, rw, "s5r")
        emit_sin(k, S(3), S(2), 0.5 * np.pi, S(13), sci[:, 0:1], rw, rw, "s5r")
        op_tt(k, "dve", S(5), S(1), S(3), ALU.mult, rw, rw)
        op_tt(k, "dve", S(6), S(1), S(4), ALU.mult, rw, rw)
        op_ts(k, "dve", S(7), S(5), -1.0, None, ALU.add, None, rw, rw)
        op_tt(k, "dve", S(8), lam[:, di:di + 1], lam[:, di:di + 1], ALU.mult, ["lam"], rw)
        op_stt(k, "dve", S(8), lam[:, 2 + di:3 + di], lam[:, 2 + di:3 + di], S(8), ALU.mult, ALU.add, ["lam"] + rw, rw)
        op_copy(k, "dve", S(12), S(8), rw, rw)
        k.p.op("dve", (lambda o=S(8), i=S(12): nc.vector.reciprocal(out=o, in_=i)), rw, rw)
        op_tt(k, "dve", S(11), S(7), lam[:, di:di + 1], ALU.mult, ["lam"] + rw, rw)
        op_stt(k, "dve", S(9), S(6), lam[:, 2 + di:3 + di], S(11), ALU.mult, ALU.add, ["lam"] + rw, rw)
        op_tt(k, "dve", S(9), S(9), S(8), ALU.mult, rw, rw)
        op_tt(k, "dve", S(11), S(7), lam[:, 2 + di:3 + di], ALU.mult, ["lam"] + rw, rw)
        op_stt(k, "dve", S(10), S(6), lam[:, di:di + 1], S(11), ALU.mult, ALU.subtract, ["lam"] + rw, rw)
        op_tt(k, "dve", S(10), S(10), S(8), ALU.mult, rw, rw)
        op_ts(k, "dve", bbf[:, di, 0, :], bp[:, di, 1, :], S(10), -1.0, ALU.mult, ALU.mult, ["bp"] + rw, [("bbf", di, 0)])
        op_stt(k, "dve", bbf[:, di, 0, :], bp[:, di, 0, :], S(9), bbf[:, di, 0, :], ALU.mult, ALU.add, ["bp", ("bbf", di, 0)] + rw, [("bbf", di, 0)])
        op_ts(k, "dve", bbf[:, di, 1, :], bp[:, di, 0, :], S(10), None, ALU.mult, None, ["bp"] + rw, [("bbf", di, 1)])
        op_stt(k, "dve", bbf[:, di, 1, :], bp[:, di, 1, :], S(9), bbf[:, di, 1, :], ALU.mult, ALU.add, ["bp", ("bbf", di, 1)] + rw, [("bbf", di, 1)])
        for ri in range(2):
            op_mm(k, pst[:, di * 2 + ri, :], bbf[:, di, ri, :], ident[:], True, True, [("bbf", di, ri), "ident"], [pst_n])
            op_copy(k, "dve", bbT[di][ri][:], pst[:, di * 2 + ri, :], [pst_n], [("bbT", di, ri)])
    jfi = k.sb("jfi", [128, T], I32)
    jf = k.sb("jf", [128, T], F32)
    tang = k.sb("tang", [128, T], F32)
    ttmp = k.sb("ttmp", [128, T], F32)
    tti = k.sb("tti", [128, T], I32)
    p.op("pool", lambda: nc.gpsimd.iota(jfi[:], pattern=[[1, T]], base=0, channel_multiplier=0), (), ["jfi"])
    op_copy(k, "dve", jf[:], jfi[:], ["jfi"], ["jf"])
    tab = [[k.sb("tab%d%d" % (di, cs), [128, T], F32) for cs in range(2)] for di in range(2)]
    for di in range(2):
        op_ts(k, "dve", tang[:], jf[:], col(16 * di + 2), None, ALU.mult, None, ["jf", "s5sc"], ["tang"])
        emit_sin(k, tab[di][1][:], tang[:], 0.0, ttmp[:], tti[:], ["tang"], [("tab", di, 1)], "s5t")
        emit_sin(k, tab[di][0][:], tang[:], 0.5 * np.pi, ttmp[:], tti[:], ["tang"], [("tab", di, 0)], "s5t")
    segs = [(0, CTX)] + [(CTX + i * T, CTX + (i + 1) * T) for i in range(SEQ // T)]
    ub = [k.sb("ub%d" % i, [32, T], F32) for i in range(2)]
    W = {n: k.sb("s5" + n, [128, T], F32) for n in ("br", "bi", "pr", "pi", "hr", "hi", "t1", "t2")}
    yst = [k.sb("yst%d" % i, [32, T], F32) for i in range(2)]
    carry = k.sb("carry", [128, 4], F32)
    it = 0
    for di in range(2):
        order = segs if di == 0 else [segs[0]] + segs[:0:-1]
        cosT, sinT = tab[di]
        rcol = col(16 * di + 1)
        cth, sth = cosT[:, 1:2], sinT[:, 1:2]
        for si, (t0, t1) in enumerate(order):
            n = t1 - t0
            b = it % 2
            it += 1
            first = si == 0
            fw = (lambda ap: ap) if di == 0 else rev_ap
            cv = cosT[:, 0:n] if di == 0 else rev_ap(cosT[:, 0:n])
            sv = sinT[:, 0:n] if di == 0 else rev_ap(sinT[:, 0:n])
            tabr = [("tab", di, 0), ("tab", di, 1)]
            p.dma("sp", ub[b][:, :n], u_d[:, t0:t1], writes=[("ub", b)])
            nb = (n + 511) // 512
            for j in range(nb):
                c0, c1 = j * 512, min(n, (j + 1) * 512)
                w = c1 - c0
                bur, burn = bk.next()
                bui, buin = bk.next()
                op_mm(k, bur[:, :w], bbT[di][0][:], ub[b][:, c0:c1], True, True, [("bbT", di, 0), ("ub", b)], [burn])
                op_mm(k, bui[:, :w], bbT[di][1][:], ub[b][:, c0:c1], True, True, [("bbT", di, 1), ("ub", b)], [buin])
                op_tt(k, "dve", W["t1"][:, c0:c1], bur[:, :w], cv[:, c0:c1], ALU.mult, [burn] + tabr, [("t1", j)])
                op_tt(k, "dve", W["t2"][:, c0:c1], bui[:, :w], sv[:, c0:c1], ALU.mult, [buin] + tabr, [("t2", j)])
                op_tt(k, "pool", W["br"][:, c0:c1], W["t1"][:, c0:c1], W["t2"][:, c0:c1], ALU.add, [("t1", j), ("t2", j)], [("br", j)])
                op_tt(k, "dve", W["t1"][:, c0:c1], bui[:, :w], cv[:, c0:c1], ALU.mult, [buin] + tabr, [("t1", j)])
                op_tt(k, "dve", W["t2"][:, c0:c1], bur[:, :w], sv[:, c0:c1], ALU.mult, [burn] + tabr, [("t2", j)])
                op_tt(k, "pool", W["bi"][:, c0:c1], W["t1"][:, c0:c1], W["t2"][:, c0:c1], ALU.subtract, [("t1", j), ("t2", j)], [("bi", j)])
            rb = bcast_cols(rcol, n)
            ire = 0.0 if first else carry[:, 2:3]
            iim = 0.0 if first else carry[:, 3:4]
            op_scan(k, "dve", fw(W["pr"][:, :n]), rb, fw(W["br"][:, :n]), ire, ["br", "s5sc", "carry"], ["pr"])
            op_scan(k, "dve", fw(W["pi"][:, :n]), rb, fw(W["bi"][:, :n]), iim, ["bi", "s5sc", "carry"], ["pi"])
            op_tt(k, "dve", W["t1"][:, :n], W["pr"][:, :n], cv, ALU.mult, ["pr"] + tabr, ["t1"])
            op_tt(k, "pool", W["t2"][:, :n], W["pi"][:, :n], sv, ALU.mult, ["pi"] + tabr, ["t2"])
            op_tt(k, "dve", W["hr"][:, :n], W["t1"][:, :n], W["t2"][:, :n], ALU.subtract, ["t1", "t2"], ["hr"])
            op_tt(k, "pool", W["t1"][:, :n], W["pr"][:, :n], sv, ALU.mult, ["pr"] + tabr, ["t1"])
            op_tt(k, "dve", W["t2"][:, :n], W["pi"][:, :n], cv, ALU.mult, ["pi"] + tabr, ["t2"])
            op_tt(k, "pool", W["hi"][:, :n], W["t1"][:, :n], W["t2"][:, :n], ALU.add, ["t1", "t2"], ["hi"])
            lc = n - 1 if di == 0 else 0
            op_tt(k, "dve", carry[:, 0:1], W["hr"][:, lc:lc + 1], cth, ALU.mult, ["hr"] + tabr, ["carry0"])
            op_tt(k, "dve", carry[:, 1:2], W["hi"][:, lc:lc + 1], sth, ALU.mult, ["hi"] + tabr, ["carry1"])
            op_tt(k, "dve", carry[:, 2:3], carry[:, 0:1], carry[:, 1:2], ALU.subtract, ["carry0", "carry1", "pr", "pi"], ["carry"])
            op_tt(k, "dve", carry[:, 0:1], W["hr"][:, lc:lc + 1], sth, ALU.mult, ["hr", "carry"] + tabr, ["carry0"])
            op_tt(k, "dve", carry[:, 1:2], W["hi"][:, lc:lc + 1], cth, ALU.mult, ["hi", "carry"] + tabr, ["carry1"])
            op_tt(k, "dve", carry[:, 3:4], carry[:, 0:1], carry[:, 1:2], ALU.add, ["carry0", "carry1"], ["carry"])
            if di == 1:
                p.dma("pool", yst[b][:, :n], y_d[:, t0:t1], reads=[("yout", t0)], writes=[("yst", b)])
            for j in range(nb):
                c0, c1 = j * 512, min(n, (j + 1) * 512)
                w = c1 - c0
                ypt, ypn = bk.next()
                yps_ = ypt[0:32, :]
                op_mm(k, yps_[:, :w], ct[:, di, 0, :], W["hr"][:, c0:c1], True, False, ["ct", "hr"], [ypn])
                op_mm(k, yps_[:, :w], ct[:, di, 1, :], W["hi"][:, c0:c1], False, True, ["ct", "hi"], [ypn])
                if di == 0:
                    op_copy(k, "act", yst[b][:, c0:c1], yps_[:, :w], [ypn], [("yst", b)])
                else:
                    op_tt(k, "dve", yst[b][:, c0:c1], yst[b][:, c0:c1], yps_[:, :w], ALU.add, [ypn, ("yst", b)], [("yst", b)])
            p.dma("sp", y_d[:, t0:t1], yst[b][:, :n], reads=[("yst", b)], writes=[("yout", t0)])


def s5_inputs(core, zu_all, lam_re, lam_im, lstep, b_re, b_im, c_re, c_im):
    g0 = 2 * core
    lam = np.zeros((128, 6), np.float32)
    bp = np.zeros((128, 2, 2, 32), np.float32)
    ct = np.zeros((128, 2, 2, 32), np.float32)
    for di in range(2):
        for gl in range(2):
            g = g0 + gl
            sl = slice(gl * 64, (gl + 1) * 64)
            lam[sl, di] = lam_re[di, g]
            lam[sl, 2 + di] = lam_im[di, g]
            lam[sl, 4 + di] = lstep[di, g]
            bp[sl, di, 0, gl * 16:(gl + 1) * 16] = b_re[di, g]
            bp[sl, di, 1, gl * 16:(gl + 1) * 16] = b_im[di, g]
            ct[sl, di, 0, gl * 16:(gl + 1) * 16] = c_re[di, g].T
            ct[sl, di, 1, gl * 16:(gl + 1) * 16] = c_im[di, g].T
    return dict(uT=np.ascontiguousarray(zu_all[32 * core:32 * core + 32]), lam=lam, bpad=bp, ctp=ct)


LRU_T = 1024


def build_lru():
    k = K()
    rec_lru(k, Banks(k, 2))
    return k.finish(["lout"])


def rec_lru(k, bk):
    nc, p = k.nc, k.p
    T = LRU_T
    x_d = k.din("lxT", [32, NSEQ])
    y_d = k.din("lyT", [32, NSEQ])
    par_d = k.din("lpar", [32, 12])
    w_d = k.din("lw", [32, 2, 2, 32])
    o_d = k.dout("lo", [32, NSEQ])
    par = k.sb("lpar", [32, 12], F32)
    w = k.sb("lw", [32, 2, 2, 32], F32)
    cl = k.sb("lcl", [32, 2], F32)
    p.dma("sp", par[:], par_d, writes=["lpar"])
    p.dma("sp", w[:], w_d, writes=["lw"])
    op_act(k, cl[:], par[:, 9:11], AF.Exp, ["lpar"], ["lcl"], scale=-1.0)
    op_ts(k, "dve", cl[:], cl[:], 1.0, None, ALU.add, None, ["lcl"], ["lcl"])
    op_act(k, cl[:], cl[:], AF.Ln, ["lcl"], ["lcl"])
    op_ts(k, "dve", cl[:], cl[:], -8.0, None, ALU.mult, None, ["lcl"], ["lcl"])
    segs = [(0, CTX, 0, CTX)] + [(CTX + i * T, CTX + (i + 1) * T, CTX, NSEQ) for i in range(SEQ // T)]
    xs = [k.sb("lxs%d" % i, [32, T + 3], F32) for i in range(2)]
    W = {n: k.sb("l" + n, [32, T], F32) for n in ("xc", "r", "i", "a", "q", "b", "h")}
    ys = [k.sb("lys%d" % i, [32, T], F32) for i in range(2)]
    hf = [k.sb("lhf%d" % i, [32, T], F32) for i in range(2)]
    carry = k.sb("lcarry", [32, 1], F32)
    it = 0
    for di in range(2):
        order = segs if di == 0 else [segs[0]] + segs[:0:-1]
        for si, (t0, t1, lo, hi) in enumerate(order):
            n = t1 - t0
            b = it % 2
            it += 1
            fw = (lambda ap: ap) if di == 0 else rev_ap
            a0, a1 = max(lo, t0 - 2), min(hi, t1 + 1)
            if a0 > t0 - 2:
                op_memset(k, "pool", xs[b][:, 0:2], 0.0, [("lxs", b)])
            if a1 < t1 + 1:
                op_memset(k, "pool", xs[b][:, n + 2:n + 3], 0.0, [("lxs", b)])
            p.dma("sp", xs[b][:, a0 - (t0 - 2):a1 - (t0 - 2)], x_d[:, a0:a1], writes=[("lxs", b)])
            X = xs[b]
            op_ts(k, "dve", W["xc"][:, :n], X[:, 0:n], par[:, 0:1], par[:, 4:5], ALU.mult, ALU.add, [("lxs", b), "lpar"], ["xc"])
            for j in range(1, 4):
                op_stt(k, "dve", W["xc"][:, :n], X[:, j:j + n], par[:, j:j + 1], W["xc"][:, :n], ALU.mult, ALU.add, [("lxs", b), "lpar", "xc"], ["xc"])
            nb = (n + 511) // 512
            for j in range(nb):
                c0, c1 = j * 512, min(n, (j + 1) * 512)
                wd = c1 - c0
                prt, prn = bk.next()
                pit, pin = bk.next()
                op_mm(k, prt[0:32, :wd], w[:, di, 0, :], W["xc"][:, c0:c1], True, True, ["lw", "xc"], [prn])
                op_mm(k, pit[0:32, :wd], w[:, di, 1, :], W["xc"][:, c0:c1], True, True, ["lw", "xc"], [pin])
                op_act(k, W["r"][:, c0:c1], prt[0:32, :wd], AF.Sigmoid, [prn, "lpar"], [("r", j)], bias=par[:, 5 + di:6 + di])
                op_act(k, W["i"][:, c0:c1], pit[0:32, :wd], AF.Sigmoid, [pin, "lpar"], [("i", j)], bias=par[:, 7 + di:8 + di])
            op_act(k, W["a"][:, :n], W["r"][:, :n], AF.Exp, ["r", "lcl"], ["a"], scale=cl[:, di:di + 1])
            op_tt(k, "dve", W["q"][:, :n], W["a"][:, :n], W["a"][:, :n], ALU.mult, ["a"], ["q"])
            op_ts(k, "dve", W["q"][:, :n], W["q"][:, :n], -1.0, 1.0, ALU.mult, ALU.add, ["q"], ["q"])
            op_act(k, W["q"][:, :n], W["q"][:, :n], AF.Sqrt, ["q"], ["q"])
            op_tt(k, "pool", W["b"][:, :n], W["i"][:, :n], W["xc"][:, :n], ALU.mult, ["i", "xc"], ["b"])
            op_tt(k, "dve", W["b"][:, :n], W["b"][:, :n], W["q"][:, :n], ALU.mult, ["b", "q"], ["b"])
            init = 0.0 if si == 0 else carry[:, 0:1]
            op_scan(k, "dve", fw(W["h"][:, :n]), fw(W["a"][:, :n]), fw(W["b"][:, :n]), init, ["a", "b", "lcarry"], ["h"])
            lc = n - 1 if di == 0 else 0
            op_copy(k, "dve", carry[:, 0:1], W["h"][:, lc:lc + 1], ["h"], ["lcarry"])
            if di == 0:
                p.dma("pool", o_d[:, t0:t1], W["h"][:, :n], reads=["h"], writes=[("lout", t0)])
            else:
                p.dma("pool", hf[b][:, :n], o_d[:, t0:t1], reads=[("lout", t0)], writes=[("lhf", b)])
                p.dma("sp", ys[b][:, :n], y_d[:, t0:t1], writes=[("lys", b)])
                op_act(k, ys[b][:, :n], ys[b][:, :n], AF.Gelu_apprx_tanh, [("lys", b)], [("lys", b)])
                op_tt(k, "pool", hf[b][:, :n], hf[b][:, :n], W["h"][:, :n], ALU.add, [("lhf", b), "h"], [("lhf", b)])
                op_tt(k, "dve", hf[b][:, :n], hf[b][:, :n], ys[b][:, :n], ALU.mult, [("lhf", b), ("lys", b)], [("lhf", b)])
                p.dma("sp", o_d[:, t0:t1], hf[b][:, :n], reads=[("lhf", b)], writes=[("lout", t0)])


def lru_inputs(core, zx_all, zy_all, conv_w, conv_b, wr, br, wi, bi, lam):
    ch = slice(32 * core, 32 * core + 32)
    par = np.zeros((32, 12), np.float32)
    par[:, 0:4] = conv_w[:, ch].T
    par[:, 4] = conv_b[ch]
    for di in range(2):
        par[:, 5 + di] = br[di, ch]
        par[:, 7 + di] = bi[di, ch]
        par[:, 9 + di] = lam[di, ch]
    w = np.zeros((32, 2, 2, 32), np.float32)
    for di in range(2):
        w[:, di, 0, :] = wr[di, core]
        w[:, di, 1, :] = wi[di, core]
    return dict(lxT=np.ascontiguousarray(zx_all[ch]), lyT=np.ascontiguousarray(zy_all[ch]), lpar=par, lw=w)


GLA_C = 64
NCHUNK = NSEQ // GLA_C


def build_gla():
    k = K()
    rec_gla(k, Banks(k, 3))
    return k.finish(["gout"])


def rec_gla(k, bk):
    nc, p = k.nc, k.p
    q_d = k.din("gq", [32, NSEQ])
    k_d = k.din("gk", [32, NSEQ])
    a_d = k.din("ga", [32, NSEQ])
    vt_d = k.din("gvt", [64, NCHUNK, 32])
    kt_d = k.din("gkt", [64, NCHUNK, 32])
    w_d = k.din("gw", [33, 2, 32])
    o_d = k.dout("go", [32, NSEQ])
    ioi = k.sb("gioi", [64, 64], I32)
    iof = k.sb("giof", [64, 64], F32)
    p.op("pool", lambda: nc.gpsimd.iota(ioi[:], pattern=[[1, 64]], base=0, channel_multiplier=-1), (), ["gioi"])
    op_copy(k, "dve", iof[:], ioi[:], ["gioi"], ["giof"])
    msk = {}
    for nm, cmp in (("ge", ALU.is_ge), ("le", ALU.is_le), ("lt", ALU.is_lt), ("gt", ALU.is_gt)):
        msk[nm] = k.sb("gm" + nm, [64, 64], F32)
        op_ts(k, "dve", msk[nm][:], iof[:], 0.0, None, cmp, None, ["giof"], ["gm" + nm])
    amask, triI, triS = [], [], []
    for di in range(2):
        am = k.sb("gam%d" % di, [64, 8, 64], F32)
        src = msk["ge"] if di == 0 else msk["le"]
        for n in range(8):
            op_copy(k, "dve", am[:, n, :], src[:], ["gmge", "gmle"], ["gam%d" % di])
        ti = k.sb("gti%d" % di, [64, 64], F32)
        ts_ = k.sb("gts%d" % di, [64, 64], F32)
        op_ts(k, "dve", ti[:], src[:], -1.0 / 16, None, ALU.mult, None, ["gmge", "gmle"], ["gti%d" % di])
        op_ts(k, "dve", ts_[:], (msk["lt"] if di == 0 else msk["gt"])[:], -1.0 / 16, None, ALU.mult, None, ["gmlt", "gmgt"], ["gts%d" % di])
        amask.append(am); triI.append(ti); triS.append(ts_)
    w = k.sb("gw", [33, 2, 32], F32)
    p.dma("sp", w[:], w_d, writes=["gw"])
    NB = 2
    a1 = [k.sb("ga1%d" % i, [33, 512], F32) for i in range(NB)]
    qb = [k.sb("gqb%d" % i, [32, 512], F32) for i in range(NB)]
    kb = [k.sb("gkb%d" % i, [32, 512], F32) for i in range(NB)]
    vt = [k.sb("gvt%d" % i, [64, 8, 32], F32) for i in range(NB)]
    kt = [k.sb("gkt%d" % i, [64, 8, 32], F32) for i in range(NB)]
    of = [k.sb("gof%d" % i, [32, 512], F32) for i in range(NB)]
    for i in range(NB):
        op_memset(k, "pool", a1[i][32:33, :], 1.0, [("ga1", i, "one")])
    L = k.sb("gL", [64, 8, 32], F32)
    kd = k.sb("gkd", [64, 8, 32], F32)
    eb = k.sb("geb", [32, 8, 64], F32)
    enb = k.sb("genb", [32, 8, 64], F32)
    qe = k.sb("gqe", [32, 512], F32)
    ke = k.sb("gke", [32, 512], F32)
    attm = k.sb("gattm", [64, 8, 64], F32)
    Sl = k.sb("gS", [32, 9, 32], F32)
    blocks = [(0, 4)] + [(4 + 8 * i, 8) for i in range(32)]
    it = 0
    for di in range(2):
        order = blocks if di == 0 else [blocks[0]] + blocks[:0:-1]
        for bi_, (n0, nch) in enumerate(order):
            b = it % NB
            it += 1
            t0, n = n0 * 64, nch * 64
            t1 = t0 + n
            p.dma("sp", a1[b][0:32, :n], a_d[:, t0:t1], writes=[("ga1", b, "a")])
            p.dma("pool", qb[b][:, :n], q_d[:, t0:t1], writes=[("gqb", b)])
            p.dma("sp", kb[b][:, :n], k_d[:, t0:t1], writes=[("gkb", b)])
            p.dma("pool", vt[b][:, :nch, :], vt_d[:, n0:n0 + nch, :], writes=[("gvt", b)])
            p.dma("sp", kt[b][:, :nch, :], kt_d[:, n0:n0 + nch, :], writes=[("gkt", b)])
            if di == 1:
                p.dma("pool", of[b][:, :n], o_d[:, t0:t1], reads=[("gout", t0)], writes=[("gof", b)])
            if bi_ == 0:
                op_memset(k, "dve", Sl[:, 0, :], 0.0, [("gS", 0)])
            v32 = lambda t: t[0:64, 0:256].rearrange("p (c j) -> p c j", j=32)
            v64 = lambda t, np_: t[0:np_, :].rearrange("p (c j) -> p c j", j=64)
            _t, gpz = bk.next(); pz = v32(_t)
            for c in range(nch):
                op_mm(k, pz[:, c, :], a1[b][:, c * 64:(c + 1) * 64], w[:, di, :], True, True, [("ga1", b), "gw"], [gpz])
            op_act(k, L[:, :nch, :], pz[:, :nch, :], AF.Exp, [gpz], ["gL"], scale=-1.0)
            op_ts(k, "dve", L[:, :nch, :], L[:, :nch, :], 1.0, None, ALU.add, None, ["gL"], ["gL"])
            op_act(k, L[:, :nch, :], L[:, :nch, :], AF.Ln, ["gL"], ["gL"])
            _t, gpg = bk.next(); pg = v32(_t)
            op_mm(k, pg[:, :nch, :], triS[di][:], L[:, :nch, :], True, True, ["gts%d" % di, "gL"], [gpg])
            op_act(k, kd[:, :nch, :], pg[:, :nch, :], AF.Exp, [gpg], ["gkd"])
            op_tt(k, "dve", kd[:, :nch, :], kd[:, :nch, :], kt[b][:, :nch, :], ALU.mult, ["gkd", ("gkt", b)], ["gkd"])
            _t, gpb = bk.next(); pb = v64(_t, 32)
            for c in range(nch):
                op_mm(k, pb[:, c, :], L[:, c, :], triI[di][:], True, True, ["gL", "gti%d" % di], [gpb])
            op_act(k, eb[:, :nch, :], pb[:, :nch, :], AF.Exp, [gpb], ["geb"])
            op_act(k, enb[:, :nch, :], pb[:, :nch, :], AF.Exp, [gpb], ["genb"], scale=-1.0)
            ebf = eb[:].rearrange("p c j -> p (c j)")
            enbf = enb[:].rearrange("p c j -> p (c j)")
            op_stt(k, "dve", qe[:, :n], ebf[:, :n], GLA_C_SCALE, qb[b][:, :n], ALU.mult, ALU.mult, ["geb", ("gqb", b)], ["gqe"])
            op_tt(k, "pool", ke[:, :n], enbf[:, :n], kb[b][:, :n], ALU.mult, ["genb", ("gkb", b)], ["gke"])
            _t, gpa = bk.next(); pa = v64(_t, 64)
            for c in range(nch):
                op_mm(k, pa[:, c, :], ke[:, c * 64:(c + 1) * 64], qe[:, c * 64:(c + 1) * 64], True, True, ["gke", "gqe"], [gpa])
            op_tt(k, "dve", attm[:, :nch, :], pa[:, :nch, :], amask[di][:, :nch, :], ALU.mult, [gpa, "gam%d" % di], ["gattm"])
            _t, gpkv = bk.next(); pkv = _t[0:32, 0:256].rearrange("p (c j) -> p c j", j=32)
            for c in range(nch):
                op_mm(k, pkv[:, c, :], kd[:, c, :], vt[b][:, c, :], True, True, ["gkd", ("gvt", b)], [gpkv])
            cho = list(range(nch)) if di == 0 else list(range(nch - 1, -1, -1))
            lastj = 63 if di == 0 else 0
            for i, c in enumerate(cho):
                op_stt(k, "dve", Sl[:, i + 1, :], Sl[:, i, :], eb[:, c, lastj:lastj + 1], pkv[:, c, :], ALU.mult, ALU.add,
                       [("gS", i), "geb", gpkv], [("gS", i + 1)])
            _t, gpo = bk.next(); po = v64(_t, 32)
            for i, c in enumerate(cho):
                op_mm(k, po[:, c, :], vt[b][:, c, :], attm[:, c, :], True, False, [("gvt", b), "gattm"], [gpo])
                op_mm(k, po[:, c, :], Sl[:, i, :], qe[:, c * 64:(c + 1) * 64], False, True, [("gS", i), "gqe"], [gpo])
            pof = po.rearrange("p c j -> p (c j)")
            if di == 0:
                op_copy(k, "act", of[b][:, :n], pof[:, :n], [gpo], [("gof", b)])
            else:
                op_tt(k, "dve", of[b][:, :n], of[b][:, :n], pof[:, :n], ALU.add, [gpo, ("gof", b)], [("gof", b)])
            p.dma("sp", o_d[:, t0:t1], of[b][:, :n], reads=[("gof", b)], writes=[("gout", t0)])
            op_copy(k, "dve", Sl[:, 0, :], Sl[:, nch, :], [("gS", nch)], [("gS", 0)])


GLA_C_SCALE = 32 ** -0.5


def gla_inputs(core, zq, zk, zv, zg_a, wa2, ba):
    h, vh = core // 2, core % 2
    qT = np.ascontiguousarray(zq[32 * h:32 * h + 32])
    kT = np.ascontiguousarray(zk[32 * h:32 * h + 32])
    vT = zv[64 * h + 32 * vh:64 * h + 32 * vh + 32]
    tok = lambda xT: np.ascontiguousarray(xT.T.reshape(NCHUNK, 64, 32).transpose(1, 0, 2))
    w = np.zeros((33, 2, 32), np.float32)
    for di in range(2):
        w[16 * di:16 * di + 16, di, :] = wa2[di][:, 32 * h:32 * h + 32]
        w[32, di, :] = ba[di][32 * h:32 * h + 32]
    return dict(gq=qT, gk=kT, ga=np.ascontiguousarray(zg_a), gvt=tok(vT), gkt=tok(kT), gw=w)


def _interleave(lists):
    out = []
    n = [len(l) for l in lists]
    pos = [0] * len(lists)
    total = sum(n)
    while len(out) < total:
        best = min((pos[i] / n[i], i) for i in range(len(lists)) if pos[i] < n[i])[1]
        out.append(lists[best][pos[best]])
        pos[best] += 1
    return out


def build_scans():
    k = K()
    allb = Banks(k, 8)
    lists = []
    for rec, bk in ((rec_s5, allb.sub(0, 3)), (rec_lru, allb.sub(3, 5)), (rec_gla, allb.sub(5, 8))):
        k.p = Prog(k.nc)
        rec(k, bk)
        lists.append(k.p.ops)
    k.p = Prog(k.nc)
    k.p.ops = _interleave(lists)
    return k.finish(["yout", "lout", "gout"])


NKT = NSEQ // 128
QH = SEQ // 2
LN1E4 = float(np.log(10000.0))


def build_da(lam_init, nqb=QH // 512):
    k = K()
    nc, p = k.nc, k.p
    q_d = k.din("dq", [64, QH])
    k_d = k.din("dk", [64, NSEQ])
    vt_d = k.din("dvt", [128, NKT, 64])
    qc_d = k.din("dqc", [64, CTX])
    par_d = k.din("dpar", [64, 4])
    lam_d = k.din("dlam", [32, 4])
    y_d = k.dout("dy", [64, QH])
    yc_d = k.dout("dyc", [64, CTX])
    emit_consts(k)
    par = k.sb("dpar", [64, 4], F32)
    lp = k.sb("dlp", [32, 4], F32)
    p.dma("sp", par[:], par_d, writes=["dpar"])
    p.dma("sp", lp[:], lam_d, writes=["dlp"])
    sc = k.sb("dsc", [64, 16], F32)
    col = lambda i: sc[:, i:i + 1]
    R_ = ["dsc"]
    pr2 = k.sb("dpr2", [32, 2], F32)
    op_tt(k, "dve", pr2[:, 0:1], lp[:, 0:1], lp[:, 1:2], ALU.mult, ["dlp"], ["dpr2"])
    op_tt(k, "dve", pr2[:, 1:2], lp[:, 2:3], lp[:, 3:4], ALU.mult, ["dlp"], ["dpr2"])
    pS = [k.ps("dpS%d" % i, [128, 2, 512], F32) for i in range(3)]
    pm = pS[0][:, 0, :]
    op_mm(k, pm[0:64, 0:2], k.ones[0:32, 0:64], pr2[:], True, True, ["ones", "dpr2"], [("dpS", 0)])
    op_act(k, sc[:, 3:5], pm[0:64, 0:2], AF.Exp, [("dpS", 0)], R_)
    op_tt(k, "dve", col(0), col(4), col(3), ALU.subtract, R_, R_)
    op_ts(k, "dve", col(0), col(0), -float(lam_init), None, ALU.add, None, R_, R_)
    op_ts(k, "dve", col(1), par[:, 2:3], 1.0 - float(lam_init), None, ALU.mult, None, ["dpar"], R_)
    op_ts(k, "dve", col(2), par[:, 0:1], 32 ** -0.5, None, ALU.mult, None, ["dpar"], R_)
    pi_ = k.sb("dpi", [64, 4], I32)
    p.op("pool", lambda: nc.gpsimd.iota(pi_[:, 0:1], pattern=[[0, 1]], base=0, channel_multiplier=1), (), ["dpi"])
    op_ts(k, "dve", pi_[:, 1:2], pi_[:, 0:1], 7, None, ALU.bitwise_and, None, ["dpi"], ["dpi"])
    op_ts(k, "dve", pi_[:, 2:3], pi_[:, 0:1], 4, 1, ALU.logical_shift_right, ALU.bitwise_and, ["dpi"], ["dpi"])
    op_copy(k, "dve", sc[:, 5:7], pi_[:, 1:3], ["dpi"], R_)
    op_act(k, col(7), col(5), AF.Exp, R_, R_, scale=-LN1E4 / 8.0)
    op_tt(k, "dve", col(9), col(7), col(6), ALU.mult, R_, R_)
    op_tt(k, "dve", col(8), col(7), col(9), ALU.subtract, R_, R_)
    jfi = k.sb("djfi", [64, 256], I32)
    jf = k.sb("djf", [64, 256], F32)
    ang = k.sb("dang", [64, 256], F32)
    ttmp = k.sb("dttmp", [64, 256], F32)
    tti = k.sb("dtti", [64, 256], I32)
    p.op("pool", lambda: nc.gpsimd.iota(jfi[:], pattern=[[1, 256]], base=0, channel_multiplier=0), (), ["djfi"])
    op_copy(k, "dve", jf[:], jfi[:], ["djfi"], ["djf"])
    T = {n: k.sb("dT" + n, [64, 256], F32) for n in ("crk", "srk", "crq", "srq", "cc", "sc")}

    def table(cn, sn, n, scale_col, add_col):
        if add_col is None:
            op_ts(k, "dve", ang[:, :n], jf[:, :n], scale_col, None, ALU.mult, None, ["djf"] + R_, ["dang"])
        else:
            op_ts(k, "dve", ang[:, :n], jf[:, :n], add_col, scale_col, ALU.add, ALU.mult, ["djf", "dpar"] + R_, ["dang"])
        emit_sin(k, T[sn][:, :n], ang[:, :n], 0.0, ttmp[:, :n], tti[:, :n], ["dang"], ["dT" + sn], "dts")
        emit_sin(k, T[cn][:, :n], ang[:, :n], 0.5 * np.pi, ttmp[:, :n], tti[:, :n], ["dang"], ["dT" + cn], "dts")
    table("crk", "srk", 256, col(8), None)
    table("crq", "srq", 128, col(8), par[:, 3:4])
    table("cc", "sc", 64, col(9), None)
    op_ts(k, "dve", T["cc"][:, :64], T["cc"][:, :64], -1.0, None, ALU.add, None, ["dTcc"], ["dTcc"])
    ri = k.sb("dri", [64, 64], I32)
    rf = k.sb("drf", [64, 64], F32)
    mi = k.sb("dmi", [64, 64], I32)
    mf = k.sb("dmf", [64, 64], F32)
    e1 = k.sb("de1", [64, 64], F32)
    RmT = k.sb("dRmT", [64, 64], F32)
    p.op("pool", lambda: nc.gpsimd.iota(ri[:], pattern=[[1, 64]], base=0, channel_multiplier=-1), (), ["dri"])
    p.op("pool", lambda: nc.gpsimd.iota(mi[:], pattern=[[1, 64]], base=0, channel_multiplier=0), (), ["dmi"])
    op_copy(k, "dve", rf[:], ri[:], ["dri"], ["drf"])
    op_ts(k, "dve", mi[:], mi[:], 3, 1, ALU.logical_shift_right, ALU.bitwise_and, ["dmi"], ["dmi"])
    op_copy(k, "dve", mf[:], mi[:], ["dmi"], ["dmf"])
    op_ts(k, "dve", e1[:], rf[:], -8.0, None, ALU.is_equal, None, ["drf"], ["de1"])
    op_ts(k, "dve", RmT[:], rf[:], 8.0, None, ALU.is_equal, None, ["drf"], ["dRmT"])
    op_tt(k, "dve", RmT[:], RmT[:], mf[:], ALU.mult, ["dRmT", "dmf"], ["dRmT"])
    op_ts(k, "dve", mf[:], mf[:], -1.0, 1.0, ALU.mult, ALU.add, ["dmf"], ["dmf"])
    op_tt(k, "dve", e1[:], e1[:], mf[:], ALU.mult, ["de1", "dmf"], ["de1"])
    op_tt(k, "dve", RmT[:], RmT[:], e1[:], ALU.subtract, ["dRmT", "de1"], ["dRmT"])
    blk = k.sb("dblk", [64, 64], F32)
    op_memset(k, "dve", blk[:], 0.0, ["dblk"])
    op_memset(k, "dve", blk[0:32, 0:32], 1.0 / 32, ["dblk"])
    op_memset(k, "dve", blk[32:64, 32:64], 1.0 / 32, ["dblk"])
    o64 = k.sb("do64", [64, 64], F32)
    op_memset(k, "dve", o64[:], 1.0 / 64, ["do64"])
    sel = k.sb("dsel", [65, 64], F32)
    op_memset(k, "dve", sel[:], 0.0, ["dsel"])
    op_memset(k, "dve", sel[64:65, :], 1.0, ["dsel"])
    kh = k.sb("dkh", [64, NSEQ], BF16)
    qh = k.sb("dqh", [64, QH], BF16)
    qch = k.sb("dqch", [64, CTX], BF16)
    vaug = k.sb("dvaug", [128, NKT, 128], BF16)
    op_memset(k, "pool", vaug[:, :, 64:128], 0.0, [("dvaug", "one")])
    op_memset(k, "pool", vaug[:, :, 64:65], 1.0, [("dvaug", "one")])
    vst = [k.sb("dvst%d" % i, [128, 13, 64], F32) for i in range(2)]
    for i in range(10):
        b = i % 2
        p.dma(k.dq(), vst[b][:], vt_d[:, 13 * i:13 * i + 13, :], writes=[("dvst", b)])
        op_copy(k, "pool", vaug[:, 13 * i:13 * i + 13, 0:64], vst[b][:], [("dvst", b)], [("dvaug", i)])
    xs = [k.sb("dxs%d" % i, [64, 512], F32) for i in range(2)]
    W = {n: k.sb("dw" + n, [64, 512], F32) for n in ("sq", "rs", "xn", "cb", "sb", "t1")}
    bankA = k.ps("dbA", [128, 512], F32)
    bankB = k.ps("dbB", [128, 512], F32)
    pss = bankA[0:64, :]
    prx = bankB[0:64, :]
    it = 0

    def prep(src_d, c0, n, gcol, dst, dname, rope, crn, srn, a0):
        nonlocal it
        b = it % 2
        it += 1
        p.dma(k.dq(), xs[b][:, :n], src_d[:, c0:c0 + n], writes=[("dxs", b)])
        op_tt(k, "pool", W["sq"][:, :n], xs[b][:, :n], xs[b][:, :n], ALU.mult, [("dxs", b)], ["dwsq"])
        op_mm(k, pss[:, :n], blk[:], W["sq"][:, :n], True, True, ["dblk", "dwsq"], ["dbA"])
        op_ts(k, "dve", W["rs"][:, :n], pss[:, :n], EPS, None, ALU.add, None, ["dbA"], ["dwrs"])
        op_act(k, W["rs"][:, :n], W["rs"][:, :n], AF.Sqrt, ["dwrs"], ["dwrs"])
        k.p.op("dve", (lambda o=W["rs"][:, :n]: nc.vector.reciprocal(out=o, in_=o)), ["dwrs"], ["dwrs"])
        if not rope:
            op_stt(k, "dve", dst, xs[b][:, :n], gcol, W["rs"][:, :n], ALU.mult, ALU.mult, [("dxs", b), "dwrs", "dpar"] + R_, [dname])
            return
        op_stt(k, "dve", W["xn"][:, :n], xs[b][:, :n], gcol, W["rs"][:, :n], ALU.mult, ALU.mult, [("dxs", b), "dwrs", "dpar"] + R_, ["dwxn"])
        op_mm(k, prx[:, :n], RmT[:], W["xn"][:, :n], True, True, ["dRmT", "dwxn"], ["dbB"])
        na = n // 64
        v3 = lambda ap: ap.rearrange("p (a b) -> p a b", b=64)
        bc_a = lambda t2: bass.AP(t2.tensor, t2[:, a0:a0 + 1].offset, [list(t2.ap[0]), [1, na], [0, 64]])
        bc_b = lambda t2: bass.AP(t2.tensor, t2[:, 0:1].offset, [list(t2.ap[0]), [0, na], [1, 64]])
        op_tt(k, "pool", v3(W["cb"][:, :n]), bc_a(T[crn][:]), bc_b(T["cc"][:]), ALU.add, ["dT" + crn, "dTcc"], ["dwcb"])
        op_tt(k, "pool", v3(W["sb"][:, :n]), bc_a(T[srn][:]), bc_b(T["sc"][:]), ALU.add, ["dT" + srn, "dTsc"], ["dwsb"])
        op_tt(k, "dve", W["cb"][:, :n], W["cb"][:, :n], W["xn"][:, :n], ALU.mult, ["dwcb", "dwxn"], ["dwcb"])
        op_tt(k, "dve", W["sb"][:, :n], W["sb"][:, :n], prx[:, :n], ALU.mult, ["dwsb", "dbB"], ["dwsb"])
        op_tt(k, "dve", dst, W["cb"][:, :n], W["sb"][:, :n], ALU.add, ["dwcb", "dwsb"], [dname])

    prep(k_d, 0, CTX, par[:, 1:2], kh[:, 0:CTX], ("dkh", "c"), False, None, None, 0)
    for i in range(SEQ // 512):
        prep(k_d, CTX + 512 * i, 512, par[:, 1:2], kh[:, CTX + 512 * i:CTX + 512 * (i + 1)], ("dkh", i), True, "crk", "srk", 8 * i)
    prep(qc_d, 0, CTX, col(2), qch[:], "dqch", False, None, None, 0)
    for i in range(nqb):
        prep(q_d, 512 * i, 512, col(2), qh[:, 512 * i:512 * (i + 1)], ("dqh", i), True, "crq", "srq", 8 * i)
    pO = [bankA, bankB]
    pOn = ["dbA", "dbB"]
    Pb = [k.sb("dP%d" % i, [128, 2, 512], BF16) for i in range(3)]
    osb = [k.sb("dosb%d" % c, [65, 512], F32) for c in range(2)]
    F = {n: k.sb("df" + n, [64, 512], F32) for n in ("rd", "o0", "o1", "o", "sq", "rs")}

    def attend(qsrc, qname, n, kts, out_d, oc0):
        nk = len(kts)

        def scores(ki):
            kt = kts[ki]
            b = ki % 3
            kname = ("dkh", "c") if kt < 2 else ("dkh", (kt - 2) // 4)
            for c in range(2):
                rows = slice(32 * c, 32 * c + 32)
                op_mm(k, pS[b][:, c, :n], kh[rows, kt * 128:(kt + 1) * 128], qsrc[rows, :n], True, True, [kname, qname], [("dpS", b)])
            op_act(k, Pb[b][:, :, :n], pS[b][:, :, :n], AF.Exp, [("dpS", b)], [("dP", b)])

        def pv(ki):
            kt = kts[ki]
            b = ki % 3
            for c in range(2):
                op_mm(k, pO[c][:, :n], vaug[:, kt, :], Pb[b][:, c, :n], ki == 0, ki == nk - 1, ["dvaug", ("dP", b)], [pOn[c]])

        DEPTH = 2
        for ki in range(nk + DEPTH):
            if ki < nk:
                scores(ki)
            if ki >= DEPTH:
                pv(ki - DEPTH)
        for c in range(2):
            op_copy(k, "act", osb[c][:, :n], pO[c][0:65, :n], [pOn[c]], [("dosb", c)])
            op_mm(k, pm[0:64, :n], sel[:], osb[c][:, :n], True, True, ["dsel", ("dosb", c)], [("dpS", 0)])
            k.p.op("dve", (lambda o=F["rd"][:, :n], i_=pm[0:64, :n]: nc.vector.reciprocal(out=o, in_=i_)), [("dpS", 0)], ["dfrd"])
            op_tt(k, "dve", F["o%d" % c][:, :n], osb[c][0:64, :n], F["rd"][:, :n], ALU.mult, [("dosb", c), "dfrd"], ["dfo%d" % c])
        op_stt(k, "dve", F["o"][:, :n], F["o1"][:, :n], col(0), F["o0"][:, :n], ALU.mult, ALU.add, ["dfo0", "dfo1"] + R_, ["dfo"])
        op_tt(k, "pool", F["sq"][:, :n], F["o"][:, :n], F["o"][:, :n], ALU.mult, ["dfo"], ["dfsq"])
        op_mm(k, pm[0:64, :n], o64[:], F["sq"][:, :n], True, True, ["do64", "dfsq"], [("dpS", 0)])
        op_ts(k, "dve", F["rs"][:, :n], pm[0:64, :n], EPS, None, ALU.add, None, [("dpS", 0)], ["dfrs"])
        op_act(k, F["rs"][:, :n], F["rs"][:, :n], AF.Sqrt, ["dfrs"], ["dfrs"])
        k.p.op("dve", (lambda o=F["rs"][:, :n]: nc.vector.reciprocal(out=o, in_=o)), ["dfrs"], ["dfrs"])
        op_stt(k, "dve", F["o"][:, :n], F["o"][:, :n], col(1), F["rs"][:, :n], ALU.mult, ALU.mult, ["dfo", "dfrs"] + R_, ["dfo"])
        p.dma("sp", out_d[:, oc0:oc0 + n], F["o"][:, :n], reads=["dfo"], writes=[("dout", id(out_d), oc0)])

    attend(qch, "dqch", CTX, [0, 1], yc_d, 0)
    for i in range(nqb):
        attend(qh[:, 512 * i:512 * (i + 1)], ("dqh", i), 512, list(range(NKT)), y_d, 512 * i)
    return k.finish(["dout"])


def da_inputs(core, zq, zk, zv, zqc, q_norm, k_norm, out_norm, da_lam):
    h, qh = core // 2, core % 2
    rows = slice(64 * h, 64 * h + 64)
    par = np.zeros((64, 4), np.float32)
    par[:, 0] = np.tile(q_norm, 2)
    par[:, 1] = np.tile(k_norm, 2)
    par[:, 2] = out_norm
    par[:, 3] = 128.0 * qh
    vt = np.ascontiguousarray(zv[rows].T.reshape(NKT, 128, 64).transpose(1, 0, 2))
    return dict(dq=np.ascontiguousarray(zq[rows, QH * qh:QH * (qh + 1)]), dk=np.ascontiguousarray(zk[rows]),
                dvt=vt, dqc=np.ascontiguousarray(zqc[rows]), dpar=par, dlam=np.ascontiguousarray(np.asarray(da_lam, np.float32).T))


NL = 1024
C_HL, C_HR, C_CTX0 = NL, NL + 1, NL + 2
NFF = D_FF // 128


class WStream:
    def __init__(self, k, tag, nk, mw, nbuf=2):
        self.k, self.tag, self.nbuf = k, tag, nbuf
        self.st = [k.sb(tag + "s%d" % i, [128, nk, mw], F32) for i in range(nbuf)]
        self.bf = [k.sb(tag + "b%d" % i, [128, nk, mw], BF16) for i in range(nbuf)]
        self.i = 0

    def load(self, view, nk=None, mw=None):
        k = self.k
        b = self.i % self.nbuf
        self.i += 1
        st, bf = self.st[b], self.bf[b]
        nk = nk or st.shape[1]
        mw = mw or st.shape[2]
        k.p.dma(k.dq(), st[:, :nk, :mw], view, writes=[(self.tag + "s", b)])
        op_copy(k, "pool", bf[:, :nk, :mw], st[:, :nk, :mw], [(self.tag + "s", b)], [(self.tag + "b", b)])
        return bf, (self.tag + "b", b)


def build_phaseC(with_ctx):
    k = K()
    nc, p = k.nc, k.p
    NTS = NL + 2
    nsec = 3 if with_ctx else 2
    x_d = k.din("xT", [nsec, D, NTS])
    cvec_d = k.din("cvec", [128, 8, 2])
    wada_d = k.din("wada", [D, 6 * D])
    bada_d = k.din("bada", [128, 48])
    g1_d = k.din("g1", [128, 8])
    g2_d = k.din("g2", [128, 8])
    wg_d = k.din("wg", [D, 4 * D])
    u_d = k.din("bu", [nsec, 256, NTS])
    ys_d = k.din("bys", [nsec, 256, NTS])
    yb_d = k.din("byb", [nsec, 256, NTS])
    yc_d = k.din("byc", [nsec, 256, NTS])
    go_d = k.din("bgo", [nsec, 256, NTS])
    gg_d = k.din("bgg", [nsec, 256, NTS])
    sp_d = k.din("spar", [128, 8])
    wglu_d = k.din("wglu", [256, 256])
    wbr_d = k.din("wbr", [4, 256, D])
    wo_d = k.din("wo", [D, D])
    wup_d = k.din("wup", [D, 2 * D_FF])
    fcw_d = k.din("fcw", [128, NFF, 4])
    wdn_d = k.din("wdn", [D_FF, D])
    xo_d = k.dout("xo", [2, D, NL])
    co_d = k.dout("co", [D, CTX]) if with_ctx else None

    emit_consts(k)
    k.modname = "Cmod"
    mod = emit_adaln(k, wada_d, bada_d, cvec_d, [0, 1, 2, 3, 4, 5], "C")
    banks = Banks(k, 7)
    spar = k.sb("spar", [128, 8], F32)
    fcw = k.sb("fcw", [128, NFF, 4], F32)
    p.dma("sp", spar[:], sp_d, writes=["spar"])
    p.dma("sp", fcw[:], fcw_d, writes=["fcw"])
    blk64 = k.sb("blk64", [128, 128], F32)
    op_memset(k, "dve", blk64[:], 0.0, ["blk64"])
    op_memset(k, "dve", blk64[0:64, 0:64], 1.0 / 64, ["blk64"])
    op_memset(k, "dve", blk64[64:128, 64:128], 1.0 / 64, ["blk64"])
    wglu = WStream(k, "wglu", 2, 256, nbuf=1)
    wglu_bf, wglu_n = wglu.load(wglu_d.rearrange("(k p) m -> p k m", p=128))

    xT = k.sb("xT_sb", [128, 8, NTS], F32)
    hnT = k.sb("hnT", [128, 8, NTS], BF16)
    yT = k.sb("yT", [128, 8, NTS], BF16)
    mg = k.sb("mg", [128, 8, NTS], BF16)
    ws_g = WStream(k, "wsg", 8, 128, nbuf=4)
    ws_b = WStream(k, "wsb", 2, 128, nbuf=3)
    ws_o = ws_g
    ws_u = ws_g
    ws_d = WStream(k, "wsd", 1, 1024, nbuf=4)
    stg = [k.sb("stg%d" % i, [128, NTS], F32) for i in range(3)]
    stg_i = [0]

    def stage():
        stg_i[0] += 1
        b = stg_i[0] % 3
        return stg[b], ("stg", b)

    zf = k.sb("zf", [128, 2, NTS], F32)
    zb = k.sb("zb", [128, 2, NTS], BF16)
    acc = k.sb("acc", [128, NTS], F32)
    sig = [k.sb("sig%d" % i, [128, 512], F32) for i in range(2)]
    tmpm = [k.sb("tmpm%d" % i, [128, 512], F32) for i in range(2)]
    gts = k.sb("gts", [128, NL + 2], F32)
    cvt = acc
    asb = stg[0]
    hT = [k.sb("hT%d" % i, [128, NL], BF16) for i in range(4)]

    for sec in range(nsec):
        sfx = "s%d" % sec
        isctx = sec == 2
        if isctx:
            blocks = [(0, CTX, 1)]
            oblocks = [(0, CTX, 1, 0)]
            op_memset(k, "dve", gts[:, 0:1], 0.0, ["gts"])
            op_memset(k, "dve", gts[:, CTX + 1:CTX + 2], 0.0, ["gts"])
        else:
            blocks = [(0, 512, 0), (512, 1024, 0), (NL, NL + 2, 0)]
            oblocks = [(0, 512, 0, 0), (512, 1024, 0, 512)]
        xv = x_d[sec].rearrange("(k p) t -> p k t", p=128)
        for kk in range(8):
            p.dma(k.dq(), xT[:, kk, :], xv[:, kk, :], writes=[("xT", kk)])
        emit_norm_mod(k, xT, NTS, blocks, g1_d, mod, 0, 1, hnT, "n1" + sfx, "xT", banks.b)
        hn_name = "n1" + sfx + "hnT"
        for j in range(2):
            su, sun = stage()
            sy, syn = stage()
            p.dma(k.dq(), su[:], u_d[sec, 128 * j:128 * (j + 1), :], writes=[sun])
            p.dma(k.dq(), sy[:], ys_d[sec, 128 * j:128 * (j + 1), :], writes=[syn])
            op_stt(k, "dve", sy[:], su[:], spar[:, j:j + 1], sy[:], ALU.mult, ALU.add, [sun, syn, "spar"], [syn])
            op_act(k, zf[:, j, :], sy[:], AF.Gelu_apprx_tanh, [syn], [("zf", j)])
            op_copy(k, "pool", zb[:, j, :], zf[:, j, :], [("zf", j)], [("zb", j)])
        for j in range(2):
            for (c0, c1, v) in blocks:
                w = c1 - c0
                pt, pn = banks.next()
                for kk in range(2):
                    op_mm(k, pt[:, :w], wglu_bf[:, kk, 128 * j:128 * (j + 1)], zb[:, kk, c0:c1], kk == 0, kk == 1, [wglu_n, ("zb", kk)], [pn])
                b = c0 // 512 % 2
                op_act(k, sig[b][:, :w], pt[:, :w], AF.Sigmoid, [pn], [("sig", b)])
                op_tt(k, "dve", yT[:, j, c0:c1], sig[b][:, :w], zf[:, j, c0:c1], ALU.mult, [("sig", b), ("zf", j)], [("yT", j, c0)])
        for bi_, src in ((1, yb_d), (2, yc_d)):
            for j in range(2):
                s_, sn_ = stage()
                p.dma(k.dq(), s_[:], src[sec, 128 * j:128 * (j + 1), :], writes=[sn_])
                op_copy(k, "pool", yT[:, 2 * bi_ + j, :], s_[:], [sn_], [("yT", 2 * bi_ + j)])
        for j in range(2):
            so, son = stage()
            sg, sgn = stage()
            p.dma(k.dq(), so[:], go_d[sec, 128 * j:128 * (j + 1), :], writes=[son])
            p.dma(k.dq(), sg[:], gg_d[sec, 128 * j:128 * (j + 1), :], writes=[sgn])
            op_act(k, sg[:], sg[:], AF.Silu, [sgn], [sgn])
            for (c0, c1, v) in blocks:
                w = c1 - c0
                b = c0 // 512 % 2
                op_tt(k, "pool", tmpm[b][:, :w], so[:, c0:c1], so[:, c0:c1], ALU.mult, [son], [("tmpm", b)])
                pt, pn = banks.next()
                op_mm(k, pt[:, :w], blk64[:], tmpm[b][:, :w], True, True, ["blk64", ("tmpm", b)], [pn])
                op_ts(k, "dve", sig[b][:, :w], pt[:, :w], EPS, None, ALU.add, None, [pn], [("sig", b)])
                op_act(k, sig[b][:, :w], sig[b][:, :w], AF.Sqrt, [("sig", b)], [("sig", b)])
                k.p.op("dve", (lambda o=sig[b][:, :w]: nc.vector.reciprocal(out=o, in_=o)), [("sig", b)], [("sig", b)])
                op_stt(k, "dve", tmpm[b][:, :w], so[:, c0:c1], spar[:, 2 + j:3 + j], sig[b][:, :w], ALU.mult, ALU.mult, [son, "spar", ("sig", b), ("tmpm", b)], [("tmpm", b)])
                op_tt(k, "dve", yT[:, 6 + j, c0:c1], tmpm[b][:, :w], sg[:, c0:c1], ALU.mult, [("tmpm", b), sgn], [("yT", 6 + j, c0)])
        wgv = wg_d.rearrange("(k p) m -> p k m", p=128)
        for m in range(8):
            for n in range(4):
                wgb, wgn = ws_g.load(wgv[:, :, n * D + m * 128:n * D + (m + 1) * 128])
                wbb, wbn = ws_b.load(wbr_d[n].rearrange("(k p) m -> p k m", p=128)[:, :, m * 128:(m + 1) * 128])
                for (c0, c1, v) in blocks:
                    w = c1 - c0
                    b = (c0 // 512 + n) % 2
                    ptg, png = banks.next()
                    for kk in range(8):
                        op_mm(k, ptg[:, :w], wgb[:, kk, :], hnT[:, kk, c0:c1], kk == 0, kk == 7, [wgn, (hn_name, kk)], [png])
                    ptp, pnp = banks.next()
                    for kk in range(2):
                        op_mm(k, ptp[:, :w], wbb[:, kk, :], yT[:, 2 * n + kk, c0:c1], kk == 0, kk == 1, [wbn, ("yT", 2 * n + kk)], [pnp])
                    op_act(k, sig[b][:, :w], ptg[:, :w], AF.Sigmoid, [png], [("sig", b)])
                    if n == 0:
                        op_tt(k, "dve", acc[:, c0:c1], sig[b][:, :w], ptp[:, :w], ALU.mult, [("sig", b), pnp], [("acc", c0)])
                    else:
                        op_tt(k, "dve", tmpm[b][:, :w], sig[b][:, :w], ptp[:, :w], ALU.mult, [("sig", b), pnp], [("tmpm", b)])
                        dst = mg[:, m, c0:c1] if n == 3 else acc[:, c0:c1]
                        dn = ("mg", m, c0) if n == 3 else ("acc", c0)
                        op_tt(k, "dve", dst, acc[:, c0:c1], tmpm[b][:, :w], ALU.add, [("acc", c0), ("tmpm", b)], [dn])
        wov = wo_d.rearrange("(k p) m -> p k m", p=128)
        for m in range(8):
            wob, won = ws_o.load(wov[:, :, m * 128:(m + 1) * 128])
            for (c0, c1, v) in blocks:
                w = c1 - c0
                pt, pn = banks.next()
                for kk in range(8):
                    op_mm(k, pt[:, :w], wob[:, kk, :], mg[:, kk, c0:c1], kk == 0, kk == 7, [won, ("mg", kk)], [pn])
                op_stt(k, "dve", xT[:, m, c0:c1], pt[:, :w], mod[:, 16 + m, v:v + 1], xT[:, m, c0:c1], ALU.mult, ALU.add,
                       [pn, "Cmod", ("xT", m)], [("xT", m)])
        emit_norm_mod(k, xT, NTS, blocks, g2_d, mod, 3, 4, hnT, "n2" + sfx, "xT", banks.b)
        hn2 = "n2" + sfx + "hnT"
        wuv = wup_d.rearrange("(k p) m -> p k m", p=128)
        for g in range(NFF // 2):
          wds = []
          for f in (2 * g, 2 * g + 1):
            wab, wan = ws_u.load(wuv[:, :, f * 128:(f + 1) * 128])
            wtb, wtn = ws_u.load(wuv[:, :, D_FF + f * 128:D_FF + (f + 1) * 128])
            wdb, wdn_ = ws_d.load(wdn_d[f * 128:(f + 1) * 128, :].rearrange("p (o m) -> p o m", o=1))
            hb = f % 4
            wds.append((wdb, wdn_))
            for (c0, c1, v) in blocks:
                w = c1 - c0
                pta, pna = banks.next()
                for kk in range(8):
                    op_mm(k, pta[:, :w], wab[:, kk, :], hnT[:, kk, c0:c1], kk == 0, kk == 7, [wan, (hn2, kk)], [pna])
                op_copy(k, "act", asb[:, c0:c1], pta[:, :w], [pna], [("stg", 0, c0)])
                ptt, pnt = banks.next()
                for kk in range(8):
                    op_mm(k, ptt[:, :w], wtb[:, kk, :], hnT[:, kk, c0:c1], kk == 0, kk == 7, [wtn, (hn2, kk)], [pnt])
                if c0 < NL:
                    op_copy(k, "act", gts[:, 1 + c0:1 + c1], ptt[:, :w], [pnt], [("gts", "l", c0)])
                else:
                    op_tt(k, "dve", gts[:, 0:1], ptt[:, 0:1], spar[:, 4 + 2 * sec:5 + 2 * sec], ALU.mult, [pnt, "spar"], [("gts", "hl")])
                    op_tt(k, "dve", gts[:, NL + 1:NL + 2], ptt[:, 1:2], spar[:, 5 + 2 * sec:6 + 2 * sec], ALU.mult, [pnt, "spar"], [("gts", "hr")])
            segs = [(0, CTX if isctx else NL, 0, 0)]
            for (g0, n, oc, ac) in segs:
                op_ts(k, "dve", cvt[:, oc:oc + n], gts[:, g0:g0 + n], fcw[:, f, 0:1], fcw[:, f, 3:4], ALU.mult, ALU.add, ["gts", "fcw"], [("acc", "cv", oc)])
                op_stt(k, "dve", cvt[:, oc:oc + n], gts[:, g0 + 1:g0 + 1 + n], fcw[:, f, 1:2], cvt[:, oc:oc + n], ALU.mult, ALU.add, ["gts", "fcw", ("acc", "cv", oc)], [("acc", "cv", oc)])
                op_stt(k, "dve", cvt[:, oc:oc + n], gts[:, g0 + 2:g0 + 2 + n], fcw[:, f, 2:3], cvt[:, oc:oc + n], ALU.mult, ALU.add, ["gts", "fcw", ("acc", "cv", oc)], [("acc", "cv", oc)])
                op_act(k, cvt[:, oc:oc + n], cvt[:, oc:oc + n], AF.Gelu_apprx_tanh, [("acc", "cv", oc)], [("acc", "cv", oc)])
                op_tt(k, "dve", hT[hb][:, oc:oc + n], cvt[:, oc:oc + n], asb[:, ac:ac + n], ALU.mult, [("acc", "cv", oc), ("stg", 0)], [("hT", hb, oc)])
          for m in range(8):
              for (c0, c1, v, xc0) in oblocks:
                  w = c1 - c0
                  pt, pn = banks.next()
                  for fi, f in enumerate((2 * g, 2 * g + 1)):
                      wdb, wdn_ = wds[fi]
                      op_mm(k, pt[:, :w], wdb[:, 0, m * 128:(m + 1) * 128], hT[f % 4][:, c0:c1], fi == 0, fi == 1, [wdn_, ("hT", f % 4)], [pn])
                  if (m + c0 // 512) % 2 == 1:
                      op_act(k, tmpm[m % 2][:, :w], pt[:, :w], AF.Identity, [pn, "Cmod"], [("tmpm", m % 2)], scale=mod[:, 40 + m, v:v + 1])
                      op_tt(k, "pool", xT[:, m, xc0:xc0 + w], xT[:, m, xc0:xc0 + w], tmpm[m % 2][:, :w], ALU.add,
                            [("tmpm", m % 2), ("xT", m)], [("xT", m)])
                  else:
                      op_stt(k, "dve", xT[:, m, xc0:xc0 + w], pt[:, :w], mod[:, 40 + m, v:v + 1], xT[:, m, xc0:xc0 + w], ALU.mult, ALU.add,
                             [pn, "Cmod", ("xT", m)], [("xT", m)])
        if not isctx:
            xov = xo_d[sec].rearrange("(k p) t -> p k t", p=128)
            for kk in range(8):
                p.dma(k.dq(), xov[:, kk, :], xT[:, kk, 0:NL], reads=[("xT", kk)], writes=[("xout", sec, kk)])
        else:
            cov = co_d.rearrange("(k p) t -> p k t", p=128)
            for kk in range(8):
                p.dma(k.dq(), cov[:, kk, :], xT[:, kk, 0:CTX], reads=[("xT", kk)], writes=[("cout", kk)])
    return k.finish(["xout", "cout"] if with_ctx else ["xout"])


def _sec_cols(arrT, core, sec, off):
    s0 = TL * core + NL * sec
    out = np.zeros((arrT.shape[0], NL + 2), np.float32)
    out[:, :NL] = arrT[:, off + s0:off + s0 + NL]
    if s0 > 0:
        out[:, NL] = arrT[:, off + s0 - 1]
    if s0 + NL < SEQ:
        out[:, NL + 1] = arrT[:, off + s0 + NL]
    return out


def _ctx_cols(arrT):
    out = np.zeros((arrT.shape[0], NL + 2), np.float32)
    out[:, :CTX] = arrT[:, :CTX]
    return out


def phaseC_inputs(core, with_ctx, xT_all, cT_all, br, c, c_ctx, w_ada_l, b_ada_l, g1_l, g2_l, w_in_l, ssm_d, w_glu, gla_g,
                  w_branch, w_out, w_up, fcw, fcb, w_down):
    nsec = 3 if with_ctx else 2
    def sec_stack(aT, off, ctxT):
        parts = [_sec_cols(aT, core, s, off) for s in range(2)]
        if with_ctx:
            parts.append(_ctx_cols(ctxT))
        return np.ascontiguousarray(np.stack(parts))
    d = dict(xT=sec_stack(xT_all, 0, cT_all))
    for nm, key in (("bu", "u"), ("bys", "ys"), ("byb", "yb"), ("byc", "yc"), ("bgo", "go"), ("bgg", "gg")):
        d[nm] = sec_stack(br[key], CTX, br[key])
    spar = np.zeros((128, 8), np.float32)
    spar[:, 0:2] = np.asarray(ssm_d, np.float32).reshape(2, 128).T
    spar[:, 2:4] = np.tile(np.asarray(gla_g, np.float32), 2)[:, None]
    for s in range(2):
        s0 = TL * core + NL * s
        spar[:, 4 + 2 * s] = 1.0 if s0 > 0 else 0.0
        spar[:, 5 + 2 * s] = 1.0 if s0 + NL < SEQ else 0.0
    fc = np.zeros((128, NFF, 4), np.float32)
    fc[:, :, 0:3] = np.asarray(fcw, np.float32).T.reshape(NFF, 128, 3).transpose(1, 0, 2)
    fc[:, :, 3] = np.asarray(fcb, np.float32).reshape(NFF, 128).T
    cvec = np.ascontiguousarray(np.stack([_ft(np.asarray(c).reshape(-1)), _ft(np.asarray(c_ctx).reshape(-1))], axis=-1))
    d.update(cvec=cvec, wada=np.ascontiguousarray(w_ada_l), bada=_ft(b_ada_l), g1=_ft(g1_l), g2=_ft(g2_l),
             wg=np.ascontiguousarray(w_in_l[:, IN_MIX:]), spar=spar, wglu=np.ascontiguousarray(w_glu),
             wbr=np.ascontiguousarray(w_branch), wo=np.ascontiguousarray(w_out), wup=np.ascontiguousarray(w_up),
             fcw=fc, wdn=np.ascontiguousarray(w_down))
    return d


_PIECES = (('ssm_u', 256), ('lru_x', 256), ('lru_y', 256), ('da_q', 256), ('da_k', 256), ('da_v', 256),
           ('gla_q', 128), ('gla_k', 128), ('gla_v', 256), ('gla_g', 256), ('gla_a', 32))


def _spmd(nc, in_maps):
    res = run_bass_kernel_spmd(nc, in_maps, core_ids=list(range(NCORE)))
    return res.results


def kernel(x, c, ctx, c_ctx, w_ada, b_ada, norm1_g, norm2_g, w_in,
           ssm_lam_re, ssm_lam_im, ssm_log_step, ssm_b_re, ssm_b_im, ssm_c_re, ssm_c_im, ssm_d, ssm_w_glu,
           lru_conv_w, lru_conv_b, lru_wr, lru_br, lru_wi, lru_bi, lru_lam,
           da_q_norm, da_k_norm, da_lam, da_out_norm,
           gla_wa2, gla_ba, gla_out_norm,
           w_branch, w_out, w_up, ffn_conv_w, ffn_conv_b, w_down):
    A = lambda a: np.asarray(a, dtype=np.float32)
    xT = np.ascontiguousarray(A(x)[0].T)
    cT = np.ascontiguousarray(A(ctx)[0].T)
    depth = A(w_in).shape[0]
    for l in range(depth):
        with_ctx = l < depth - 1
        zs = run_phaseA(xT, cT, A(c), A(c_ctx), A(w_ada)[l], A(b_ada)[l], A(norm1_g)[l], A(w_in)[l])
        z_all = np.concatenate([zs[0][:, TL:]] + [zs[i][:, :TL] for i in range(NCORE)], axis=1)
        z = {}
        o = 0
        for nm, sz in _PIECES:
            z[nm] = z_all[o:o + sz]
            o += sz
        ims = []
        for i in range(NCORE):
            d = s5_inputs(i, z['ssm_u'], A(ssm_lam_re)[l], A(ssm_lam_im)[l], A(ssm_log_step)[l],
                          A(ssm_b_re)[l], A(ssm_b_im)[l], A(ssm_c_re)[l], A(ssm_c_im)[l])
            d.update(lru_inputs(i, z['lru_x'], z['lru_y'], A(lru_conv_w)[l], A(lru_conv_b)[l], A(lru_wr)[l],
                                A(lru_br)[l], A(lru_wi)[l], A(lru_bi)[l], A(lru_lam)[l]))
            d.update(gla_inputs(i, z['gla_q'], z['gla_k'], z['gla_v'], z['gla_a'], A(gla_wa2)[l], A(gla_ba)[l]))
            ims.append(d)
        r = _spmd(build_scans(), ims)
        ys = np.concatenate([q["ys"] for q in r], axis=0)
        yb = np.concatenate([q["lo"] for q in r], axis=0)
        go = np.concatenate([q["go"] for q in r], axis=0)
        lam_init = 0.8 - 0.6 * float(np.exp(-0.3 * l))
        zq_lat = np.ascontiguousarray(z['da_q'][:, CTX:])
        zq_ctx = np.ascontiguousarray(z['da_q'][:, :CTX])
        r = _spmd(build_da(lam_init), [da_inputs(i, zq_lat, z['da_k'], z['da_v'], zq_ctx, A(da_q_norm)[l], A(da_k_norm)[l],
                                                 A(da_out_norm)[l], A(da_lam)[l]) for i in range(NCORE)])
        yc = np.zeros((256, NSEQ), np.float32)
        for i in range(NCORE):
            h, qh = i // 2, i % 2
            yc[64 * h:64 * h + 64, CTX + QH * qh:CTX + QH * (qh + 1)] = r[i]["dy"]
            if qh == 0:
                yc[64 * h:64 * h + 64, :CTX] = r[i]["dyc"]
        br = dict(u=z['ssm_u'], ys=ys, yb=yb, yc=yc, go=go, gg=z['gla_g'])
        r = _spmd(build_phaseC(with_ctx), [phaseC_inputs(i, with_ctx, xT, cT, br, A(c), A(c_ctx), A(w_ada)[l], A(b_ada)[l],
                                                         A(norm1_g)[l], A(norm2_g)[l], A(w_in)[l], A(ssm_d)[l], A(ssm_w_glu)[l],
                                                         A(gla_out_norm)[l], A(w_branch)[l], A(w_out)[l], A(w_up)[l],
                                                         A(ffn_conv_w)[l], A(ffn_conv_b)[l], A(w_down)[l]) for i in range(NCORE)])
        xT = np.concatenate([np.concatenate([q["xo"][0], q["xo"][1]], axis=1) for q in r], axis=1)
        if with_ctx:
            cT = np.ascontiguousarray(r[0]["co"])
    return np.ascontiguousarray(xT.T)[None].astype(np.float32)
```
